# Optimizing a Trainium2 kernel written in Bass

```python
import math
import jax, jax.numpy as jnp
from jax import lax
import numpy as np

D_MODEL = 4096
BATCH = 4
SEQ = 4096
DEPTH = 1

D_FF = int(round(8 * D_MODEL / 3 / 256)) * 256
GLA_HEADS = max(4, D_MODEL // 512)
GLA_KEY_W = D_MODEL // 4
GLA_VAL_W = D_MODEL // 2
GLA_DK = GLA_KEY_W // GLA_HEADS
GLA_DV = GLA_VAL_W // GLA_HEADS
GLA_GATE_RANK = 16
GLA_GATE_NORM = 16.0
GLA_CHUNK = 64
RWKV_HEAD = 64
RWKV_W = D_MODEL // 2
RWKV_HEADS = RWKV_W // RWKV_HEAD
RWKV_DECAY_LORA = max(32, int(round(math.sqrt(D_MODEL) * 1.8 / 32)) * 32)
RWKV_AAA_LORA = max(32, int(round(math.sqrt(D_MODEL) * 1.8 / 32)) * 32)
RWKV_GATE_LORA = max(32, int(round(D_MODEL ** 0.8 * 0.6 / 32)) * 32)
RWKV_LN_EPS = 64e-5
NORM_EPS = 1e-6
MACARON_WEIGHT = 0.5

GLA_SPLITS = (GLA_KEY_W, GLA_KEY_W, GLA_VAL_W, GLA_GATE_RANK, GLA_VAL_W)
RWKV_SPLITS = (RWKV_W, RWKV_DECAY_LORA, RWKV_W, RWKV_W, RWKV_AAA_LORA, RWKV_GATE_LORA)
GLA_COLS = sum(GLA_SPLITS)
RWKV_COLS = sum(RWKV_SPLITS)
W_IN_COLS = GLA_COLS + RWKV_COLS + 2 * D_MODEL

kernel_name = "hybrid_gla_rwkv7_macaron_sandwich"


def _split(t, sizes):
    idx = np.cumsum(sizes)[:-1].tolist()
    return jnp.split(t, idx, axis=-1)


def rms_norm(x, g):
    xf = x.astype(jnp.float32)
    y = xf * lax.rsqrt(jnp.mean(xf * xf, axis=-1, keepdims=True) + NORM_EPS)
    return (y * g.astype(jnp.float32)).astype(x.dtype)


def swiglu(x, w_gate, w_up, w_down):
    return (jax.nn.silu(x @ w_gate) * (x @ w_up)) @ w_down


def gla_chunked(q, k, v, log_a):
    B, S, H, dk = q.shape
    dv = v.shape[-1]
    C = GLA_CHUNK
    N = S // C
    f32 = jnp.float32

    def to_chunks(t):
        return t.astype(f32).reshape(B, N, C, H, t.shape[-1]).transpose(1, 0, 3, 2, 4)

    xs = (to_chunks(q), to_chunks(k), to_chunks(v), to_chunks(log_a))
    causal = jnp.tril(jnp.ones((C, C), dtype=bool))[None, None, :, :, None]

    def step(state, inp):
        qb, kb, vb, gb = inp
        b = jnp.cumsum(gb, axis=2)
        diff = jnp.where(causal, b[:, :, :, None, :] - b[:, :, None, :, :], -jnp.inf)
        attn = jnp.einsum('bhid,bhjd,bhijd->bhij', qb, kb, jnp.exp(diff))
        o = jnp.einsum('bhij,bhjv->bhiv', attn, vb) + jnp.einsum('bhid,bhdv->bhiv', qb * jnp.exp(b), state)
        b_last = b[:, :, -1, :]
        state = state * jnp.exp(b_last)[..., None] + jnp.einsum(
            'bhjd,bhjv->bhdv', kb * jnp.exp(b_last[:, :, None, :] - b), vb)
        return state, o

    init = jnp.zeros((B, H, dk, dv), f32)
    _, o = lax.scan(step, init, xs)
    return o.transpose(1, 0, 3, 2, 4).reshape(B, S, H, dv)


def gla_branch(p, gate_up, gate_bias, out_norm):
    B, S, _ = p.shape
    q, k, v, g_down, out_gate = _split(p, GLA_SPLITS)
    q = (q * (GLA_DK ** -0.5)).reshape(B, S, GLA_HEADS, GLA_DK)
    k = k.reshape(B, S, GLA_HEADS, GLA_DK)
    v = v.reshape(B, S, GLA_HEADS, GLA_DV)
    log_a = jax.nn.log_sigmoid((g_down @ gate_up + gate_bias).astype(jnp.float32)) / GLA_GATE_NORM
    log_a = log_a.reshape(B, S, GLA_HEADS, GLA_DK)
    o = gla_chunked(q, k, v, log_a)
    o = rms_norm(o, out_norm).reshape(B, S, GLA_VAL_W)
    return (o * jax.nn.silu(out_gate.astype(jnp.float32))).astype(p.dtype)


def rwkv7_scan(r, decay, k, v, kk, a):
    B, S, H, N = r.shape

    def step(state, inp):
        r_t, w_t, k_t, v_t, kk_t, a_t = inp
        sa = jnp.einsum('bhvk,bhk->bhv', state, kk_t)
        state = (state * w_t[:, :, None, :] - sa[..., None] * (kk_t * a_t)[:, :, None, :]
                 + v_t[..., None] * k_t[:, :, None, :])
        return state, jnp.einsum('bhvk,bhk->bhv', state, r_t)

    xs = tuple(t.astype(jnp.float32).transpose(1, 0, 2, 3) for t in (r, decay, k, v, kk, a))
    init = jnp.zeros((B, H, N, N), jnp.float32)
    _, y = lax.scan(step, init, xs)
    return y.transpose(1, 0, 2, 3)


def rwkv7_branch(p, shift_mix, w0, w2, a0, a2, g2, k_k, k_a, r_k, ln_w, ln_b):
    B, S, _ = p.shape
    f32 = jnp.float32
    p_prev = jnp.pad(p[:, :-1], ((0, 0), (1, 0), (0, 0)))
    p = p + (p_prev - p) * shift_mix
    r, w_down, k, v, a_down, g_down = _split(p, RWKV_SPLITS)
    w = -jax.nn.softplus(-(w0 + jnp.tanh(w_down) @ w2).astype(f32)) - 0.5
    decay = jnp.exp(-jnp.exp(w))
    a = jax.nn.sigmoid((a0 + a_down @ a2).astype(f32))
    g = jax.nn.sigmoid(g_down) @ g2

    def heads(t):
        return t.reshape(t.shape[:-1] + (RWKV_HEADS, RWKV_HEAD))

    r, k, v, decay, a = heads(r), heads(k), heads(v), heads(decay), heads(a)
    kk = (k * heads(k_k)).astype(f32)
    kk = kk / jnp.maximum(jnp.sqrt(jnp.sum(kk * kk, axis=-1, keepdims=True)), 1e-12)
    k = k * (1.0 + (a - 1.0) * heads(k_a))
    y = rwkv7_scan(r, decay, k, v, kk, a)
    mu = jnp.mean(y, axis=-1, keepdims=True)
    var = jnp.mean(jnp.square(y - mu), axis=-1, keepdims=True)
    y = ((y - mu) * lax.rsqrt(var + RWKV_LN_EPS)).reshape(B, S, RWKV_W) * ln_w + ln_b
    bonus = jnp.sum(r * k * r_k, axis=-1, keepdims=True) * v
    y = (y + bonus.reshape(B, S, RWKV_W)) * g
    return y.astype(p.dtype)


def setup_inputs(seed: int = 0) -> dict:
    key = jax.random.key(seed)
    ks = iter(jax.random.split(key, 48))
    L = DEPTH
    f32 = jnp.float32

    def nrm(shape, scale):
        return jax.random.normal(next(ks), shape, f32) * scale

    def gain(n):
        return 1.0 + nrm((L, n), 0.05)

    pos = jnp.arange(RWKV_W, dtype=f32) / (RWKV_W - 1)
    return {
        "x": nrm((BATCH, SEQ, D_MODEL), 1.0),
        "ffn1_pre_norm": gain(D_MODEL),
        "ffn1_w_gate": nrm((L, D_MODEL, D_FF), D_MODEL ** -0.5),
        "ffn1_w_up": nrm((L, D_MODEL, D_FF), D_MODEL ** -0.5),
        "ffn1_w_down": nrm((L, D_FF, D_MODEL), D_FF ** -0.5),
        "ffn1_post_norm": gain(D_MODEL),
        "mix_pre_norm": gain(D_MODEL),
        "w_in": nrm((L, D_MODEL, W_IN_COLS), D_MODEL ** -0.5),
        "gla_gate_up": nrm((L, GLA_GATE_RANK, GLA_KEY_W), GLA_GATE_RANK ** -0.5),
        "gla_gate_bias": nrm((L, GLA_KEY_W), 0.5) + 2.0,
        "gla_out_norm": gain(GLA_DV),
        "rwkv_shift_mix": jax.random.uniform(next(ks), (L, RWKV_COLS), f32),
        "rwkv_w0": (-6.0 + 5.0 * pos ** 0.85)[None] + nrm((L, RWKV_W), 0.1),
        "rwkv_w2": nrm((L, RWKV_DECAY_LORA, RWKV_W), 0.1 * RWKV_DECAY_LORA ** -0.5),
        "rwkv_a0": nrm((L, RWKV_W), 0.1),
        "rwkv_a2": nrm((L, RWKV_AAA_LORA, RWKV_W), 0.1 * RWKV_AAA_LORA ** -0.5),
        "rwkv_g2": nrm((L, RWKV_GATE_LORA, RWKV_W), RWKV_GATE_LORA ** -0.5),
        "rwkv_k_k": 0.85 + nrm((L, RWKV_W), 0.05),
        "rwkv_k_a": 1.0 + nrm((L, RWKV_W), 0.05),
        "rwkv_r_k": nrm((L, RWKV_HEADS, RWKV_HEAD), 0.1),
        "rwkv_ln_w": gain(RWKV_W),
        "rwkv_ln_b": nrm((L, RWKV_W), 0.02),
        "w_up_gla": nrm((L, GLA_VAL_W, D_MODEL), GLA_VAL_W ** -0.5),
        "w_up_rwkv": nrm((L, RWKV_W, D_MODEL), RWKV_W ** -0.5),
        "w_out": nrm((L, D_MODEL, D_MODEL), D_MODEL ** -0.5),
        "mix_post_norm": gain(D_MODEL),
        "ffn2_pre_norm": gain(D_MODEL),
        "ffn2_w_gate": nrm((L, D_MODEL, D_FF), D_MODEL ** -0.5),
        "ffn2_w_up": nrm((L, D_MODEL, D_FF), D_MODEL ** -0.5),
        "ffn2_w_down": nrm((L, D_FF, D_MODEL), D_FF ** -0.5),
        "ffn2_post_norm": gain(D_MODEL),
    }


def reference(x, ffn1_pre_norm, ffn1_w_gate, ffn1_w_up, ffn1_w_down, ffn1_post_norm,
              mix_pre_norm, w_in, gla_gate_up, gla_gate_bias, gla_out_norm,
              rwkv_shift_mix, rwkv_w0, rwkv_w2, rwkv_a0, rwkv_a2, rwkv_g2, rwkv_k_k, rwkv_k_a,
              rwkv_r_k, rwkv_ln_w, rwkv_ln_b, w_up_gla, w_up_rwkv, w_out, mix_post_norm,
              ffn2_pre_norm, ffn2_w_gate, ffn2_w_up, ffn2_w_down, ffn2_post_norm):
    h = x
    for l in range(DEPTH):
        f = swiglu(rms_norm(h, ffn1_pre_norm[l]), ffn1_w_gate[l], ffn1_w_up[l], ffn1_w_down[l])
        h = h + MACARON_WEIGHT * rms_norm(f, ffn1_post_norm[l])

        u = rms_norm(h, mix_pre_norm[l])
        proj = u @ w_in[l]
        p_gla = proj[..., :GLA_COLS]
        p_rwkv = proj[..., GLA_COLS:GLA_COLS + RWKV_COLS]
        gate_gla, gate_rwkv = _split(proj[..., GLA_COLS + RWKV_COLS:], (D_MODEL, D_MODEL))
        y_gla = gla_branch(p_gla, gla_gate_up[l], gla_gate_bias[l], gla_out_norm[l]) @ w_up_gla[l]
        y_rwkv = rwkv7_branch(p_rwkv, rwkv_shift_mix[l], rwkv_w0[l], rwkv_w2[l], rwkv_a0[l], rwkv_a2[l],
                              rwkv_g2[l], rwkv_k_k[l], rwkv_k_a[l], rwkv_r_k[l], rwkv_ln_w[l],
                              rwkv_ln_b[l]) @ w_up_rwkv[l]
        merged = jax.nn.sigmoid(gate_gla) * y_gla + jax.nn.sigmoid(gate_rwkv) * y_rwkv
        h = h + rms_norm(merged @ w_out[l], mix_post_norm[l])

        f = swiglu(rms_norm(h, ffn2_pre_norm[l]), ffn2_w_gate[l], ffn2_w_up[l], ffn2_w_down[l])
        h = h + MACARON_WEIGHT * rms_norm(f, ffn2_post_norm[l])
    return h
```

```python
import math
import os
import numpy as np
import concourse.bass as bass
import concourse.mybir as mybir
from concourse.bass_utils import run_bass_kernel_spmd

F32 = mybir.dt.float32
BF16 = mybir.dt.bfloat16
ALU = mybir.AluOpType
AF = mybir.ActivationFunctionType

NORM_EPS = 1e-6
RWKV_LN_EPS = 64e-5
C0 = math.exp(-0.5)
SAME_ENGINE_SYNC = os.environ.get("KSES", "0") == "1"


class Cfg:
    def __init__(s, D=4096, DFF=11008, GH=8, RW=2048, T=512, NPRE=4, NMAIN=4, stop_after="full"):
        s.D, s.DFF, s.GH, s.RW, s.T, s.NPRE, s.NMAIN = D, DFF, GH, RW, T, NPRE, NMAIN
        s.stop_after = stop_after
        s.NCH = D // 128
        s.NFF = DFF // 128
        s.DK, s.DV = 128, 256
        s.KEYW, s.VALW = GH * 128, GH * 256
        s.RH = RW // 64
        s.NP = s.RH // 2
        s.LW, s.LA, s.LG = 128, 128, 480
        s.GLA_COLS = 2 * s.KEYW + 2 * s.VALW + 16
        s.RWKV_COLS = 3 * RW + s.LW + s.LA + s.LG
        s.WIN = s.GLA_COLS + s.RWKV_COLS + 2 * D
        s.NT = NPRE + NMAIN
        s.NCK = T // 64
        s.NKS = -(-s.NFF // 43)
        assert s.NFF % s.NKS == 0
        s.KCD = s.NFF // s.NKS
        s.VC = s.VALW // 128
        o = 0
        ch = {}
        def take(name, n):
            nonlocal o
            lst = []
            r = n
            while r > 0:
                w = min(128, r)
                lst.append((o, w))
                o += w
                r -= w
            ch[name] = lst
        take("gq", s.KEYW); take("gk", s.KEYW); take("gv", s.VALW); take("gg", 16); take("go", s.VALW)
        take("rr", RW); take("rw", s.LW); take("rk", RW); take("rv", RW); take("ra", s.LA); take("rg", s.LG)
        take("ga", D); take("gb", D)
        assert o == s.WIN
        s.win_ch = ch
        order = []
        for nm in ["gg", "gq", "gk", "gv", "go", "rw", "ra", "rg"]:
            order += [(nm, i) for i in range(len(ch[nm]))]
        for p in range(s.NP):
            order += [("rr", p), ("rk", p), ("rv", p)]
        for c in range(s.NCH):
            order += [("ga", c), ("gb", c)]
        s.win_order = order
        s.win_index = {k: i for i, k in enumerate(order)}
        s.rw_chunks = ([("rr", p) for p in range(s.NP)] + [("rw", 0)] + [("rk", p) for p in range(s.NP)]
                       + [("rv", p) for p in range(s.NP)] + [("ra", 0)] + [("rg", i) for i in range(4)])
        s.rw_cidx = {k: i for i, k in enumerate(s.rw_chunks)}


FULL = Cfg()
LAST_TK = None


EP = 20000
EPD = 1500


class Tracker:
    def __init__(s, nc):
        s.nc = nc
        s.engs = {"pe": nc.tensor, "dve": nc.vector, "act": nc.scalar, "pool": nc.gpsimd, "sp": nc.sync}
        s.ops = {k: [] for k in s.engs}
        s.cnt = {k: 0 for k in s.engs}
        s.seen = {k: {} for k in s.engs}
        s.last_w = {}
        s.readers = {}
        s.streams = {}
        s.groups = []
        s.sems = {}

    def _deps(s, reads, writes):
        deps = []
        for b in list(reads) + list(writes):
            t = s.last_w.get(b)
            if t is not None:
                deps.append(t)
        for b in writes:
            deps += s.readers.get(b, [])
        return deps

    def _waits(s, eng, deps, pe_acc=False):
        out = []
        seen = s.seen[eng]
        for t in deps:
            if t[0] == "e":
                if t[1] == eng:
                    if eng in ("pe", "sp", "pool") or not SAME_ENGINE_SYNC:
                        continue
                key = ("e", t[1])
                if seen.get(key, -1) >= t[2]:
                    continue
                seen[key] = t[2]
                out.append(t)
            elif t[0] == "d":
                key = ("d", t[1])
                if seen.get(key, 0) >= t[2]:
                    continue
                seen[key] = t[2]
                out.append(t)
            else:
                key = ("g", t[1])
                if key in seen:
                    continue
                seen[key] = 1
                out.append(t)
        best = {}
        for t in out:
            k = (t[0], t[1])
            if k not in best or best[k][2 if t[0] != "g" else 1] < t[2 if t[0] != "g" else 1]:
                best[k] = t
        return list(best.values())

    def _record(s, tok, reads, writes):
        for b in reads:
            s.readers.setdefault(b, []).append(tok)
        for b in writes:
            s.last_w[b] = tok
            s.readers[b] = []

    def op(s, eng, fn, reads=(), writes=()):
        deps = s._deps(reads, writes)
        for b in reads:
            if isinstance(b, tuple) and b[0] == "ps":
                deps += [t for t in s.readers.get(b, []) if not (t[0] == "e" and t[1] == eng)]
        waits = s._waits(eng, deps)
        idx = s.cnt[eng]
        s.cnt[eng] += 1
        tok = ("e", eng, idx)
        s.ops[eng].append((fn, waits, tok))
        s._record(tok, reads, writes)
        return tok

    def dma(s, queue, fn, reads=(), writes=(), stream=None, group=None):
        deps = s._deps(reads, writes)
        waits = s._waits(queue, deps)
        if group is not None:
            tok = ("g", group)
        else:
            n = s.streams.get(stream, 0) + 1
            s.streams[stream] = n
            tok = ("d", stream, n)
        s.ops[queue].append((fn, waits, tok))
        s._record(tok, reads, writes)
        return tok

    def barrier(s):
        toks = []
        for e in s.engs:
            if s.cnt[e] > 0 and e not in ("sp",):
                toks.append(("e", e, s.cnt[e] - 1))
        for st, n in s.streams.items():
            toks.append(("d", st, n))
        for e in s.engs:
            w = s._waits(e, toks)
            if w:
                s.ops[e].append((None, w, None))

    def _sem(s, key):
        if key not in s.sems:
            s.sems[key] = s.nc.alloc_semaphore("s_" + "_".join(str(k) for k in key))
        return s.sems[key]

    def _wait_args(s, t):
        if t[0] == "e":
            return s._sem(("e", t[1], t[2] // EP)), (t[2] % EP) + 1
        if t[0] == "d":
            n = t[2] - 1
            return s._sem(("d", t[1], n // EPD)), 16 * ((n % EPD) + 1)
        return s._sem(("g", t[1])), 16 * s.groups[t[1]]

    def emit(s, block):
        def run(engname):
            def body(eng):
                for fn, waits, tok in s.ops[engname]:
                    for t in waits:
                        sem, val = s._wait_args(t)
                        eng.wait_ge(sem, val)
                    if fn is None:
                        continue
                    ins = fn(eng)
                    if tok[0] == "e":
                        ins.then_inc(s._sem(("e", tok[1], tok[2] // EP)), 1)
                    elif tok[0] == "d":
                        ins.then_inc(s._sem(("d", tok[1], (tok[2] - 1) // EPD)), 16)
                    else:
                        ins.then_inc(s._sem(("g", tok[1])), 16)
            return body
        block.tensor(run("pe"))
        block.vector(run("dve"))
        block.scalar(run("act"))
        block.gpsimd(run("pool"))
        block.sync(run("sp"))


def weight_specs(cfg):
    KC = cfg.NCH
    return {
        "g1": (KC, cfg.NFF), "u1": (KC, cfg.NFF), "d1": (cfg.KCD, cfg.NCH * cfg.NKS),
        "win": (KC, len(cfg.win_order)),
        "upg": (cfg.VC, cfg.NCH), "upr": (cfg.NP, cfg.NCH), "wo": (KC, cfg.NCH),
        "g2": (KC, cfg.NFF), "u2": (KC, cfg.NFF), "d2": (cfg.KCD, cfg.NCH * cfg.NKS),
        "w2l": (1, cfg.NP), "a2l": (1, cfg.NP), "g2l": (4, cfg.NP),
    }


def build(cfg):
    nc = bass.Bass("TRN2", target_bir_lowering=False)
    tk = Tracker(nc)
    T, NCH, NFF, NP, NCK, D = cfg.T, cfg.NCH, cfg.NFF, cfg.NP, cfg.NCK, cfg.D
    GH, VC = cfg.GH, cfg.VC
    specs = weight_specs(cfg)

    xT = nc.dram_tensor("xT", [cfg.NT, 128, NCH * T], F32, kind="ExternalInput").ap()
    yT = nc.dram_tensor("yT", [cfg.NMAIN, 128, NCH * T], F32, kind="ExternalOutput").ap()
    hs = nc.dram_tensor("hs", [128, NCH * T], F32, kind="Internal").ap()
    wsrc, wcache = {}, {}
    for nm, (kct, nt) in specs.items():
        wsrc[nm] = nc.dram_tensor("w_" + nm, [nt, 128, kct * 128], F32, kind="ExternalInput").ap()
        wcache[nm] = nc.dram_tensor("c_" + nm, [nt, 128, kct * 128], BF16, kind="Internal").ap()
    NRC = len(cfg.rw_chunks)
    pspec = {
        "gains": [128, 6 * NCH], "rmix": [128, NRC], "rvec": [128, 7 * NP], "gbias": [128, GH],
        "gonorm": [128, 2], "gup": [16, cfg.KEYW], "cmask": [128, 4 * 64], "ident": [128, 128], "bones": [128, 128],
        "scanm": [128, T],
    }
    pin = {k: nc.dram_tensor("p_" + k, v, F32, kind="ExternalInput").ap() for k, v in pspec.items()}

    base = (nc.sbuf_base + 63) // 64 * 64
    total = nc.sbuf_top - base
    arena = nc.alloc_sbuf_tensor("arena", [128, total // 4 - 8], F32)
    cur = [base]

    def alloc(name, shape, dt, at=None):
        nb = int(np.prod(shape[1:])) * (2 if dt == BF16 else 4)
        nb = (nb + 63) // 64 * 64
        if at is None:
            off = cur[0]
            cur[0] += nb
        else:
            off = at
        assert off + nb <= nc.sbuf_top, (name, off, nb, nc.sbuf_top)
        return nc.alloc_sbuf_tensor_at(name, list(shape), dt, offset=off)

    gains = alloc("gains", [128, 6 * NCH], F32)
    rmix = alloc("rmix", [128, NRC], F32)
    rvec = alloc("rvec", [128, 7 * NP], F32)
    gbias_n = alloc("gbias_n", [128, GH], F32)
    gonorm = alloc("gonorm", [128, 2], F32)
    gup_bf = alloc("gup_bf", [16, cfg.KEYW], BF16)
    cmask = alloc("cmask", [128, 4, 64], F32)
    identb = alloc("identb", [128, 128], BF16)
    ident8 = alloc("ident8", [128, NCK, 64], BF16)
    bones = alloc("bones", [128, 128], BF16)
    bones64 = alloc("bones64", [128, 128], BF16)
    ones = alloc("ones", [128, 128], BF16)
    scanm = alloc("scanm", [128, T], F32)
    Sg = alloc("Sg", [128, GH, 256], F32)
    Sg_bf = alloc("Sg_bf", [128, GH, 256], BF16)
    Hr = alloc("Hr", [128, NP, 64], F32)
    Hr_bf = alloc("Hr_bf", [128, NP, 64], BF16)
    carry = alloc("carry", [128, NRC], F32)
    NSLOT = 3
    WSLOT = 43 * 128
    wslots = [alloc(f"wslot{i}", [128, WSLOT], BF16) for i in range(NSLOT)]
    tmpf = [alloc(f"tmpf{i}", [128, T], F32) for i in range(3)]
    tmpb = [alloc(f"tmpb{i}", [128, T], BF16) for i in range(2)]
    rstd = alloc("rstd", [128, T], F32)
    BA = cur[0]
    BA_SZ = NCH * T * 2
    BB = BA + BA_SZ
    BB_SZ = max(NFF * T * 2, NCH * T * 4, 2 * VC * T * 2 + 54 * 1024)
    assert BB + BB_SZ <= nc.sbuf_top, ("sbuf overflow", BB + BB_SZ, nc.sbuf_top)
    xn = alloc("xn", [128, NCH, T], BF16, at=BA)
    fbf = alloc("fbf", [128, NCH, T], BF16, at=BA)
    act = alloc("act", [128, NFF, T], BF16, at=BB)
    hT = alloc("hT", [128, NCH, T], F32, at=BB)

    psb = [nc.alloc_psum_tensor(f"ps{i}", [128, 512], F32) for i in range(8)]
    ps_i = [0]

    reserved = set()

    def nps():
        while True:
            i = ps_i[0] % 8
            ps_i[0] += 1
            if i not in reserved:
                return i

    def PS(b):
        return ("ps", b)

    rr = {"ev": 0}

    def mm(out, lhsT, rhs, start, stop, reads, writes):
        tk.op("pe", lambda e: e.matmul(out, lhsT=lhsT, rhs=rhs, start=start, stop=stop), reads=reads, writes=writes)

    def act_op(out, in_, func, reads, writes, bias=None, scale=None):
        kw = {}
        if bias is not None:
            kw["bias"] = bias
        if scale is not None:
            kw["scale"] = scale
        tk.op("act", lambda e: e.activation(out=out, in_=in_, func=func, **kw), reads=reads, writes=writes)

    def tt(out, in0, in1, op, reads, writes):
        tk.op("dve", lambda e: e.tensor_tensor(out=out, in0=in0, in1=in1, op=op), reads=reads, writes=writes)

    def ttp(out, in0, in1, op, reads, writes):
        if os.environ.get("KNOPOOL", "0") == "1":
            return tt(out, in0, in1, op, reads, writes)
        tk.op("pool", lambda e: e.tensor_tensor(out=out, in0=in0, in1=in1, op=op), reads=reads, writes=writes)

    def ts(out, in0, s1, s2, op0, op1, reads, writes):
        if op1 is None:
            tk.op("dve", lambda e: e.tensor_scalar(out=out, in0=in0, scalar1=s1, scalar2=None, op0=op0), reads=reads, writes=writes)
        else:
            tk.op("dve", lambda e: e.tensor_scalar(out=out, in0=in0, scalar1=s1, scalar2=s2, op0=op0, op1=op1), reads=reads, writes=writes)

    def stt(out, in0, scalar, in1, op0, op1, reads, writes):
        tk.op("dve", lambda e: e.scalar_tensor_tensor(out=out, in0=in0, scalar=scalar, in1=in1, op0=op0, op1=op1), reads=reads, writes=writes)

    def rsqrt(out, in_, mul, add, reads, wid, clamp=None):
        if clamp is not None:
            ts(out, in_, clamp, None, ALU.max, None, reads, [wid])
        else:
            ts(out, in_, mul, add, ALU.mult, ALU.add, reads, [wid])
        act_op(out, out, AF.Ln, [wid], [wid])
        act_op(out, out, AF.Exp, [wid], [wid], scale=-0.5)

    def copy_any(out, in_, reads, writes):
        rr["ev"] += 1
        if rr["ev"] % 2:
            act_op(out, in_, AF.Copy, reads, writes)
        else:
            tk.op("dve", lambda e: e.tensor_copy(out=out, in_=in_), reads=reads, writes=writes)

    conv_order = ["g1", "u1", "d1", "win", "upg", "upr", "wo", "g2", "u2", "d2"]
    conv_list = []
    for j in range(NFF):
        conv_list += [("g1", j), ("u1", j)]
    conv_list += [("d1", i) for i in range(specs["d1"][1])]
    conv_list += [("win", i) for i in range(specs["win"][1])]
    for p in range(NP):
        conv_list += [("w2l", p), ("a2l", p), ("g2l", p)]
    for c in range(NCH):
        conv_list += [("upg", c), ("upr", c)]
    conv_list += [("wo", c) for c in range(NCH)]
    for j in range(NFF):
        conv_list += [("g2", j), ("u2", j)]
    conv_list += [("d2", i) for i in range(specs["d2"][1])]
    EARLY_WIN = ("gg", "gk", "gv", "rw", "ra", "rk", "rv")
    def is_early(nm, t):
        if nm in ("g1", "u1", "d1", "w2l", "a2l"):
            return True
        return nm == "win" and cfg.win_order[t][0] in EARLY_WIN
    early = [x for x in conv_list if is_early(*x)]
    late = [x for x in conv_list if not is_early(*x)]
    late.sort(key=lambda x: 0 if (x[0] == "win" and cfg.win_order[x[1]][0] in ("rr", "rg")) else (1 if x[0] == "g2l" else 2))
    conv_state = {"g": 0}

    def issue_group(grp, dep_reads=()):
        gi = conv_state["g"]
        conv_state["g"] += 1
        tk.groups.append(len(grp))
        for (nm, t) in grp:
            src, dst = wsrc[nm][t], wcache[nm][t]
            tk.dma("pool", lambda e, src=src, dst=dst: e.dma_start(out=dst, in_=src), reads=list(dep_reads), writes=[("wc", nm, t)], group=gi)

    i = 0
    for gs in [2, 4, 8, 16] + [32] * 1000:
        if i >= len(early):
            break
        issue_group(early[i:i + gs])
        i += gs
    LG_SZ = 12
    late_groups = [late[i:i + LG_SZ] for i in range(0, len(late), LG_SZ)]
    first_rel = 1 if cfg.NPRE >= 2 else 0
    slots = [(ti_, j_) for ti_ in range(first_rel, max(cfg.NPRE, 1)) for j_ in range(NFF)]
    release_plan = {}
    for gi_, grp in enumerate(late_groups):
        sl_ = slots[min(len(slots) - 1, gi_ * len(slots) // len(late_groups))]
        release_plan.setdefault(sl_, []).append(grp)
    cur_tile = [0]

    def load(dst_ap, src_ap, bufid, queue="sp"):
        tk.dma(queue, lambda e: e.dma_start(out=dst_ap, in_=src_ap), reads=(), writes=[bufid], stream=("ld", bufid))

    load(gains[:], pin["gains"], "gains")
    load(rmix[:], pin["rmix"], "rmix")
    load(rvec[:], pin["rvec"], "rvec")
    load(gonorm[:], pin["gonorm"], "gonorm")
    load(scanm[:], pin["scanm"], "scanm")
    load(cmask[:].rearrange("p a b -> p (a b)"), pin["cmask"], "cmask")
    stg = alloc("stg", [128, max(cfg.KEYW, 128)], F32, at=BB)
    def load_cast(dst, src, np_, ncols, name):
        load(stg[0:np_, 0:ncols], src, "stg")
        tk.op("dve", lambda e: e.tensor_copy(out=dst, in_=stg[0:np_, 0:ncols]), reads=["stg"], writes=[name])
    load_cast(gup_bf[:], pin["gup"], 16, cfg.KEYW, "gup_bf")
    load_cast(identb[:], pin["ident"], 128, 128, "identb")
    load_cast(bones[:], pin["bones"], 128, 128, "bones")
    load(stg[:, 0:GH], pin["gbias"], "stg")
    ts(gbias_n[:], stg[:, 0:GH], -1.0, None, ALU.mult, None, ["stg"], ["gbias_n"])
    ts(gains[:, NCH:2 * NCH], gains[:, NCH:2 * NCH], 0.5, None, ALU.mult, None, ["gains"], ["gains"])
    ts(gains[:, 5 * NCH:6 * NCH], gains[:, 5 * NCH:6 * NCH], 0.5, None, ALU.mult, None, ["gains"], ["gains"])
    ts(bones64[:], bones[:], 1.0 / 64.0, None, ALU.mult, None, ["bones"], ["bones64"])
    tk.op("dve", lambda e: e.memset(ones[:], 1.0), reads=(), writes=["ones"])
    for c in range(NCK):
        tk.op("dve", lambda e, c=c: e.tensor_copy(out=ident8[0:64, c, :], in_=identb[0:64, 0:64]), reads=["identb"], writes=["ident8"])
        tk.op("dve", lambda e, c=c: e.tensor_copy(out=ident8[64:128, c, :], in_=identb[64:128, 64:128]), reads=["identb"], writes=["ident8"])
    tk.op("dve", lambda e: e.memset(Sg[:], 0.0), reads=(), writes=["Sg"])
    tk.op("dve", lambda e: e.memset(Sg_bf[:], 0.0), reads=(), writes=["Sg_bf"])
    tk.op("dve", lambda e: e.memset(Hr[:], 0.0), reads=(), writes=["Hr"])
    tk.op("dve", lambda e: e.memset(Hr_bf[:], 0.0), reads=(), writes=["Hr_bf"])
    tk.op("dve", lambda e: e.memset(carry[:], 0.0), reads=(), writes=["carry"])
    tk.barrier()

    wstate = {"n": 0}

    def wload(nm, t):
        kct = specs[nm][0]
        si = wstate["n"] % NSLOT
        wstate["n"] += 1
        sl = wslots[si]
        src = wcache[nm][t]
        dst = sl[:, 0:kct * 128]
        tk.dma("sp", lambda e: e.dma_start(out=dst, in_=src), reads=[("wc", nm, t)], writes=[("ws", si)], stream=("ws", si))
        return sl[:, 0:kct * 128].rearrange("p (k n) -> p k n", n=128), ("ws", si)

    class WStream:
        def __init__(s, tiles, depth=NSLOT - 1):
            s.tiles, s.depth, s.q, s.i = tiles, depth, [], 0
            for _ in range(min(depth, len(tiles))):
                s._issue()
        def _issue(s):
            s.q.append(wload(*s.tiles[s.i]))
            s.i += 1
        def next(s):
            r = s.q.pop(0)
            if s.i < len(s.tiles):
                s._issue()
            return r

    def dense(ws, kct, rhs_fn, rhs_ids, out_ps, ncols=128, np_=128, nfree=T, first=True, last=True, kparts=None):
        wt, wid = ws.next()
        for kc in range(kct):
            kp = 128 if kparts is None else kparts[kc]
            mm(psb[out_ps][0:ncols, 0:nfree], wt[0:kp, kc, 0:ncols], rhs_fn(kc, kp),
               first and kc == 0, last and kc == kct - 1, [wid] + rhs_ids(kc), [PS(out_ps)])

    def sumsq_accum(src_fn, src_ids, nchunks, ps_bank, lhs=None, from_psum_ids=None):
        for c in range(nchunks):
            tb = tmpb[c % 2]
            act_op(tb[:], src_fn(c), AF.Square, src_ids(c), [("tmpb", c % 2)])
            mm(psb[ps_bank][:, 0:T], ones[:], tb[:], c == 0, c == nchunks - 1, ["ones", ("tmpb", c % 2)], [PS(ps_bank)])

    def rstd_from(ps_bank, n, eps):
        rsqrt(rstd[:], psb[ps_bank][:, 0:T], 1.0 / n, eps, [PS(ps_bank)], "rstd")

    def prenorm(gi_):
        b = nps()
        sumsq_accum(lambda c: hT[:, c, :], lambda c: [("hT", c)], NCH, b)
        rstd_from(b, D, NORM_EPS)
        for c in range(NCH):
            stt(xn[:, c, :], hT[:, c, :], gains[:, gi_ * NCH + c:gi_ * NCH + c + 1], rstd[:], ALU.mult, ALU.mult,
                [("hT", c), "gains", "rstd"], [("xn", c)])

    def post_residual(gi_, ps_ss, final_out=None):
        rstd_from(ps_ss, D, NORM_EPS)
        tk.dma("sp", lambda e: e.dma_start(out=hT[:].rearrange("p c t -> p (c t)"), in_=hs),
               reads=["hs"], writes=[("hT", c) for c in range(NCH)] + [("act", j) for j in range(NFF)] + ["BBall"], stream="hld")
        for c in range(NCH):
            tf = tmpf[c % 3]
            stt(tf[:], fbf[:, c, :], gains[:, gi_ * NCH + c:gi_ * NCH + c + 1], rstd[:], ALU.mult, ALU.mult,
                [("xn", c), "gains", "rstd"], [("tmpf", c % 3)])
            tt(hT[:, c, :], tf[:], hT[:, c, :], ALU.add, [("tmpf", c % 3), ("hT", c)], [("hT", c)])
        dst = hs if final_out is None else final_out
        tk.dma("sp", lambda e: e.dma_start(out=dst, in_=hT[:].rearrange("p c t -> p (c t)")),
               reads=[("hT", c) for c in range(NCH)], writes=["hs" if final_out is None else "yout"], stream="hst")

    def ffn(gn, un, dn, gpre, gpost, final_out=None):
        prenorm(gpre)
        tiles = []
        for j in range(NFF):
            tiles += [(gn, j), (un, j)]
        ws = WStream(tiles)
        for j in range(NFF):
            pg, pu = nps(), nps()
            dense(ws, NCH, lambda kc, kp: xn[:, kc, :], lambda kc: [("xn", kc)], pg)
            dense(ws, NCH, lambda kc, kp: xn[:, kc, :], lambda kc: [("xn", kc)], pu)
            tf = tmpf[j % 3]
            act_op(tf[:], psb[pg][:, 0:T], AF.Silu, [PS(pg)], [("tmpf", j % 3)])
            tt(act[:, j, :], tf[:], psb[pu][:, 0:T], ALU.mult, [("tmpf", j % 3), PS(pu)], [("act", j)] + ([("hT", j // 2)] if j // 2 < NCH else []))
            if gn == "g1":
                for grp in release_plan.get((cur_tile[0], j), []):
                    issue_group(grp, dep_reads=[("act", j)])
        if os.environ.get("KDBG", "") == "gu":
            return
        pss = nps()
        ws = WStream([(dn, i) for i in range(NCH * cfg.NKS)])
        for c in range(NCH):
            pf = nps()
            while pf == pss:
                pf = nps()
            for ks in range(cfg.NKS):
                k0 = ks * cfg.KCD
                dense(ws, cfg.KCD, lambda kc, kp, k0=k0: act[:, k0 + kc, :], lambda kc, k0=k0: [("act", k0 + kc)], pf,
                      first=(ks == 0), last=(ks == cfg.NKS - 1))
            if c > 0:
                mm(psb[pss][:, 0:T], ones[:], tmpb[(c - 1) % 2][:], c - 1 == 0, False, ["ones", ("tmpb", (c - 1) % 2)], [PS(pss)])
            copy_any(fbf[:, c, :], psb[pf][:, 0:T], [PS(pf)], [("xn", c)])
            tb = tmpb[c % 2]
            act_op(tb[:], psb[pf][:, 0:T], AF.Square, [PS(pf)], [("tmpb", c % 2)])
        mm(psb[pss][:, 0:T], ones[:], tmpb[(NCH - 1) % 2][:], NCH - 1 == 0, True, ["ones", ("tmpb", (NCH - 1) % 2)], [PS(pss)])
        if os.environ.get("KDBG", "") == "down":
            return
        post_residual(gpost, pss, final_out)

    class Bump:
        def __init__(s, start, end):
            s.o, s.end = start, end
        def get(s, name, shape, dt):
            nb = (int(np.prod(shape[1:])) * (2 if dt == BF16 else 4) + 63) // 64 * 64
            t_ = alloc(name, shape, dt, at=s.o)
            s.o += nb
            assert s.o <= s.end, ("mixer working set overflow", name, s.o, s.end)
            return t_

    uid = [0]
    def U(p):
        uid[0] += 1
        return f"{p}{uid[0]}"

    def win_tiles(keys):
        return [("win", cfg.win_index[k]) for k in keys]

    def proj_u(ws, key, ps_bank):
        ncols = cfg.win_ch[key[0]][key[1]][1]
        dense(ws, NCH, lambda kc, kp: xn[:, kc, :], lambda kc: [("xn", kc)], ps_bank, ncols=ncols)
        return ncols

    def to_tok(dst, dst_id, srcT, src_id, ncols=128):
        per_bank = 512 // ncols
        c = 0
        while c < NCK:
            b = nps()
            n = min(per_bank, NCK - c)
            for i in range(n):
                mm(psb[b][0:64, i * ncols:(i + 1) * ncols], srcT[0:ncols, (c + i) * 64:(c + i + 1) * 64], identb[0:ncols, 0:ncols],
                   True, True, [src_id, "identb"], [PS(b)])
            copy_any(dst[:, c:c + n, :], psb[b][0:64, 0:n * ncols].rearrange("p (a b) -> p a b", b=ncols), [PS(b)], [dst_id])
            c += n

    def gla(full, yg):
        bp = Bump(BB + (2 * VC * T * 2 if True else 0), BB + BB_SZ)
        ggT = bp.get(U("ggT"), [16, T], BF16)
        cs = bp.get(U("gcs"), [128, T], F32)
        ex = bp.get(U("gex"), [128, T], F32)
        eq = bp.get(U("geq"), [128, T], F32)
        ek = bp.get(U("gek"), [128, T], F32)
        el = bp.get(U("gel"), [128, T], F32)
        edec = bp.get(U("gedec"), [128, NCK], F32)
        qt = bp.get(U("gqt"), [128, T], BF16)
        kt = bp.get(U("gkt"), [128, T], BF16)
        khT = bp.get(U("gkhT"), [128, T], BF16)
        vT = [bp.get(U("gvT"), [128, T], BF16) for _ in range(2)]
        vtok = bp.get(U("gvtok"), [64, NCK, 256], BF16)
        khtok = bp.get(U("gkhtok"), [64, NCK, 128], BF16)
        attn = bp.get(U("gattn"), [64, NCK, 64], BF16)
        sgo = [bp.get(U("gsgo"), [128, T], F32) for _ in range(2)]
        ws = WStream(win_tiles([("gg", 0)]))
        b = nps()
        proj_u(ws, ("gg", 0), b)
        copy_any(ggT[:], psb[b][0:16, 0:T], [PS(b)], ["ggT"])
        for h in range(GH):
            keys = [("gq", h), ("gk", h), ("gv", 2 * h), ("gv", 2 * h + 1)] + ([("go", 2 * h), ("go", 2 * h + 1)] if full else [])
            if not full:
                keys = [("gk", h), ("gv", 2 * h), ("gv", 2 * h + 1)]
            ws = WStream(win_tiles(keys))
            b = nps()
            mm(psb[b][:, 0:T], gup_bf[:, h * 128:(h + 1) * 128], ggT[:], True, True, ["gup_bf", "ggT"], [PS(b)])
            act_op(ex[:], psb[b][:, 0:T], AF.Exp, [PS(b), "gbias_n"], ["gex"], bias=gbias_n[:, h:h + 1], scale=-1.0)
            act_op(ex[:], ex[:], AF.Ln, ["gex"], ["gex"], bias=1.0)
            tk.op("dve", lambda e: e.tensor_tensor_scan(out=cs[:], data0=scanm[:], data1=ex[:], initial=0.0, op0=ALU.mult, op1=ALU.add),
                  reads=["scanm", "gex"], writes=["gcs"])
            cs3 = cs[:].rearrange("p (c t) -> p c t", t=64)
            csl = cs3[:, :, 63:64]
            tt(el[:].rearrange("p (c t) -> p c t", t=64), csl.to_broadcast([128, NCK, 64]), cs3, ALU.subtract, ["gcs"], ["gel"])
            act_op(el[:], el[:], AF.Exp, ["gel"], ["gel"], scale=-1.0 / 16)
            act_op(edec[:].rearrange("p (c o) -> p c o", o=1), csl, AF.Exp, ["gcs"], ["gedec"], scale=-1.0 / 16)
            act_op(ek[:], cs[:], AF.Exp, ["gcs"], ["gek"], scale=1.0 / 16)
            if full:
                act_op(eq[:], cs[:], AF.Exp, ["gcs"], ["geq"], scale=-1.0 / 16, bias=float(math.log(128 ** -0.5)))
                b = nps()
                proj_u(ws, ("gq", h), b)
                tt(qt[:], psb[b][:, 0:T], eq[:], ALU.mult, [PS(b), "geq"], ["gqt"])
            b = nps()
            proj_u(ws, ("gk", h), b)
            if full:
                tt(kt[:], psb[b][:, 0:T], ek[:], ALU.mult, [PS(b), "gek"], ["gkt"])
            tt(khT[:], psb[b][:, 0:T], el[:], ALU.mult, [PS(b), "gel"], ["gkhT"])
            for hf in range(2):
                b = nps()
                proj_u(ws, ("gv", 2 * h + hf), b)
                copy_any(vT[hf][:], psb[b][:, 0:T], [PS(b)], [("gvT", hf)])
            for hf in range(2):
                c = 0
                while c < NCK:
                    b = nps()
                    n = min(4, NCK - c)
                    for i in range(n):
                        mm(psb[b][0:64, i * 128:(i + 1) * 128], vT[hf][:, (c + i) * 64:(c + i + 1) * 64], identb[:], True, True,
                           [("gvT", hf), "identb"], [PS(b)])
                    copy_any(vtok[:, c:c + n, hf * 128:(hf + 1) * 128], psb[b][0:64, 0:n * 128].rearrange("p (a b) -> p a b", b=128),
                             [PS(b)], ["gvtok"])
                    c += n
            to_tok(khtok, "gkhtok", khT, "gkhT")
            if full:
                b = nps()
                for c in range(NCK):
                    mm(psb[b][0:64, c * 64:(c + 1) * 64], kt[:, c * 64:(c + 1) * 64], qt[:, c * 64:(c + 1) * 64], True, True,
                       ["gkt", "gqt"], [PS(b)])
                tt(attn[:], psb[b][0:64, 0:NCK * 64].rearrange("p (c t) -> p c t", t=64),
                   cmask[0:64, 1:2, :].to_broadcast([64, NCK, 64]), ALU.mult, [PS(b), "cmask"], ["gattn"])
                po = [nps(), nps()]
                reserved.update(po)
                bgo = [nps(), nps()]
                reserved.update(bgo)
                go_mm = []
                for hf in range(2):
                    wt_, wid_ = ws.next()
                    for kc in range(NCH):
                        go_mm.append((bgo[hf], wt_[:, kc, :], kc, wid_))
                per_step = -(-len(go_mm) // NCK)
            for c in range(NCK):
                if full:
                    for hf in range(2):
                        mm(psb[po[hf]][:, c * 64:(c + 1) * 64], vtok[:, c, hf * 128:(hf + 1) * 128], attn[:, c, :], True, False,
                           ["gvtok", "gattn"], [PS(po[hf])])
                        mm(psb[po[hf]][:, c * 64:(c + 1) * 64], Sg_bf[:, h, hf * 128:(hf + 1) * 128], qt[:, c * 64:(c + 1) * 64], False, True,
                           [("Sg_bf", h), "gqt"], [PS(po[hf])])
                b = nps()
                mm(psb[b][:, 0:256], khtok[:, c, :], vtok[:, c, :], True, True, ["gkhtok", "gvtok"], [PS(b)])
                stt(Sg_bf[:, h, :], Sg[:, h, :], edec[:, c:c + 1], psb[b][:, 0:256], ALU.mult, ALU.add,
                    [("Sg", h), "gedec", PS(b)], [("Sg_bf", h)])
                stt(Sg[:, h, :], Sg[:, h, :], edec[:, c:c + 1], psb[b][:, 0:256], ALU.mult, ALU.add,
                    [("Sg", h), "gedec", PS(b)], [("Sg", h)])
                if full:
                    for (bk, lw, kc, wid_) in go_mm[c * per_step:(c + 1) * per_step]:
                        mm(psb[bk][:, 0:T], lw, xn[:, kc, :], kc == 0, kc == NCH - 1, [wid_, ("xn", kc)], [PS(bk)])
            if full:
                for hf in range(2):
                    act_op(sgo[hf][:], psb[bgo[hf]][:, 0:T], AF.Silu, [PS(bgo[hf])], [("gsgo", hf)])
                reserved.difference_update(bgo)
                bn = nps()
                for hf in range(2):
                    tb = tmpb[hf]
                    act_op(tb[:], psb[po[hf]][:, 0:T], AF.Square, [PS(po[hf])], [("tmpb", hf)])
                    mm(psb[bn][:, 0:T], ones[:], tb[:], hf == 0, hf == 1, ["ones", ("tmpb", hf)], [PS(bn)])
                rsqrt(rstd[:], psb[bn][:, 0:T], 1.0 / 256, NORM_EPS, [PS(bn)], "rstd")
                for hf in range(2):
                    tf = tmpf[hf]
                    stt(tf[:], psb[po[hf]][:, 0:T], gonorm[:, hf:hf + 1], rstd[:], ALU.mult, ALU.mult,
                        [PS(po[hf]), "gonorm", "rstd"], [("tmpf", hf)])
                    tt(yg[:, 2 * h + hf, :], tf[:], sgo[hf][:], ALU.mult, [("tmpf", hf), ("gsgo", hf)], [("yg", 2 * h + hf)])
                reserved.difference_update(po)

    def rwkv(full, yr, carry_all):
        bp = Bump(BB + 2 * VC * T * 2, BB + BB_SZ)
        pbuf = [bp.get(U("rp"), [128, T + 1], F32) for _ in range(2)]
        twd = bp.get(U("twd"), [128, T], BF16)
        tad = bp.get(U("tad"), [128, T], BF16)
        sgd = [bp.get(U("sgd"), [128, T], BF16) for _ in range(4)]
        rq = bp.get(U("rq"), [128, T], F32)
        kq = bp.get(U("kq"), [128, T], F32)
        vq = bp.get(U("vq"), [128, T], F32)
        ld = bp.get(U("ld"), [128, T], F32)
        aa = bp.get(U("aa"), [128, T], F32)
        kap = bp.get(U("kap"), [128, T], F32)
        bbq = bp.get(U("bbq"), [128, T], F32)
        trT = bp.get(U("trT"), [128, T], BF16)
        edh2 = [bp.get(U("edh"), [128, NCK], F32) for _ in range(2)]
        KR2 = [bp.get(U("KR"), [128, NCK, 128], BF16) for _ in range(2)]
        bbar2 = [bp.get(U("bbar"), [128, T], BF16) for _ in range(2)]
        kbar2 = [bp.get(U("kbar"), [128, T], BF16) for _ in range(2)]
        vtok2 = [bp.get(U("rvtok"), [128, NCK, 64], BF16) for _ in range(2)]
        khtok2 = [bp.get(U("rkhtok"), [128, NCK, 64], BF16) for _ in range(2)]
        bhtok2 = [bp.get(U("rbhtok"), [128, NCK, 64], BF16) for _ in range(2)]
        gq2 = [tmpf[1], bp.get(U("gq1"), [128, T], F32)]
        bon2 = [tmpf[2], bp.get(U("bon1"), [128, T], F32)]
        GQ2 = [("tmpf", 1), "gq1"]
        BON2 = [("tmpf", 2), "bon1"]
        oY = bp.o
        Y = bp.get(U("Y"), [128, NCK, 64], BF16)
        YT = bp.get(U("YT"), [128, NCK, 64], BF16)
        oI = bp.o
        IYT = bp.get(U("IYT"), [128, NCK, 64], BF16)
        oG = bp.o
        G0 = bp.get(U("G0"), [128, NCK, 64], BF16)
        G1 = bp.get(U("G1"), [128, NCK, 64], BF16)
        AkT = bp.get(U("AkT"), [128, NCK, 64], BF16)
        BkT = bp.get(U("BkT"), [128, NCK, 64], BF16)
        BbT = bp.get(U("BbT"), [128, NCK, 64], BF16)
        rhs_sb = bp.get(U("rhs"), [128, 64], BF16)
        nU = bp.get(U("nU"), [128, 64], BF16)
        assert NCK * 64 * 2 * 2 >= T * 4 and NCK * 64 * 2 >= T * 2
        ysb = alloc(U("ysb"), [128, T], F32, at=oY)
        ybf = alloc(U("ybf"), [128, T], BF16, at=oI)
        e1b = alloc(U("e1b"), [128, T], F32, at=oG)
        YS, YB, EB = ["Y", "YT"], ["IYT"], ["G0", "G1"]
        dtmp, csr, e2, e3 = aa, kq, ld, aa
        e1, kmod = tmpf[0], rstd
        E1, KMOD = ("tmpf", 0), "rstd"
        HS = (slice(0, 64), slice(64, 128))

        def shifted(key, ws, dst, dst_id, func=None):
            ci = cfg.rw_cidx[key]
            pb = pbuf[ci % 2]
            pid = ("rp", ci % 2)
            b = nps()
            ncols = proj_u(ws, key, b)
            act_op(pb[0:ncols, 1:T + 1], psb[b][0:ncols, 0:T], AF.Copy, [PS(b)], [pid])
            act_op(pb[0:ncols, 0:1], carry[0:ncols, ci:ci + 1], AF.Copy, [("carry", ci)], [pid])
            act_op(carry[0:ncols, ci:ci + 1], pb[0:ncols, T:T + 1], AF.Copy, [pid], [("carry", ci)])
            if dst is None:
                return ncols
            ttp(dtmp[0:ncols, :], pb[0:ncols, 0:T], pb[0:ncols, 1:T + 1], ALU.subtract, [pid], ["aa"])
            if func is None:
                stt(dst[0:ncols, :], dtmp[0:ncols, :], rmix[0:ncols, ci:ci + 1], pb[0:ncols, 1:T + 1], ALU.mult, ALU.add,
                    ["aa", "rmix", pid], [dst_id])
            else:
                stt(dtmp[0:ncols, :], dtmp[0:ncols, :], rmix[0:ncols, ci:ci + 1], pb[0:ncols, 1:T + 1], ALU.mult, ALU.add,
                    ["aa", "rmix", pid], ["aa"])
                act_op(dst[0:ncols, :], dtmp[0:ncols, :], func, ["aa"], [dst_id])
            return ncols

        keys = [("rw", 0), ("ra", 0)] + ([("rg", i) for i in range(4)] if (full or carry_all) else [])
        ws = WStream(win_tiles(keys))
        shifted(("rw", 0), ws, twd, "twd", AF.Tanh)
        shifted(("ra", 0), ws, tad, "tad", AF.Copy)
        if full or carry_all:
            for i in range(4):
                if full:
                    shifted(("rg", i), ws, sgd[i], ("sgd", i), AF.Sigmoid)
                else:
                    shifted(("rg", i), ws, None, None)
        V = lambda j, p: rvec[:, j * NP + p:j * NP + p + 1]
        c3 = lambda a_: a_[:].rearrange("p (c t) -> p c t", t=64)

        def stageA(p):
            q = p % 2
            KR, bbar, kbar, vtok, khtok, bhtok, edh = KR2[q], bbar2[q], kbar2[q], vtok2[q], khtok2[q], bhtok2[q], edh2[q]
            gq, bon, GQ, BON = gq2[q], bon2[q], GQ2[q], BON2[q]
            KRi, BBi, KBi, VTi, KHi, BHi, EDi = ("KR", q), ("bbar", q), ("kbar", q), ("rvtok", q), ("rkhtok", q), ("rbhtok", q), ("edh", q)
            tl = win_tiles(([("rr", p)] if (full or carry_all) else []) + [("rk", p), ("rv", p)])
            tl += [("w2l", p), ("a2l", p)] + ([("g2l", p)] if full else [])
            ws = WStream(tl)
            if full:
                shifted(("rr", p), ws, rq, "rq")
                yield
            elif carry_all:
                shifted(("rr", p), ws, None, None)
                yield
            shifted(("rk", p), ws, kq, "kq")
            yield
            shifted(("rv", p), ws, vq, "vq")
            yield
            wt, wid = ws.next()
            b = nps()
            mm(psb[b][:, 0:T], wt[:, 0, :], twd[:], True, True, [wid, "twd"], [PS(b)])
            act_op(ld[:], psb[b][:, 0:T], AF.Sigmoid, [PS(b), "rvec"], ["ld"], bias=V(0, p))
            wt, wid = ws.next()
            b = nps()
            mm(psb[b][:, 0:T], wt[:, 0, :], tad[:], True, True, [wid, "tad"], [PS(b)])
            act_op(aa[:], psb[b][:, 0:T], AF.Sigmoid, [PS(b), "rvec"], ["aa"], bias=V(1, p))
            if full:
                wt, wid = ws.next()
                b = nps()
                for kc in range(4):
                    kp = 128 if kc < 3 else cfg.LG - 384
                    mm(psb[b][:, 0:T], wt[0:kp, kc, :], sgd[kc][0:kp, :], kc == 0, kc == 3, [wid, ("sgd", kc)], [PS(b)])
                act_op(gq[:], psb[b][:, 0:T], AF.Copy, [PS(b)], [GQ])
            yield
            act_op(kap[:], kq[:], AF.Copy, ["kq", "rvec"], ["kap"], scale=V(2, p))
            act_op(tmpb[0][:], kq[:], AF.Square, ["kq", "rvec"], [("tmpb", 0)], scale=V(2, p))
            b = nps()
            mm(psb[b][:, 0:T], bones[:], tmpb[0][:], True, True, ["bones", ("tmpb", 0)], [PS(b)])
            rsqrt(e1[:], psb[b][:, 0:T], None, None, [PS(b)], E1, clamp=1e-18)
            ttp(kap[:], kap[:], e1[:], ALU.mult, ["kap", E1], ["kap"])
            yield
            ts(e1[:], aa[:], 1.0, V(3, p), ALU.subtract, ALU.mult, ["aa", "rvec"], [E1])
            stt(kmod[:], e1[:], 1.0, kq[:], ALU.add, ALU.mult, [E1, "kq"], [KMOD])
            ttp(bbq[:], aa[:], kap[:], ALU.mult, ["aa", "kap"], ["bbq"])
            if full:
                stt(tmpb[1][:], rq[:], V(4, p), kmod[:], ALU.mult, ALU.mult, ["rq", "rvec", KMOD], [("tmpb", 1)])
                b = nps()
                mm(psb[b][:, 0:T], bones[:], tmpb[1][:], True, True, ["bones", ("tmpb", 1)], [PS(b)])
                tt(bon[:], psb[b][:, 0:T], vq[:], ALU.mult, [PS(b), "vq"], [BON])
            yield
            tk.op("dve", lambda e: e.tensor_tensor_scan(out=csr[:], data0=scanm[:], data1=ld[:], initial=0.0, op0=ALU.mult, op1=ALU.add),
                  reads=["scanm", "ld", KMOD], writes=["kq"])
            cs3 = csr[:].rearrange("p (c t) -> p c t", t=64)
            csl = cs3[:, :, 63:64]
            ttp(e1[:], csr[:], ld[:], ALU.subtract, ["kq", "ld"], [E1])
            act_op(e1[:], e1[:], AF.Exp, [E1], [E1], scale=-C0)
            ttp(KR[:, :, 0:64], c3(kap), c3(e1), ALU.mult, ["kap", E1], [KRi])
            if full:
                act_op(e2[:], csr[:], AF.Exp, ["kq", E1], ["ld"], scale=-C0)
                ttp(KR[:, :, 64:128], c3(rq), c3(e2), ALU.mult, ["rq", "ld"], [KRi])
            yield
            act_op(e3[:], csr[:], AF.Exp, ["kq", "bbq"], ["aa"], scale=C0)
            ttp(bbar[:], bbq[:], e3[:], ALU.mult, ["bbq", "aa"], [BBi])
            ttp(kbar[:], kmod[:], e3[:], ALU.mult, [KMOD, "aa"], [KBi])
            tt(c3(e2), csl.to_broadcast([128, NCK, 64]), cs3, ALU.subtract, ["kq", KRi], ["ld"])
            act_op(e2[:], e2[:], AF.Exp, ["ld"], ["ld"], scale=-C0)
            act_op(edh[:].rearrange("p (c o) -> p c o", o=1), csl, AF.Exp, ["kq"], [EDi], scale=-C0)
            yield

            def to_tok_pair(dst, dst_id):
                bq = nps()
                for c in range(NCK):
                    for hs_ in HS:
                        mm(psb[bq][hs_, c * 64:(c + 1) * 64], trT[hs_, c * 64:(c + 1) * 64], identb[hs_, hs_], True, True,
                           ["trT", "identb"], [PS(bq)])
                copy_any(dst[:], psb[bq][:, 0:NCK * 64].rearrange("p (c t) -> p c t", t=64), [PS(bq)], [dst_id])

            act_op(trT[:], vq[:], AF.Copy, ["vq"], ["trT"])
            to_tok_pair(vtok, VTi)
            yield
            ttp(trT[:], kmod[:], e2[:], ALU.mult, [KMOD, "ld"], ["trT"])
            to_tok_pair(khtok, KHi)
            yield
            ttp(trT[:], bbq[:], e2[:], ALU.mult, ["bbq", "ld"], ["trT"])
            to_tok_pair(bhtok, BHi)
            yield

        def stageB(p):
            q = p % 2
            KR, bbar, kbar, vtok, khtok, bhtok, edh = KR2[q], bbar2[q], kbar2[q], vtok2[q], khtok2[q], bhtok2[q], edh2[q]
            gq, bon, GQ, BON = gq2[q], bon2[q], GQ2[q], BON2[q]
            KRi, BBi, KBi, VTi, KHi, BHi, EDi = ("KR", q), ("bbar", q), ("kbar", q), ("rvtok", q), ("rkhtok", q), ("rbhtok", q), ("edh", q)
            v1 = lambda bb_: psb[bb_][:, 0:NCK * 64].rearrange("p (c t) -> p c t", t=64)
            for (src, srcid, dA, dAid, mA, dB, dBid) in ((bbar, BBi, Y, "Y", 2, BbT, "BbT"), (kbar, KBi, AkT, "AkT", 0, BkT, "BkT")):
                c = 0
                while c < NCK:
                    b = nps()
                    n = min(4, NCK - c)
                    ncol = 128 if full else 64
                    for i in range(n):
                        for hs_ in HS:
                            mm(psb[b][hs_, i * 128:i * 128 + ncol], src[hs_, (c + i) * 64:(c + i + 1) * 64], KR[hs_, c + i, 0:ncol], True, True,
                               [srcid, KRi], [PS(b)])
                    v4 = psb[b][:, 0:n * 128].rearrange("p (a b) -> p a b", b=128)
                    tt(dA[:, c:c + n, :], v4[:, :, 0:64], cmask[:, mA:mA + 1, :].to_broadcast([128, n, 64]), ALU.mult, [PS(b), "cmask"], [dAid])
                    if full:
                        tt(dB[:, c:c + n, :], v4[:, :, 64:128], cmask[:, 1:2, :].to_broadcast([128, n, 64]), ALU.mult, [PS(b), "cmask"], [dBid])
                    c += n
                    yield
            b = nps()
            for c in range(NCK):
                for hs_ in HS:
                    mm(psb[b][hs_, c * 64:(c + 1) * 64], KR[hs_, c, 0:64], bbar[hs_, c * 64:(c + 1) * 64], True, True, [KRi, BBi], [PS(b)])
            tt(YT[:], v1(b), cmask[:, 3:4, :].to_broadcast([128, NCK, 64]), ALU.mult, [PS(b), "cmask"], ["YT"])
            tt(G0[:], Y[:], ident8[:], ALU.add, ["Y", "ident8"], ["G0"])
            yield
            gcur, gcid, gnxt, gnid = G0, "G0", G1, "G1"
            for lvl in range(5):
                last = lvl == 4
                if not last:
                    b1 = nps()
                    for c in range(NCK):
                        for hs_ in HS:
                            mm(psb[b1][hs_, c * 64:(c + 1) * 64], YT[hs_, c, :], Y[hs_, c, :], True, True, ["YT", "Y"], [PS(b1)])
                b2 = nps()
                for c in range(NCK):
                    for hs_ in HS:
                        mm(psb[b2][hs_, c * 64:(c + 1) * 64], Y[hs_, c, :], YT[hs_, c, :], True, True, ["YT", "Y"], [PS(b2)])
                tt(IYT[:], v1(b2), ident8[:], ALU.add, [PS(b2), "ident8"], ["IYT"])
                if not last:
                    copy_any(Y[:], v1(b1), [PS(b1)], ["Y"])
                    copy_any(YT[:], v1(b2), [PS(b2)], ["YT"])
                yield
                b3 = nps()
                for c in range(NCK):
                    for hs_ in HS:
                        mm(psb[b3][hs_, c * 64:(c + 1) * 64], IYT[hs_, c, :], gcur[hs_, c, :], True, True, ["IYT", gcid], [PS(b3)])
                copy_any(gnxt[:], v1(b3), [PS(b3)], [gnid])
                gcur, gcid, gnxt, gnid = gnxt, gnid, gcur, gcid
                yield
            assert gcid == "G1"
            ktok, WT, AkV, Uv = Y, YT, IYT, G0
            b = nps()
            for c in range(NCK):
                for hs_ in HS:
                    mm(psb[b][hs_, c * 64:(c + 1) * 64], KR[hs_, c, 0:64], identb[hs_, hs_], True, True, [KRi, "identb"], [PS(b)])
            copy_any(ktok[:], v1(b), [PS(b)], ["Y"])
            b = nps()
            for c in range(NCK):
                for hs_ in HS:
                    mm(psb[b][hs_, c * 64:(c + 1) * 64], AkT[hs_, c, :], vtok[hs_, c, :], True, True, ["AkT", VTi], [PS(b)])
            copy_any(AkV[:], v1(b), [PS(b)], ["IYT"])
            yield
            b = nps()
            for c in range(NCK):
                for hs_ in HS:
                    mm(psb[b][hs_, c * 64:(c + 1) * 64], ktok[hs_, c, :], G1[hs_, c, :], True, True, ["Y", "G1"], [PS(b)])
            copy_any(WT[:], v1(b), [PS(b)], ["YT"])
            b = nps()
            for c in range(NCK):
                for hs_ in HS:
                    mm(psb[b][hs_, c * 64:(c + 1) * 64], G1[hs_, c, :], AkV[hs_, c, :], True, True, ["G1", "IYT"], [PS(b)])
            copy_any(Uv[:], v1(b), [PS(b)], ["G0"])
            yield
            if full:
                py = nps()
                reserved.add(py)
            for c in range(NCK):
                b = nps()
                for hs_ in HS:
                    o = psb[b][hs_, 0:64]
                    mm(o, WT[hs_, c, :], Hr_bf[hs_, p, :], True, False, ["YT", ("Hr_bf", p)], [PS(b)])
                    mm(o, identb[hs_, hs_], Uv[hs_, c, :], False, True, ["identb", "G0"], [PS(b)])
                ts(nU[:], psb[b][:, 0:64], -1.0, None, ALU.mult, None, [PS(b)], ["nU"])
                yield
                if full:
                    for hs_ in HS:
                        o = psb[py][hs_, c * 64:(c + 1) * 64]
                        mm(o, Hr_bf[hs_, p, :], KR[hs_, c, 64:128], True, False, [("Hr_bf", p), KRi], [PS(py)])
                        mm(o, vtok[hs_, c, :], BkT[hs_, c, :], False, False, [VTi, "BkT"], [PS(py)])
                        mm(o, nU[hs_, :], BbT[hs_, c, :], False, True, ["nU", "BbT"], [PS(py)])
                b = nps()
                for hs_ in HS:
                    o = psb[b][hs_, 0:64]
                    mm(o, khtok[hs_, c, :], vtok[hs_, c, :], True, False, [KHi, VTi], [PS(b)])
                    mm(o, bhtok[hs_, c, :], nU[hs_, :], False, True, [BHi, "nU"], [PS(b)])
                stt(Hr_bf[:, p, :], Hr[:, p, :], edh[:, c:c + 1], psb[b][:, 0:64], ALU.mult, ALU.add, [("Hr", p), EDi, PS(b)], [("Hr_bf", p)])
                stt(Hr[:, p, :], Hr[:, p, :], edh[:, c:c + 1], psb[b][:, 0:64], ALU.mult, ALU.add, [("Hr", p), EDi, PS(b)], [("Hr", p)])
                yield
            if full:
                act_op(ysb[:], psb[py][:, 0:T], AF.Copy, [PS(py)] + YS, YS)
                act_op(ybf[:], psb[py][:, 0:T], AF.Copy, [PS(py)] + YB, YB)
                reserved.discard(py)
                b = nps()
                mm(psb[b][:, 0:T], bones64[:], ybf[:], True, True, ["bones64"] + YB, [PS(b)])
                tt(ysb[:], ysb[:], psb[b][:, 0:T], ALU.subtract, YS + [PS(b)], YS)
                act_op(ybf[:], ysb[:], AF.Square, YS + YB, YB)
                yield
                b = nps()
                mm(psb[b][:, 0:T], bones64[:], ybf[:], True, True, ["bones64"] + YB, [PS(b)])
                ts(e1b[:], psb[b][:, 0:T], 1.0, RWKV_LN_EPS, ALU.mult, ALU.add, [PS(b)] + EB, EB)
                act_op(e1b[:], e1b[:], AF.Ln, EB, EB)
                act_op(e1b[:], e1b[:], AF.Exp, EB, EB, scale=-0.5)
                tt(ysb[:], ysb[:], e1b[:], ALU.mult, YS + EB, YS)
                act_op(ysb[:], ysb[:], AF.Identity, YS + ["rvec"], YS, scale=V(5, p), bias=V(6, p))
                tt(ysb[:], ysb[:], bon[:], ALU.add, YS + [BON], YS)
                tt(yr[:, p, :], ysb[:], gq[:], ALU.mult, YS + [GQ], [("yr", p)])
                yield

        def drain(g):
            for _ in g:
                pass

        drain(stageA(0))
        for p in range(NP):
            gb = stageB(p)
            ga = stageA(p + 1) if p + 1 < NP else None
            done_a, done_b = ga is None, False
            while not (done_a and done_b):
                for _ in range(3):
                    if not done_b:
                        try:
                            next(gb)
                        except StopIteration:
                            done_b = True
                if not done_a:
                    try:
                        next(ga)
                    except StopIteration:
                        done_a = True

    def mixer(full, carry_all):
        prenorm(2)
        tk.barrier()
        yg = alloc(U("yg"), [128, VC, T], BF16, at=BB)
        yr = alloc(U("yr"), [128, NP, T], BF16, at=BB + VC * T * 2)
        assert NP <= VC
        gla(full, yg)
        tk.barrier()
        rwkv(full, yr, carry_all)
        tk.barrier()
        if not full:
            return
        mrg = alloc(U("mrg"), [128, NCH, T], BF16, at=BB + 2 * VC * T * 2)
        sa = alloc(U("sa"), [128, T], F32, at=BB + 2 * VC * T * 2 + NCH * T * 2)
        sb_ = alloc(U("sb"), [128, T], F32, at=BB + 2 * VC * T * 2 + NCH * T * 2 + T * 4)
        assert BB + 2 * VC * T * 2 + NCH * T * 2 + 2 * T * 4 <= BB + BB_SZ
        tiles = []
        for c in range(NCH):
            tiles += [("win", cfg.win_index[("ga", c)]), ("win", cfg.win_index[("gb", c)]), ("upg", c), ("upr", c)]
        ws = WStream(tiles)
        for c in range(NCH):
            b1, b2, b3, b4 = nps(), nps(), nps(), nps()
            dense(ws, NCH, lambda kc, kp: xn[:, kc, :], lambda kc: [("xn", kc)], b1)
            dense(ws, NCH, lambda kc, kp: xn[:, kc, :], lambda kc: [("xn", kc)], b2)
            dense(ws, VC, lambda kc, kp: yg[:, kc, :], lambda kc: [("yg", kc)], b3)
            dense(ws, NP, lambda kc, kp: yr[:, kc, :], lambda kc: [("yr", kc)], b4)
            act_op(sa[:], psb[b1][:, 0:T], AF.Sigmoid, [PS(b1)], ["sa"])
            act_op(sb_[:], psb[b2][:, 0:T], AF.Sigmoid, [PS(b2)], ["sb"])
            tt(sa[:], sa[:], psb[b3][:, 0:T], ALU.mult, ["sa", PS(b3)], ["sa"])
            tt(sb_[:], sb_[:], psb[b4][:, 0:T], ALU.mult, ["sb", PS(b4)], ["sb"])
            tt(mrg[:, c, :], sa[:], sb_[:], ALU.add, ["sa", "sb"], [("mrg", c)])
        tk.barrier()
        pss = nps()
        ws = WStream([("wo", c) for c in range(NCH)])
        for c in range(NCH):
            pf = nps()
            while pf == pss:
                pf = nps()
            dense(ws, NCH, lambda kc, kp: mrg[:, kc, :], lambda kc: [("mrg", kc)], pf)
            if c > 0:
                mm(psb[pss][:, 0:T], ones[:], tmpb[(c - 1) % 2][:], c - 1 == 0, False, ["ones", ("tmpb", (c - 1) % 2)], [PS(pss)])
            copy_any(fbf[:, c, :], psb[pf][:, 0:T], [PS(pf)], [("xn", c)])
            tb = tmpb[c % 2]
            act_op(tb[:], psb[pf][:, 0:T], AF.Square, [PS(pf)], [("tmpb", c % 2)])
        mm(psb[pss][:, 0:T], ones[:], tmpb[(NCH - 1) % 2][:], NCH - 1 == 0, True, ["ones", ("tmpb", (NCH - 1) % 2)], [PS(pss)])
        tk.barrier()
        post_residual(3, pss)

    dbg = os.environ.get("KDBG", "")
    for ti in range(cfg.NT):
        if dbg == "setup" or (dbg == "pre" and ti >= 0):
            full = ti >= cfg.NPRE
            tk.barrier()
            tk.dma("sp", lambda e, ti=ti: e.dma_start(out=hT[:].rearrange("p c t -> p (c t)"), in_=xT[ti]),
                   reads=(), writes=[("hT", c) for c in range(NCH)], stream="xld")
            if dbg == "pre":
                prenorm(0)
                for c in range(NCH):
                    tk.op("dve", lambda e, c=c: e.tensor_copy(out=hT[:, c, :], in_=xn[:, c, :]), reads=[("xn", c)], writes=[("hT", c)])
            if full:
                tk.dma("sp", lambda e, ti=ti: e.dma_start(out=yT[ti - cfg.NPRE], in_=hT[:].rearrange("p c t -> p (c t)")),
                       reads=[("hT", c) for c in range(NCH)], writes=["yout"], stream="hst")
            continue
        full = ti >= cfg.NPRE
        tk.barrier()
        tk.dma("sp", lambda e, ti=ti: e.dma_start(out=hT[:].rearrange("p c t -> p (c t)"), in_=xT[ti]),
               reads=(), writes=[("hT", c) for c in range(NCH)], stream="xld")
        tk.dma("sp", lambda e: e.dma_start(out=hs, in_=hT[:].rearrange("p c t -> p (c t)")),
               reads=[("hT", c) for c in range(NCH)], writes=["hs"], stream="hst")
        cur_tile[0] = ti
        last_stage = cfg.stop_after == "ffn1"
        ffn("g1", "u1", "d1", 0, 1, final_out=(yT[ti - cfg.NPRE] if (full and last_stage) else None))
        if dbg in ("gu", "down"):
            tk.barrier()
            if full:
                tk.dma("sp", lambda e, ti=ti: e.dma_start(out=yT[ti - cfg.NPRE], in_=hT[:].rearrange("p c t -> p (c t)")),
                       reads=[("hT", c) for c in range(NCH)], writes=["yout"], stream="hst")
            continue
        if last_stage:
            continue
        mixer(full, ti == cfg.NPRE - 1)
        if not full:
            continue
        if cfg.stop_after == "mix":
            tk.dma("sp", lambda e, ti=ti: e.dma_start(out=yT[ti - cfg.NPRE], in_=hT[:].rearrange("p c t -> p (c t)")),
                   reads=[("hT", c) for c in range(NCH)], writes=["yout"], stream="hst")
            continue
        ffn("g2", "u2", "d2", 4, 5, final_out=yT[ti - cfg.NPRE])
    tk.barrier()
    tk.ops["sp"].append((None, tk._waits("sp", [("d", "hst", tk.streams["hst"])]), None))
    tk.ops["act"].append((None, tk._waits("act", [("d", "hst", tk.streams["hst"])]), None))

    with nc.Block() as block:
        tk.emit(block)
    global LAST_TK
    LAST_TK = tk
    return nc


def _tiles(W, kct_total_chunks, col_chunks, nks=1):
    K = W.shape[0]
    KC = K // 128
    kct = KC // nks
    out = np.zeros((len(col_chunks) * nks, 128, kct, 128), np.float32)
    Wr = W.reshape(KC, 128, W.shape[1])
    i = 0
    for (c0, nc_) in col_chunks:
        for ks in range(nks):
            blk = Wr[ks * kct:(ks + 1) * kct, :, c0:c0 + nc_]
            out[i, :, :, :nc_] = blk.transpose(1, 0, 2)
            i += 1
    return out.reshape(out.shape[0], 128, kct * 128)


def _pc(v):
    v = np.asarray(v, np.float32).reshape(-1, 128)
    return np.ascontiguousarray(v.T)


def prep_shared(cfg, inp):
    D, NCH, NFF = cfg.D, cfg.NCH, cfg.NFF
    sq = lambda k: np.asarray(inp[k], np.float32)[0]
    full_chunks = lambda n: [(i * 128, 128) for i in range(n // 128)]
    m = {}
    m["w_g1"] = _tiles(sq("ffn1_w_gate"), NCH, full_chunks(cfg.DFF))
    m["w_u1"] = _tiles(sq("ffn1_w_up"), NCH, full_chunks(cfg.DFF))
    m["w_d1"] = _tiles(sq("ffn1_w_down"), NFF, full_chunks(D), nks=cfg.NKS)
    m["w_g2"] = _tiles(sq("ffn2_w_gate"), NCH, full_chunks(cfg.DFF))
    m["w_u2"] = _tiles(sq("ffn2_w_up"), NCH, full_chunks(cfg.DFF))
    m["w_d2"] = _tiles(sq("ffn2_w_down"), NFF, full_chunks(D), nks=cfg.NKS)
    win = sq("w_in")
    m["w_win"] = _tiles(win, NCH, [cfg.win_ch[k][i] for (k, i) in cfg.win_order])
    m["w_upg"] = _tiles(sq("w_up_gla"), cfg.VC, full_chunks(D))
    m["w_upr"] = _tiles(sq("w_up_rwkv"), cfg.NP, full_chunks(D))
    m["w_wo"] = _tiles(sq("w_out"), NCH, full_chunks(D))
    m["p_gains"] = np.concatenate([_pc(sq(k)) for k in
                                   ["ffn1_pre_norm", "ffn1_post_norm", "mix_pre_norm", "mix_post_norm", "ffn2_pre_norm", "ffn2_post_norm"]], 1)
    mix = sq("rwkv_shift_mix")
    o0 = cfg.GLA_COLS
    cols = []
    for (k, i) in cfg.rw_chunks:
        c0, n = cfg.win_ch[k][i]
        col = np.zeros(128, np.float32)
        col[:n] = mix[c0 - o0:c0 - o0 + n]
        cols.append(col)
    m["p_rmix"] = np.ascontiguousarray(np.stack(cols, 1))
    m["p_rvec"] = np.concatenate([_pc(sq(k).reshape(-1)) for k in
                                  ["rwkv_w0", "rwkv_a0", "rwkv_k_k", "rwkv_k_a", "rwkv_r_k", "rwkv_ln_w", "rwkv_ln_b"]], 1)
    m["p_gbias"] = _pc(sq("gla_gate_bias"))
    m["p_gonorm"] = _pc(sq("gla_out_norm"))
    m["p_gup"] = np.ascontiguousarray(sq("gla_gate_up"))
    m["w_w2l"] = _tiles(sq("rwkv_w2"), 1, full_chunks(cfg.RW))
    m["w_a2l"] = _tiles(sq("rwkv_a2"), 1, full_chunks(cfg.RW))
    g2 = np.zeros((512, cfg.RW), np.float32)
    g2[:cfg.LG] = sq("rwkv_g2")
    m["w_g2l"] = _tiles(g2, 4, full_chunks(cfg.RW))
    s_i = np.arange(64)[:, None]
    t_i = np.arange(64)[None, :]
    strict = (s_i < t_i).astype(np.float32)
    incl = (s_i <= t_i).astype(np.float32)
    lower = (s_i > t_i).astype(np.float32)
    m["p_cmask"] = np.ascontiguousarray(np.tile(np.stack([strict, incl, -strict, -lower], 1).reshape(64, 256), (2, 1)))
    m["p_ident"] = np.eye(128, dtype=np.float32)
    bo = np.zeros((128, 128), np.float32)
    bo[:64, :64] = 1
    bo[64:, 64:] = 1
    m["p_bones"] = bo
    sm = np.ones((128, cfg.T), np.float32)
    sm[:, ::64] = 0
    m["p_scanm"] = sm
    return m


def run(cfg, inp):
    x = np.asarray(inp["x"], np.float32)
    B, S, D = x.shape
    half = S // 2
    assert half == cfg.NMAIN * cfg.T and cfg.NPRE * cfg.T == half and B * 2 == 8
    shared = prep_shared(cfg, inp)
    nc = build(cfg)
    in_maps = []
    for c in range(8):
        b, r = c // 2, c % 2
        pre = np.zeros((half, D), np.float32) if r == 0 else x[b, :half]
        main = x[b, r * half:(r + 1) * half]
        xx = np.concatenate([pre, main], 0)
        xt = xx.reshape(cfg.NT, cfg.T, cfg.NCH, 128).transpose(0, 3, 2, 1)
        m = dict(shared)
        m["xT"] = np.ascontiguousarray(xt).reshape(cfg.NT, 128, cfg.NCH * cfg.T)
        in_maps.append(m)
    res = run_bass_kernel_spmd(nc, in_maps, core_ids=list(range(8)))
    out = np.zeros((B, S, D), np.float32)
    for c in range(8):
        b, r = c // 2, c % 2
        y = np.asarray(res.results[c]["yT"]).reshape(cfg.NMAIN, 128, cfg.NCH, cfg.T)
        out[b, r * half:(r + 1) * half] = y.transpose(0, 3, 2, 1).reshape(half, D)
    return out


def kernel(**inputs):
    return run(FULL, inputs)
```

```python
import math
import os
import numpy as np
import concourse.bass as bass
import concourse.mybir as mybir
from concourse.bass_utils import run_bass_kernel_spmd

F32 = mybir.dt.float32
BF16 = mybir.dt.bfloat16
ALU = mybir.AluOpType
AF = mybir.ActivationFunctionType

NORM_EPS = 1e-6
RWKV_LN_EPS = 64e-5
C0 = math.exp(-0.5)
SAME_ENGINE_SYNC = os.environ.get("KSES", "0") == "1"


class Cfg:
    def __init__(s, D=4096, DFF=11008, GH=8, RW=2048, T=512, NPRE=4, NMAIN=4, stop_after="full"):
        s.D, s.DFF, s.GH, s.RW, s.T, s.NPRE, s.NMAIN = D, DFF, GH, RW, T, NPRE, NMAIN
        s.stop_after = stop_after
        s.NCH = D // 128
        s.NFF = DFF // 128
        s.DK, s.DV = 128, 256
        s.KEYW, s.VALW = GH * 128, GH * 256
        s.RH = RW // 64
        s.NP = s.RH // 2
        s.LW, s.LA, s.LG = 128, 128, 480
        s.GLA_COLS = 2 * s.KEYW + 2 * s.VALW + 16
        s.RWKV_COLS = 3 * RW + s.LW + s.LA + s.LG
        s.WIN = s.GLA_COLS + s.RWKV_COLS + 2 * D
        s.NT = NPRE + NMAIN
        s.NCK = T // 64
        s.NKS = -(-s.NFF // 43)
        assert s.NFF % s.NKS == 0
        s.KCD = s.NFF // s.NKS
        s.VC = s.VALW // 128
        o = 0
        ch = {}
        def take(name, n):
            nonlocal o
            lst = []
            r = n
            while r > 0:
                w = min(128, r)
                lst.append((o, w))
                o += w
                r -= w
            ch[name] = lst
        take("gq", s.KEYW); take("gk", s.KEYW); take("gv", s.VALW); take("gg", 16); take("go", s.VALW)
        take("rr", RW); take("rw", s.LW); take("rk", RW); take("rv", RW); take("ra", s.LA); take("rg", s.LG)
        take("ga", D); take("gb", D)
        assert o == s.WIN
        s.win_ch = ch
        order = []
        for nm in ["gg", "gq", "gk", "gv", "go", "rw", "ra", "rg"]:
            order += [(nm, i) for i in range(len(ch[nm]))]
        for p in range(s.NP):
            order += [("rr", p), ("rk", p), ("rv", p)]
        for c in range(s.NCH):
            order += [("ga", c), ("gb", c)]
        s.win_order = order
        s.win_index = {k: i for i, k in enumerate(order)}
        s.rw_chunks = ([("rr", p) for p in range(s.NP)] + [("rw", 0)] + [("rk", p) for p in range(s.NP)]
                       + [("rv", p) for p in range(s.NP)] + [("ra", 0)] + [("rg", i) for i in range(4)])
        s.rw_cidx = {k: i for i, k in enumerate(s.rw_chunks)}


FULL = Cfg()
LAST_TK = None


EP = 20000
EPD = 1500


class Tracker:
    def __init__(s, nc):
        s.nc = nc
        s.engs = {"pe": nc.tensor, "dve": nc.vector, "act": nc.scalar, "pool": nc.gpsimd, "sp": nc.sync}
        s.ops = {k: [] for k in s.engs}
        s.cnt = {k: 0 for k in s.engs}
        s.seen = {k: {} for k in s.engs}
        s.last_w = {}
        s.readers = {}
        s.streams = {}
        s.groups = []
        s.sems = {}

    def _deps(s, reads, writes):
        deps = []
        for b in list(reads) + list(writes):
            t = s.last_w.get(b)
            if t is not None:
                deps.append(t)
        for b in writes:
            deps += s.readers.get(b, [])
        return deps

    def _waits(s, eng, deps, pe_acc=False):
        out = []
        seen = s.seen[eng]
        for t in deps:
            if t[0] == "e":
                if t[1] == eng:
                    if eng in ("pe", "sp", "pool") or not SAME_ENGINE_SYNC:
                        continue
                key = ("e", t[1])
                if seen.get(key, -1) >= t[2]:
                    continue
                seen[key] = t[2]
                out.append(t)
            elif t[0] == "d":
                key = ("d", t[1])
                if seen.get(key, 0) >= t[2]:
                    continue
                seen[key] = t[2]
                out.append(t)
            else:
                key = ("g", t[1])
                if key in seen:
                    continue
                seen[key] = 1
                out.append(t)
        best = {}
        for t in out:
            k = (t[0], t[1])
            if k not in best or best[k][2 if t[0] != "g" else 1] < t[2 if t[0] != "g" else 1]:
                best[k] = t
        return list(best.values())

    def _record(s, tok, reads, writes):
        for b in reads:
            s.readers.setdefault(b, []).append(tok)
        for b in writes:
            s.last_w[b] = tok
            s.readers[b] = []

    def op(s, eng, fn, reads=(), writes=()):
        deps = s._deps(reads, writes)
        for b in reads:
            if isinstance(b, tuple) and b[0] == "ps":
                deps += [t for t in s.readers.get(b, []) if not (t[0] == "e" and t[1] == eng)]
        waits = s._waits(eng, deps)
        idx = s.cnt[eng]
        s.cnt[eng] += 1
        tok = ("e", eng, idx)
        s.ops[eng].append((fn, waits, tok))
        s._record(tok, reads, writes)
        return tok

    def dma(s, queue, fn, reads=(), writes=(), stream=None, group=None):
        deps = s._deps(reads, writes)
        waits = s._waits(queue, deps)
        if group is not None:
            tok = ("g", group)
        else:
            n = s.streams.get(stream, 0) + 1
            s.streams[stream] = n
            tok = ("d", stream, n)
        s.ops[queue].append((fn, waits, tok))
        s._record(tok, reads, writes)
        return tok

    def barrier(s):
        toks = []
        for e in s.engs:
            if s.cnt[e] > 0 and e not in ("sp",):
                toks.append(("e", e, s.cnt[e] - 1))
        for st, n in s.streams.items():
            toks.append(("d", st, n))
        for e in s.engs:
            w = s._waits(e, toks)
            if w:
                s.ops[e].append((None, w, None))

    def _sem(s, key):
        if key not in s.sems:
            s.sems[key] = s.nc.alloc_semaphore("s_" + "_".join(str(k) for k in key))
        return s.sems[key]

    def _wait_args(s, t):
        if t[0] == "e":
            return s._sem(("e", t[1], t[2] // EP)), (t[2] % EP) + 1
        if t[0] == "d":
            n = t[2] - 1
            return s._sem(("d", t[1], n // EPD)), 16 * ((n % EPD) + 1)
        return s._sem(("g", t[1])), 16 * s.groups[t[1]]

    def emit(s, block):
        def run(engname):
            def body(eng):
                for fn, waits, tok in s.ops[engname]:
                    for t in waits:
                        sem, val = s._wait_args(t)
                        eng.wait_ge(sem, val)
                    if fn is None:
                        continue
                    ins = fn(eng)
                    if tok[0] == "e":
                        ins.then_inc(s._sem(("e", tok[1], tok[2] // EP)), 1)
                    elif tok[0] == "d":
                        ins.then_inc(s._sem(("d", tok[1], (tok[2] - 1) // EPD)), 16)
                    else:
                        ins.then_inc(s._sem(("g", tok[1])), 16)
            return body
        block.tensor(run("pe"))
        block.vector(run("dve"))
        block.scalar(run("act"))
        block.gpsimd(run("pool"))
        block.sync(run("sp"))


def weight_specs(cfg):
    KC = cfg.NCH
    return {
        "g1": (KC, cfg.NFF), "u1": (KC, cfg.NFF), "d1": (cfg.KCD, cfg.NCH * cfg.NKS),
        "win": (KC, len(cfg.win_order)),
        "upg": (cfg.VC, cfg.NCH), "upr": (cfg.NP, cfg.NCH), "wo": (KC, cfg.NCH),
        "g2": (KC, cfg.NFF), "u2": (KC, cfg.NFF), "d2": (cfg.KCD, cfg.NCH * cfg.NKS),
        "w2l": (1, cfg.NP), "a2l": (1, cfg.NP), "g2l": (4, cfg.NP),
    }


def build(cfg):
    nc = bass.Bass("TRN2", target_bir_lowering=False)
    tk = Tracker(nc)
    T, NCH, NFF, NP, NCK, D = cfg.T, cfg.NCH, cfg.NFF, cfg.NP, cfg.NCK, cfg.D
    GH, VC = cfg.GH, cfg.VC
    specs = weight_specs(cfg)

    xT = nc.dram_tensor("xT", [cfg.NT, 128, NCH * T], F32, kind="ExternalInput").ap()
    yT = nc.dram_tensor("yT", [cfg.NMAIN, 128, NCH * T], F32, kind="ExternalOutput").ap()
    hs = nc.dram_tensor("hs", [128, NCH * T], F32, kind="Internal").ap()
    wsrc, wcache = {}, {}
    for nm, (kct, nt) in specs.items():
        wsrc[nm] = nc.dram_tensor("w_" + nm, [nt, 128, kct * 128], F32, kind="ExternalInput").ap()
        wcache[nm] = nc.dram_tensor("c_" + nm, [nt, 128, kct * 128], BF16, kind="Internal").ap()
    NRC = len(cfg.rw_chunks)
    pspec = {
        "gains": [128, 6 * NCH], "rmix": [128, NRC], "rvec": [128, 7 * NP], "gbias": [128, GH],
        "gonorm": [128, 2], "gup": [16, cfg.KEYW], "cmask": [128, 4 * 64], "ident": [128, 128], "bones": [128, 128],
        "scanm": [128, T],
    }
    pin = {k: nc.dram_tensor("p_" + k, v, F32, kind="ExternalInput").ap() for k, v in pspec.items()}

    base = (nc.sbuf_base + 63) // 64 * 64
    total = nc.sbuf_top - base
    arena = nc.alloc_sbuf_tensor("arena", [128, total // 4 - 8], F32)
    cur = [base]

    def alloc(name, shape, dt, at=None):
        nb = int(np.prod(shape[1:])) * (2 if dt == BF16 else 4)
        nb = (nb + 63) // 64 * 64
        if at is None:
            off = cur[0]
            cur[0] += nb
        else:
            off = at
        assert off + nb <= nc.sbuf_top, (name, off, nb, nc.sbuf_top)
        return nc.alloc_sbuf_tensor_at(name, list(shape), dt, offset=off)

    gains = alloc("gains", [128, 6 * NCH], F32)
    rmix = alloc("rmix", [128, NRC], F32)
    rvec = alloc("rvec", [128, 7 * NP], F32)
    gbias_n = alloc("gbias_n", [128, GH], F32)
    gonorm = alloc("gonorm", [128, 2], F32)
    gup_bf = alloc("gup_bf", [16, cfg.KEYW], BF16)
    cmask = alloc("cmask", [128, 4, 64], F32)
    identb = alloc("identb", [128, 128], BF16)
    ident8 = alloc("ident8", [128, NCK, 64], BF16)
    bones = alloc("bones", [128, 128], BF16)
    bones64 = alloc("bones64", [128, 128], BF16)
    ones = alloc("ones", [128, 128], BF16)
    scanm = alloc("scanm", [128, T], F32)
    Sg = alloc("Sg", [128, GH, 256], F32)
    Sg_bf = alloc("Sg_bf", [128, GH, 256], BF16)
    Hr = alloc("Hr", [128, NP, 64], F32)
    Hr_bf = alloc("Hr_bf", [128, NP, 64], BF16)
    carry = alloc("carry", [128, NRC], F32)
    NSLOT = 3
    WSLOT = 43 * 128
    wslots = [alloc(f"wslot{i}", [128, WSLOT], BF16) for i in range(NSLOT)]
    tmpf = [alloc(f"tmpf{i}", [128, T], F32) for i in range(3)]
    tmpb = [alloc(f"tmpb{i}", [128, T], BF16) for i in range(2)]
    rstd = alloc("rstd", [128, T], F32)
    BA = cur[0]
    BA_SZ = NCH * T * 2
    BB = BA + BA_SZ
    BB_SZ = max(NFF * T * 2, NCH * T * 4, 2 * VC * T * 2 + 54 * 1024)
    assert BB + BB_SZ <= nc.sbuf_top, ("sbuf overflow", BB + BB_SZ, nc.sbuf_top)
    xn = alloc("xn", [128, NCH, T], BF16, at=BA)
    fbf = alloc("fbf", [128, NCH, T], BF16, at=BA)
    act = alloc("act", [128, NFF, T], BF16, at=BB)
    hT = alloc("hT", [128, NCH, T], F32, at=BB)

    psb = [nc.alloc_psum_tensor(f"ps{i}", [128, 512], F32) for i in range(8)]
    ps_i = [0]

    reserved = set()

    def nps():
        while True:
            i = ps_i[0] % 8
            ps_i[0] += 1
            if i not in reserved:
                return i

    def PS(b):
        return ("ps", b)

    rr = {"ev": 0}

    def mm(out, lhsT, rhs, start, stop, reads, writes):
        tk.op("pe", lambda e: e.matmul(out, lhsT=lhsT, rhs=rhs, start=start, stop=stop), reads=reads, writes=writes)

    def act_op(out, in_, func, reads, writes, bias=None, scale=None):
        kw = {}
        if bias is not None:
            kw["bias"] = bias
        if scale is not None:
            kw["scale"] = scale
        tk.op("act", lambda e: e.activation(out=out, in_=in_, func=func, **kw), reads=reads, writes=writes)

    def tt(out, in0, in1, op, reads, writes):
        tk.op("dve", lambda e: e.tensor_tensor(out=out, in0=in0, in1=in1, op=op), reads=reads, writes=writes)

    def ts(out, in0, s1, s2, op0, op1, reads, writes):
        if op1 is None:
            tk.op("dve", lambda e: e.tensor_scalar(out=out, in0=in0, scalar1=s1, scalar2=None, op0=op0), reads=reads, writes=writes)
        else:
            tk.op("dve", lambda e: e.tensor_scalar(out=out, in0=in0, scalar1=s1, scalar2=s2, op0=op0, op1=op1), reads=reads, writes=writes)

    def stt(out, in0, scalar, in1, op0, op1, reads, writes):
        tk.op("dve", lambda e: e.scalar_tensor_tensor(out=out, in0=in0, scalar=scalar, in1=in1, op0=op0, op1=op1), reads=reads, writes=writes)

    def rsqrt(out, in_, mul, add, reads, wid, clamp=None):
        if clamp is not None:
            ts(out, in_, clamp, None, ALU.max, None, reads, [wid])
        else:
            ts(out, in_, mul, add, ALU.mult, ALU.add, reads, [wid])
        act_op(out, out, AF.Ln, [wid], [wid])
        act_op(out, out, AF.Exp, [wid], [wid], scale=-0.5)

    def copy_any(out, in_, reads, writes):
        rr["ev"] += 1
        if rr["ev"] % 2:
            act_op(out, in_, AF.Copy, reads, writes)
        else:
            tk.op("dve", lambda e: e.tensor_copy(out=out, in_=in_), reads=reads, writes=writes)

    conv_order = ["g1", "u1", "d1", "win", "upg", "upr", "wo", "g2", "u2", "d2"]
    conv_list = []
    for j in range(NFF):
        conv_list += [("g1", j), ("u1", j)]
    conv_list += [("d1", i) for i in range(specs["d1"][1])]
    conv_list += [("win", i) for i in range(specs["win"][1])]
    for p in range(NP):
        conv_list += [("w2l", p), ("a2l", p), ("g2l", p)]
    for c in range(NCH):
        conv_list += [("upg", c), ("upr", c)]
    conv_list += [("wo", c) for c in range(NCH)]
    for j in range(NFF):
        conv_list += [("g2", j), ("u2", j)]
    conv_list += [("d2", i) for i in range(specs["d2"][1])]
    EARLY_WIN = ("gg", "gk", "gv", "rw", "ra", "rk", "rv")
    def is_early(nm, t):
        if nm in ("g1", "u1", "d1", "w2l", "a2l"):
            return True
        return nm == "win" and cfg.win_order[t][0] in EARLY_WIN
    early = [x for x in conv_list if is_early(*x)]
    late = [x for x in conv_list if not is_early(*x)]
    late.sort(key=lambda x: 0 if (x[0] == "win" and cfg.win_order[x[1]][0] in ("rr", "rg")) else (1 if x[0] == "g2l" else 2))
    conv_state = {"g": 0}

    def issue_group(grp, dep_reads=()):
        gi = conv_state["g"]
        conv_state["g"] += 1
        tk.groups.append(len(grp))
        for (nm, t) in grp:
            src, dst = wsrc[nm][t], wcache[nm][t]
            tk.dma("pool", lambda e, src=src, dst=dst: e.dma_start(out=dst, in_=src), reads=list(dep_reads), writes=[("wc", nm, t)], group=gi)

    i = 0
    for gs in [2, 4, 8, 16] + [32] * 1000:
        if i >= len(early):
            break
        issue_group(early[i:i + gs])
        i += gs
    LG_SZ = 12
    late_groups = [late[i:i + LG_SZ] for i in range(0, len(late), LG_SZ)]
    first_rel = 1 if cfg.NPRE >= 2 else 0
    slots = [(ti_, j_) for ti_ in range(first_rel, max(cfg.NPRE, 1)) for j_ in range(NFF)]
    release_plan = {}
    for gi_, grp in enumerate(late_groups):
        sl_ = slots[min(len(slots) - 1, gi_ * len(slots) // len(late_groups))]
        release_plan.setdefault(sl_, []).append(grp)
    cur_tile = [0]

    def load(dst_ap, src_ap, bufid, queue="sp"):
        tk.dma(queue, lambda e: e.dma_start(out=dst_ap, in_=src_ap), reads=(), writes=[bufid], stream=("ld", bufid))

    load(gains[:], pin["gains"], "gains")
    load(rmix[:], pin["rmix"], "rmix")
    load(rvec[:], pin["rvec"], "rvec")
    load(gonorm[:], pin["gonorm"], "gonorm")
    load(scanm[:], pin["scanm"], "scanm")
    load(cmask[:].rearrange("p a b -> p (a b)"), pin["cmask"], "cmask")
    stg = alloc("stg", [128, max(cfg.KEYW, 128)], F32, at=BB)
    def load_cast(dst, src, np_, ncols, name):
        load(stg[0:np_, 0:ncols], src, "stg")
        tk.op("dve", lambda e: e.tensor_copy(out=dst, in_=stg[0:np_, 0:ncols]), reads=["stg"], writes=[name])
    load_cast(gup_bf[:], pin["gup"], 16, cfg.KEYW, "gup_bf")
    load_cast(identb[:], pin["ident"], 128, 128, "identb")
    load_cast(bones[:], pin["bones"], 128, 128, "bones")
    load(stg[:, 0:GH], pin["gbias"], "stg")
    ts(gbias_n[:], stg[:, 0:GH], -1.0, None, ALU.mult, None, ["stg"], ["gbias_n"])
    ts(gains[:, NCH:2 * NCH], gains[:, NCH:2 * NCH], 0.5, None, ALU.mult, None, ["gains"], ["gains"])
    ts(gains[:, 5 * NCH:6 * NCH], gains[:, 5 * NCH:6 * NCH], 0.5, None, ALU.mult, None, ["gains"], ["gains"])
    ts(bones64[:], bones[:], 1.0 / 64.0, None, ALU.mult, None, ["bones"], ["bones64"])
    tk.op("dve", lambda e: e.memset(ones[:], 1.0), reads=(), writes=["ones"])
    for c in range(NCK):
        tk.op("dve", lambda e, c=c: e.tensor_copy(out=ident8[0:64, c, :], in_=identb[0:64, 0:64]), reads=["identb"], writes=["ident8"])
        tk.op("dve", lambda e, c=c: e.tensor_copy(out=ident8[64:128, c, :], in_=identb[64:128, 64:128]), reads=["identb"], writes=["ident8"])
    tk.op("dve", lambda e: e.memset(Sg[:], 0.0), reads=(), writes=["Sg"])
    tk.op("dve", lambda e: e.memset(Sg_bf[:], 0.0), reads=(), writes=["Sg_bf"])
    tk.op("dve", lambda e: e.memset(Hr[:], 0.0), reads=(), writes=["Hr"])
    tk.op("dve", lambda e: e.memset(Hr_bf[:], 0.0), reads=(), writes=["Hr_bf"])
    tk.op("dve", lambda e: e.memset(carry[:], 0.0), reads=(), writes=["carry"])
    tk.barrier()

    wstate = {"n": 0}

    def wload(nm, t):
        kct = specs[nm][0]
        si = wstate["n"] % NSLOT
        wstate["n"] += 1
        sl = wslots[si]
        src = wcache[nm][t]
        dst = sl[:, 0:kct * 128]
        tk.dma("sp", lambda e: e.dma_start(out=dst, in_=src), reads=[("wc", nm, t)], writes=[("ws", si)], stream=("ws", si))
        return sl[:, 0:kct * 128].rearrange("p (k n) -> p k n", n=128), ("ws", si)

    class WStream:
        def __init__(s, tiles, depth=NSLOT - 1):
            s.tiles, s.depth, s.q, s.i = tiles, depth, [], 0
            for _ in range(min(depth, len(tiles))):
                s._issue()
        def _issue(s):
            s.q.append(wload(*s.tiles[s.i]))
            s.i += 1
        def next(s):
            r = s.q.pop(0)
            if s.i < len(s.tiles):
                s._issue()
            return r

    def dense(ws, kct, rhs_fn, rhs_ids, out_ps, ncols=128, np_=128, nfree=T, first=True, last=True, kparts=None):
        wt, wid = ws.next()
        for kc in range(kct):
            kp = 128 if kparts is None else kparts[kc]
            mm(psb[out_ps][0:ncols, 0:nfree], wt[0:kp, kc, 0:ncols], rhs_fn(kc, kp),
               first and kc == 0, last and kc == kct - 1, [wid] + rhs_ids(kc), [PS(out_ps)])

    def sumsq_accum(src_fn, src_ids, nchunks, ps_bank, lhs=None, from_psum_ids=None):
        for c in range(nchunks):
            tb = tmpb[c % 2]
            act_op(tb[:], src_fn(c), AF.Square, src_ids(c), [("tmpb", c % 2)])
            mm(psb[ps_bank][:, 0:T], ones[:], tb[:], c == 0, c == nchunks - 1, ["ones", ("tmpb", c % 2)], [PS(ps_bank)])

    def rstd_from(ps_bank, n, eps):
        rsqrt(rstd[:], psb[ps_bank][:, 0:T], 1.0 / n, eps, [PS(ps_bank)], "rstd")

    def prenorm(gi_):
        b = nps()
        sumsq_accum(lambda c: hT[:, c, :], lambda c: [("hT", c)], NCH, b)
        rstd_from(b, D, NORM_EPS)
        for c in range(NCH):
            stt(xn[:, c, :], hT[:, c, :], gains[:, gi_ * NCH + c:gi_ * NCH + c + 1], rstd[:], ALU.mult, ALU.mult,
                [("hT", c), "gains", "rstd"], [("xn", c)])

    def post_residual(gi_, ps_ss, final_out=None):
        rstd_from(ps_ss, D, NORM_EPS)
        NQ = 4 if NCH % 4 == 0 else 1
        for q_ in range(NQ):
            c0, c1 = q_ * NCH // NQ, (q_ + 1) * NCH // NQ
            tk.dma("sp", lambda e, c0=c0, c1=c1: e.dma_start(out=hT[:, c0:c1, :].rearrange("p c t -> p (c t)"), in_=hs[:, c0 * T:c1 * T]),
                   reads=["hs"], writes=[("hT", c) for c in range(c0, c1)] + [("act", j) for j in range(2 * c0, min(2 * c1, NFF))]
                   + ([("act", j) for j in range(2 * NCH, NFF)] if q_ == NQ - 1 else []), stream=("hld", q_))
        for c in range(NCH):
            tf = tmpf[c % 3]
            stt(tf[:], fbf[:, c, :], gains[:, gi_ * NCH + c:gi_ * NCH + c + 1], rstd[:], ALU.mult, ALU.mult,
                [("xn", c), "gains", "rstd"], [("tmpf", c % 3)])
            tt(hT[:, c, :], tf[:], hT[:, c, :], ALU.add, [("tmpf", c % 3), ("hT", c)], [("hT", c)])
        dst = hs if final_out is None else final_out
        tk.dma("sp", lambda e: e.dma_start(out=dst, in_=hT[:].rearrange("p c t -> p (c t)")),
               reads=[("hT", c) for c in range(NCH)], writes=["hs" if final_out is None else "yout"], stream="hst")

    def ffn(gn, un, dn, gpre, gpost, final_out=None):
        prenorm(gpre)
        tiles = []
        for j in range(NFF):
            tiles += [(gn, j), (un, j)]
        ws = WStream(tiles)
        for j in range(NFF):
            pg, pu = nps(), nps()
            dense(ws, NCH, lambda kc, kp: xn[:, kc, :], lambda kc: [("xn", kc)], pg)
            dense(ws, NCH, lambda kc, kp: xn[:, kc, :], lambda kc: [("xn", kc)], pu)
            tf = tmpf[j % 3]
            act_op(tf[:], psb[pg][:, 0:T], AF.Silu, [PS(pg)], [("tmpf", j % 3)])
            tt(act[:, j, :], tf[:], psb[pu][:, 0:T], ALU.mult, [("tmpf", j % 3), PS(pu)], [("act", j)] + ([("hT", j // 2)] if j // 2 < NCH else []))
            if gn == "g1":
                for grp in release_plan.get((cur_tile[0], j), []):
                    issue_group(grp, dep_reads=[("act", j)])
        if os.environ.get("KDBG", "") == "gu":
            return
        pss = nps()
        ws = WStream([(dn, i) for i in range(NCH * cfg.NKS)])
        for c in range(NCH):
            pf = nps()
            while pf == pss:
                pf = nps()
            for ks in range(cfg.NKS):
                k0 = ks * cfg.KCD
                dense(ws, cfg.KCD, lambda kc, kp, k0=k0: act[:, k0 + kc, :], lambda kc, k0=k0: [("act", k0 + kc)], pf,
                      first=(ks == 0), last=(ks == cfg.NKS - 1))
            if c > 0:
                mm(psb[pss][:, 0:T], ones[:], tmpb[(c - 1) % 2][:], c - 1 == 0, False, ["ones", ("tmpb", (c - 1) % 2)], [PS(pss)])
            copy_any(fbf[:, c, :], psb[pf][:, 0:T], [PS(pf)], [("xn", c)])
            tb = tmpb[c % 2]
            act_op(tb[:], psb[pf][:, 0:T], AF.Square, [PS(pf)], [("tmpb", c % 2)])
        mm(psb[pss][:, 0:T], ones[:], tmpb[(NCH - 1) % 2][:], NCH - 1 == 0, True, ["ones", ("tmpb", (NCH - 1) % 2)], [PS(pss)])
        if os.environ.get("KDBG", "") == "down":
            return
        post_residual(gpost, pss, final_out)

    class Bump:
        def __init__(s, start, end):
            s.o, s.end = start, end
        def get(s, name, shape, dt):
            nb = (int(np.prod(shape[1:])) * (2 if dt == BF16 else 4) + 63) // 64 * 64
            t_ = alloc(name, shape, dt, at=s.o)
            s.o += nb
            assert s.o <= s.end, ("mixer working set overflow", name, s.o, s.end)
            return t_

    uid = [0]
    def U(p):
        uid[0] += 1
        return f"{p}{uid[0]}"

    def win_tiles(keys):
        return [("win", cfg.win_index[k]) for k in keys]

    def proj_u(ws, key, ps_bank):
        ncols = cfg.win_ch[key[0]][key[1]][1]
        dense(ws, NCH, lambda kc, kp: xn[:, kc, :], lambda kc: [("xn", kc)], ps_bank, ncols=ncols)
        return ncols

    def to_tok(dst, dst_id, srcT, src_id, ncols=128):
        per_bank = 512 // ncols
        c = 0
        while c < NCK:
            b = nps()
            n = min(per_bank, NCK - c)
            for i in range(n):
                mm(psb[b][0:64, i * ncols:(i + 1) * ncols], srcT[0:ncols, (c + i) * 64:(c + i + 1) * 64], identb[0:ncols, 0:ncols],
                   True, True, [src_id, "identb"], [PS(b)])
            copy_any(dst[:, c:c + n, :], psb[b][0:64, 0:n * ncols].rearrange("p (a b) -> p a b", b=ncols), [PS(b)], [dst_id])
            c += n

    def gla(full, yg):
        bp = Bump(BB + (2 * VC * T * 2 if True else 0), BB + BB_SZ)
        ggT = bp.get(U("ggT"), [16, T], BF16)
        cs = bp.get(U("gcs"), [128, T], F32)
        ex = bp.get(U("gex"), [128, T], F32)
        eq = bp.get(U("geq"), [128, T], F32)
        ek = bp.get(U("gek"), [128, T], F32)
        el = bp.get(U("gel"), [128, T], F32)
        edec = bp.get(U("gedec"), [128, NCK], F32)
        qt = bp.get(U("gqt"), [128, T], BF16)
        kt = bp.get(U("gkt"), [128, T], BF16)
        khT = bp.get(U("gkhT"), [128, T], BF16)
        vT = [bp.get(U("gvT"), [128, T], BF16) for _ in range(2)]
        vtok = bp.get(U("gvtok"), [64, NCK, 256], BF16)
        khtok = bp.get(U("gkhtok"), [64, NCK, 128], BF16)
        attn = bp.get(U("gattn"), [64, NCK, 64], BF16)
        sgo = [bp.get(U("gsgo"), [128, T], F32) for _ in range(2)]
        ws = WStream(win_tiles([("gg", 0)]))
        b = nps()
        proj_u(ws, ("gg", 0), b)
        copy_any(ggT[:], psb[b][0:16, 0:T], [PS(b)], ["ggT"])
        for h in range(GH):
            keys = [("gq", h), ("gk", h), ("gv", 2 * h), ("gv", 2 * h + 1)] + ([("go", 2 * h), ("go", 2 * h + 1)] if full else [])
            if not full:
                keys = [("gk", h), ("gv", 2 * h), ("gv", 2 * h + 1)]
            ws = WStream(win_tiles(keys))
            b = nps()
            mm(psb[b][:, 0:T], gup_bf[:, h * 128:(h + 1) * 128], ggT[:], True, True, ["gup_bf", "ggT"], [PS(b)])
            act_op(ex[:], psb[b][:, 0:T], AF.Exp, [PS(b), "gbias_n"], ["gex"], bias=gbias_n[:, h:h + 1], scale=-1.0)
            act_op(ex[:], ex[:], AF.Ln, ["gex"], ["gex"], bias=1.0)
            tk.op("dve", lambda e: e.tensor_tensor_scan(out=cs[:], data0=scanm[:], data1=ex[:], initial=0.0, op0=ALU.mult, op1=ALU.add),
                  reads=["scanm", "gex"], writes=["gcs"])
            cs3 = cs[:].rearrange("p (c t) -> p c t", t=64)
            csl = cs3[:, :, 63:64]
            tt(el[:].rearrange("p (c t) -> p c t", t=64), csl.to_broadcast([128, NCK, 64]), cs3, ALU.subtract, ["gcs"], ["gel"])
            act_op(el[:], el[:], AF.Exp, ["gel"], ["gel"], scale=-1.0 / 16)
            act_op(edec[:].rearrange("p (c o) -> p c o", o=1), csl, AF.Exp, ["gcs"], ["gedec"], scale=-1.0 / 16)
            act_op(ek[:], cs[:], AF.Exp, ["gcs"], ["gek"], scale=1.0 / 16)
            if full:
                act_op(eq[:], cs[:], AF.Exp, ["gcs"], ["geq"], scale=-1.0 / 16, bias=float(math.log(128 ** -0.5)))
                b = nps()
                proj_u(ws, ("gq", h), b)
                tt(qt[:], psb[b][:, 0:T], eq[:], ALU.mult, [PS(b), "geq"], ["gqt"])
            b = nps()
            proj_u(ws, ("gk", h), b)
            if full:
                tt(kt[:], psb[b][:, 0:T], ek[:], ALU.mult, [PS(b), "gek"], ["gkt"])
            tt(khT[:], psb[b][:, 0:T], el[:], ALU.mult, [PS(b), "gel"], ["gkhT"])
            for hf in range(2):
                b = nps()
                proj_u(ws, ("gv", 2 * h + hf), b)
                copy_any(vT[hf][:], psb[b][:, 0:T], [PS(b)], [("gvT", hf)])
            for hf in range(2):
                c = 0
                while c < NCK:
                    b = nps()
                    n = min(4, NCK - c)
                    for i in range(n):
                        mm(psb[b][0:64, i * 128:(i + 1) * 128], vT[hf][:, (c + i) * 64:(c + i + 1) * 64], identb[:], True, True,
                           [("gvT", hf), "identb"], [PS(b)])
                    copy_any(vtok[:, c:c + n, hf * 128:(hf + 1) * 128], psb[b][0:64, 0:n * 128].rearrange("p (a b) -> p a b", b=128),
                             [PS(b)], ["gvtok"])
                    c += n
            to_tok(khtok, "gkhtok", khT, "gkhT")
            if full:
                b = nps()
                for c in range(NCK):
                    mm(psb[b][0:64, c * 64:(c + 1) * 64], kt[:, c * 64:(c + 1) * 64], qt[:, c * 64:(c + 1) * 64], True, True,
                       ["gkt", "gqt"], [PS(b)])
                tt(attn[:], psb[b][0:64, 0:NCK * 64].rearrange("p (c t) -> p c t", t=64),
                   cmask[0:64, 1:2, :].to_broadcast([64, NCK, 64]), ALU.mult, [PS(b), "cmask"], ["gattn"])
                po = [nps(), nps()]
                reserved.update(po)
                bgo = [nps(), nps()]
                reserved.update(bgo)
                go_mm = []
                for hf in range(2):
                    wt_, wid_ = ws.next()
                    for kc in range(NCH):
                        go_mm.append((bgo[hf], wt_[:, kc, :], kc, wid_))
                per_step = -(-len(go_mm) // NCK)
            for c in range(NCK):
                if full:
                    for hf in range(2):
                        mm(psb[po[hf]][:, c * 64:(c + 1) * 64], vtok[:, c, hf * 128:(hf + 1) * 128], attn[:, c, :], True, False,
                           ["gvtok", "gattn"], [PS(po[hf])])
                        mm(psb[po[hf]][:, c * 64:(c + 1) * 64], Sg_bf[:, h, hf * 128:(hf + 1) * 128], qt[:, c * 64:(c + 1) * 64], False, True,
                           [("Sg_bf", h), "gqt"], [PS(po[hf])])
                b = nps()
                mm(psb[b][:, 0:256], khtok[:, c, :], vtok[:, c, :], True, True, ["gkhtok", "gvtok"], [PS(b)])
                stt(Sg_bf[:, h, :], Sg[:, h, :], edec[:, c:c + 1], psb[b][:, 0:256], ALU.mult, ALU.add,
                    [("Sg", h), "gedec", PS(b)], [("Sg_bf", h)])
                stt(Sg[:, h, :], Sg[:, h, :], edec[:, c:c + 1], psb[b][:, 0:256], ALU.mult, ALU.add,
                    [("Sg", h), "gedec", PS(b)], [("Sg", h)])
                if full:
                    for (bk, lw, kc, wid_) in go_mm[c * per_step:(c + 1) * per_step]:
                        mm(psb[bk][:, 0:T], lw, xn[:, kc, :], kc == 0, kc == NCH - 1, [wid_, ("xn", kc)], [PS(bk)])
            if full:
                for hf in range(2):
                    act_op(sgo[hf][:], psb[bgo[hf]][:, 0:T], AF.Silu, [PS(bgo[hf])], [("gsgo", hf)])
                reserved.difference_update(bgo)
                bn = nps()
                for hf in range(2):
                    tb = tmpb[hf]
                    act_op(tb[:], psb[po[hf]][:, 0:T], AF.Square, [PS(po[hf])], [("tmpb", hf)])
                    mm(psb[bn][:, 0:T], ones[:], tb[:], hf == 0, hf == 1, ["ones", ("tmpb", hf)], [PS(bn)])
                rsqrt(rstd[:], psb[bn][:, 0:T], 1.0 / 256, NORM_EPS, [PS(bn)], "rstd")
                for hf in range(2):
                    tf = tmpf[hf]
                    stt(tf[:], psb[po[hf]][:, 0:T], gonorm[:, hf:hf + 1], rstd[:], ALU.mult, ALU.mult,
                        [PS(po[hf]), "gonorm", "rstd"], [("tmpf", hf)])
                    tt(yg[:, 2 * h + hf, :], tf[:], sgo[hf][:], ALU.mult, [("tmpf", hf), ("gsgo", hf)], [("yg", 2 * h + hf)])
                reserved.difference_update(po)

    def rwkv(full, yr, carry_all):
        bp = Bump(BB + 2 * VC * T * 2, BB + BB_SZ)
        pbuf = [bp.get(U("rp"), [128, T + 1], F32) for _ in range(2)]
        twd = bp.get(U("twd"), [128, T], BF16)
        tad = bp.get(U("tad"), [128, T], BF16)
        sgd = [bp.get(U("sgd"), [128, T], BF16) for _ in range(4)]
        rq = bp.get(U("rq"), [128, T], F32)
        kq = bp.get(U("kq"), [128, T], F32)
        vq = bp.get(U("vq"), [128, T], F32)
        ld = bp.get(U("ld"), [128, T], F32)
        aa = bp.get(U("aa"), [128, T], F32)
        kap = bp.get(U("kap"), [128, T], F32)
        bbq = bp.get(U("bbq"), [128, T], F32)
        trT = bp.get(U("trT"), [128, T], BF16)
        edh2 = [bp.get(U("edh"), [128, NCK], F32) for _ in range(2)]
        KR2 = [bp.get(U("KR"), [128, NCK, 128], BF16) for _ in range(2)]
        bbar2 = [bp.get(U("bbar"), [128, T], BF16) for _ in range(2)]
        kbar2 = [bp.get(U("kbar"), [128, T], BF16) for _ in range(2)]
        vtok2 = [bp.get(U("rvtok"), [128, NCK, 64], BF16) for _ in range(2)]
        khtok2 = [bp.get(U("rkhtok"), [128, NCK, 64], BF16) for _ in range(2)]
        bhtok2 = [bp.get(U("rbhtok"), [128, NCK, 64], BF16) for _ in range(2)]
        gq2 = [tmpf[1], bp.get(U("gq1"), [128, T], F32)]
        bon2 = [tmpf[2], bp.get(U("bon1"), [128, T], F32)]
        GQ2 = [("tmpf", 1), "gq1"]
        BON2 = [("tmpf", 2), "bon1"]
        oY = bp.o
        Y = bp.get(U("Y"), [128, NCK, 64], BF16)
        YT = bp.get(U("YT"), [128, NCK, 64], BF16)
        oI = bp.o
        IYT = bp.get(U("IYT"), [128, NCK, 64], BF16)
        oG = bp.o
        G0 = bp.get(U("G0"), [128, NCK, 64], BF16)
        G1 = bp.get(U("G1"), [128, NCK, 64], BF16)
        AkT = bp.get(U("AkT"), [128, NCK, 64], BF16)
        BkT = bp.get(U("BkT"), [128, NCK, 64], BF16)
        BbT = bp.get(U("BbT"), [128, NCK, 64], BF16)
        rhs_sb = bp.get(U("rhs"), [128, 64], BF16)
        nU = bp.get(U("nU"), [128, 64], BF16)
        assert NCK * 64 * 2 * 2 >= T * 4 and NCK * 64 * 2 >= T * 2
        ysb = alloc(U("ysb"), [128, T], F32, at=oY)
        ybf = alloc(U("ybf"), [128, T], BF16, at=oI)
        e1b = alloc(U("e1b"), [128, T], F32, at=oG)
        YS, YB, EB = ["Y", "YT"], ["IYT"], ["G0", "G1"]
        dtmp, csr, e2, e3 = aa, kq, ld, aa
        e1, kmod = tmpf[0], rstd
        E1, KMOD = ("tmpf", 0), "rstd"
        HS = (slice(0, 64), slice(64, 128))

        def shifted(key, ws, dst, dst_id, func=None):
            ci = cfg.rw_cidx[key]
            pb = pbuf[ci % 2]
            pid = ("rp", ci % 2)
            b = nps()
            ncols = proj_u(ws, key, b)
            act_op(pb[0:ncols, 1:T + 1], psb[b][0:ncols, 0:T], AF.Copy, [PS(b)], [pid])
            act_op(pb[0:ncols, 0:1], carry[0:ncols, ci:ci + 1], AF.Copy, [("carry", ci)], [pid])
            act_op(carry[0:ncols, ci:ci + 1], pb[0:ncols, T:T + 1], AF.Copy, [pid], [("carry", ci)])
            if dst is None:
                return ncols
            tt(dtmp[0:ncols, :], pb[0:ncols, 0:T], pb[0:ncols, 1:T + 1], ALU.subtract, [pid], ["aa"])
            if func is None:
                stt(dst[0:ncols, :], dtmp[0:ncols, :], rmix[0:ncols, ci:ci + 1], pb[0:ncols, 1:T + 1], ALU.mult, ALU.add,
                    ["aa", "rmix", pid], [dst_id])
            else:
                stt(dtmp[0:ncols, :], dtmp[0:ncols, :], rmix[0:ncols, ci:ci + 1], pb[0:ncols, 1:T + 1], ALU.mult, ALU.add,
                    ["aa", "rmix", pid], ["aa"])
                act_op(dst[0:ncols, :], dtmp[0:ncols, :], func, ["aa"], [dst_id])
            return ncols

        keys = [("rw", 0), ("ra", 0)] + ([("rg", i) for i in range(4)] if (full or carry_all) else [])
        ws = WStream(win_tiles(keys))
        shifted(("rw", 0), ws, twd, "twd", AF.Tanh)
        shifted(("ra", 0), ws, tad, "tad", AF.Copy)
        if full or carry_all:
            for i in range(4):
                if full:
                    shifted(("rg", i), ws, sgd[i], ("sgd", i), AF.Sigmoid)
                else:
                    shifted(("rg", i), ws, None, None)
        V = lambda j, p: rvec[:, j * NP + p:j * NP + p + 1]
        c3 = lambda a_: a_[:].rearrange("p (c t) -> p c t", t=64)

        def stageA(p):
            q = p % 2
            KR, bbar, kbar, vtok, khtok, bhtok, edh = KR2[q], bbar2[q], kbar2[q], vtok2[q], khtok2[q], bhtok2[q], edh2[q]
            gq, bon, GQ, BON = gq2[q], bon2[q], GQ2[q], BON2[q]
            KRi, BBi, KBi, VTi, KHi, BHi, EDi = ("KR", q), ("bbar", q), ("kbar", q), ("rvtok", q), ("rkhtok", q), ("rbhtok", q), ("edh", q)
            tl = win_tiles(([("rr", p)] if (full or carry_all) else []) + [("rk", p), ("rv", p)])
            tl += [("w2l", p), ("a2l", p)] + ([("g2l", p)] if full else [])
            ws = WStream(tl)
            if full:
                shifted(("rr", p), ws, rq, "rq")
                yield
            elif carry_all:
                shifted(("rr", p), ws, None, None)
                yield
            shifted(("rk", p), ws, kq, "kq")
            yield
            shifted(("rv", p), ws, vq, "vq")
            yield
            wt, wid = ws.next()
            b = nps()
            mm(psb[b][:, 0:T], wt[:, 0, :], twd[:], True, True, [wid, "twd"], [PS(b)])
            act_op(ld[:], psb[b][:, 0:T], AF.Sigmoid, [PS(b), "rvec"], ["ld"], bias=V(0, p))
            wt, wid = ws.next()
            b = nps()
            mm(psb[b][:, 0:T], wt[:, 0, :], tad[:], True, True, [wid, "tad"], [PS(b)])
            act_op(aa[:], psb[b][:, 0:T], AF.Sigmoid, [PS(b), "rvec"], ["aa"], bias=V(1, p))
            if full:
                wt, wid = ws.next()
                b = nps()
                for kc in range(4):
                    kp = 128 if kc < 3 else cfg.LG - 384
                    mm(psb[b][:, 0:T], wt[0:kp, kc, :], sgd[kc][0:kp, :], kc == 0, kc == 3, [wid, ("sgd", kc)], [PS(b)])
                act_op(gq[:], psb[b][:, 0:T], AF.Copy, [PS(b)], [GQ])
            yield
            act_op(kap[:], kq[:], AF.Copy, ["kq", "rvec"], ["kap"], scale=V(2, p))
            act_op(tmpb[0][:], kq[:], AF.Square, ["kq", "rvec"], [("tmpb", 0)], scale=V(2, p))
            b = nps()
            mm(psb[b][:, 0:T], bones[:], tmpb[0][:], True, True, ["bones", ("tmpb", 0)], [PS(b)])
            rsqrt(e1[:], psb[b][:, 0:T], None, None, [PS(b)], E1, clamp=1e-18)
            tt(kap[:], kap[:], e1[:], ALU.mult, ["kap", E1], ["kap"])
            yield
            ts(e1[:], aa[:], 1.0, V(3, p), ALU.subtract, ALU.mult, ["aa", "rvec"], [E1])
            stt(kmod[:], e1[:], 1.0, kq[:], ALU.add, ALU.mult, [E1, "kq"], [KMOD])
            tt(bbq[:], aa[:], kap[:], ALU.mult, ["aa", "kap"], ["bbq"])
            if full:
                stt(tmpb[1][:], rq[:], V(4, p), kmod[:], ALU.mult, ALU.mult, ["rq", "rvec", KMOD], [("tmpb", 1)])
                b = nps()
                mm(psb[b][:, 0:T], bones[:], tmpb[1][:], True, True, ["bones", ("tmpb", 1)], [PS(b)])
                tt(bon[:], psb[b][:, 0:T], vq[:], ALU.mult, [PS(b), "vq"], [BON])
            yield
            tk.op("dve", lambda e: e.tensor_tensor_scan(out=csr[:], data0=scanm[:], data1=ld[:], initial=0.0, op0=ALU.mult, op1=ALU.add),
                  reads=["scanm", "ld", KMOD], writes=["kq"])
            cs3 = csr[:].rearrange("p (c t) -> p c t", t=64)
            csl = cs3[:, :, 63:64]
            tt(e1[:], csr[:], ld[:], ALU.subtract, ["kq", "ld"], [E1])
            act_op(e1[:], e1[:], AF.Exp, [E1], [E1], scale=-C0)
            tt(KR[:, :, 0:64], c3(kap), c3(e1), ALU.mult, ["kap", E1], [KRi])
            if full:
                act_op(e2[:], csr[:], AF.Exp, ["kq", E1], ["ld"], scale=-C0)
                tt(KR[:, :, 64:128], c3(rq), c3(e2), ALU.mult, ["rq", "ld"], [KRi])
            yield
            act_op(e3[:], csr[:], AF.Exp, ["kq", "bbq"], ["aa"], scale=C0)
            tt(bbar[:], bbq[:], e3[:], ALU.mult, ["bbq", "aa"], [BBi])
            tt(kbar[:], kmod[:], e3[:], ALU.mult, [KMOD, "aa"], [KBi])
            tt(c3(e2), csl.to_broadcast([128, NCK, 64]), cs3, ALU.subtract, ["kq", KRi], ["ld"])
            act_op(e2[:], e2[:], AF.Exp, ["ld"], ["ld"], scale=-C0)
            act_op(edh[:].rearrange("p (c o) -> p c o", o=1), csl, AF.Exp, ["kq"], [EDi], scale=-C0)
            yield

            def to_tok_pair(dst, dst_id):
                bq = nps()
                for c in range(NCK):
                    for hs_ in HS:
                        mm(psb[bq][hs_, c * 64:(c + 1) * 64], trT[hs_, c * 64:(c + 1) * 64], identb[hs_, hs_], True, True,
                           ["trT", "identb"], [PS(bq)])
                copy_any(dst[:], psb[bq][:, 0:NCK * 64].rearrange("p (c t) -> p c t", t=64), [PS(bq)], [dst_id])

            act_op(trT[:], vq[:], AF.Copy, ["vq"], ["trT"])
            to_tok_pair(vtok, VTi)
            yield
            tt(trT[:], kmod[:], e2[:], ALU.mult, [KMOD, "ld"], ["trT"])
            to_tok_pair(khtok, KHi)
            yield
            tt(trT[:], bbq[:], e2[:], ALU.mult, ["bbq", "ld"], ["trT"])
            to_tok_pair(bhtok, BHi)
            yield

        def stageB(p):
            q = p % 2
            KR, bbar, kbar, vtok, khtok, bhtok, edh = KR2[q], bbar2[q], kbar2[q], vtok2[q], khtok2[q], bhtok2[q], edh2[q]
            gq, bon, GQ, BON = gq2[q], bon2[q], GQ2[q], BON2[q]
            KRi, BBi, KBi, VTi, KHi, BHi, EDi = ("KR", q), ("bbar", q), ("kbar", q), ("rvtok", q), ("rkhtok", q), ("rbhtok", q), ("edh", q)
            v1 = lambda bb_: psb[bb_][:, 0:NCK * 64].rearrange("p (c t) -> p c t", t=64)
            for (src, srcid, dA, dAid, mA, dB, dBid) in ((bbar, BBi, Y, "Y", 2, BbT, "BbT"), (kbar, KBi, AkT, "AkT", 0, BkT, "BkT")):
                c = 0
                while c < NCK:
                    b = nps()
                    n = min(4, NCK - c)
                    ncol = 128 if full else 64
                    for i in range(n):
                        for hs_ in HS:
                            mm(psb[b][hs_, i * 128:i * 128 + ncol], src[hs_, (c + i) * 64:(c + i + 1) * 64], KR[hs_, c + i, 0:ncol], True, True,
                               [srcid, KRi], [PS(b)])
                    v4 = psb[b][:, 0:n * 128].rearrange("p (a b) -> p a b", b=128)
                    tt(dA[:, c:c + n, :], v4[:, :, 0:64], cmask[:, mA:mA + 1, :].to_broadcast([128, n, 64]), ALU.mult, [PS(b), "cmask"], [dAid])
                    if full:
                        tt(dB[:, c:c + n, :], v4[:, :, 64:128], cmask[:, 1:2, :].to_broadcast([128, n, 64]), ALU.mult, [PS(b), "cmask"], [dBid])
                    c += n
                    yield
            b = nps()
            for c in range(NCK):
                for hs_ in HS:
                    mm(psb[b][hs_, c * 64:(c + 1) * 64], KR[hs_, c, 0:64], bbar[hs_, c * 64:(c + 1) * 64], True, True, [KRi, BBi], [PS(b)])
            tt(YT[:], v1(b), cmask[:, 3:4, :].to_broadcast([128, NCK, 64]), ALU.mult, [PS(b), "cmask"], ["YT"])
            tt(G0[:], Y[:], ident8[:], ALU.add, ["Y", "ident8"], ["G0"])
            yield
            gcur, gcid, gnxt, gnid = G0, "G0", G1, "G1"
            for lvl in range(5):
                last = lvl == 4
                if not last:
                    b1 = nps()
                    for c in range(NCK):
                        for hs_ in HS:
                            mm(psb[b1][hs_, c * 64:(c + 1) * 64], YT[hs_, c, :], Y[hs_, c, :], True, True, ["YT", "Y"], [PS(b1)])
                b2 = nps()
                for c in range(NCK):
                    for hs_ in HS:
                        mm(psb[b2][hs_, c * 64:(c + 1) * 64], Y[hs_, c, :], YT[hs_, c, :], True, True, ["YT", "Y"], [PS(b2)])
                tt(IYT[:], v1(b2), ident8[:], ALU.add, [PS(b2), "ident8"], ["IYT"])
                if not last:
                    copy_any(Y[:], v1(b1), [PS(b1)], ["Y"])
                    copy_any(YT[:], v1(b2), [PS(b2)], ["YT"])
                yield
                b3 = nps()
                for c in range(NCK):
                    for hs_ in HS:
                        mm(psb[b3][hs_, c * 64:(c + 1) * 64], IYT[hs_, c, :], gcur[hs_, c, :], True, True, ["IYT", gcid], [PS(b3)])
                copy_any(gnxt[:], v1(b3), [PS(b3)], [gnid])
                gcur, gcid, gnxt, gnid = gnxt, gnid, gcur, gcid
                yield
            assert gcid == "G1"
            ktok, WT, AkV, Uv = Y, YT, IYT, G0
            b = nps()
            for c in range(NCK):
                for hs_ in HS:
                    mm(psb[b][hs_, c * 64:(c + 1) * 64], KR[hs_, c, 0:64], identb[hs_, hs_], True, True, [KRi, "identb"], [PS(b)])
            copy_any(ktok[:], v1(b), [PS(b)], ["Y"])
            b = nps()
            for c in range(NCK):
                for hs_ in HS:
                    mm(psb[b][hs_, c * 64:(c + 1) * 64], AkT[hs_, c, :], vtok[hs_, c, :], True, True, ["AkT", VTi], [PS(b)])
            copy_any(AkV[:], v1(b), [PS(b)], ["IYT"])
            yield
            b = nps()
            for c in range(NCK):
                for hs_ in HS:
                    mm(psb[b][hs_, c * 64:(c + 1) * 64], ktok[hs_, c, :], G1[hs_, c, :], True, True, ["Y", "G1"], [PS(b)])
            copy_any(WT[:], v1(b), [PS(b)], ["YT"])
            b = nps()
            for c in range(NCK):
                for hs_ in HS:
                    mm(psb[b][hs_, c * 64:(c + 1) * 64], G1[hs_, c, :], AkV[hs_, c, :], True, True, ["G1", "IYT"], [PS(b)])
            copy_any(Uv[:], v1(b), [PS(b)], ["G0"])
            yield
            if full:
                py = nps()
                reserved.add(py)
            for c in range(NCK):
                b = nps()
                for hs_ in HS:
                    o = psb[b][hs_, 0:64]
                    mm(o, WT[hs_, c, :], Hr_bf[hs_, p, :], True, False, ["YT", ("Hr_bf", p)], [PS(b)])
                    mm(o, identb[hs_, hs_], Uv[hs_, c, :], False, True, ["identb", "G0"], [PS(b)])
                ts(nU[:], psb[b][:, 0:64], -1.0, None, ALU.mult, None, [PS(b)], ["nU"])
                yield
                if full:
                    for hs_ in HS:
                        o = psb[py][hs_, c * 64:(c + 1) * 64]
                        mm(o, Hr_bf[hs_, p, :], KR[hs_, c, 64:128], True, False, [("Hr_bf", p), KRi], [PS(py)])
                        mm(o, vtok[hs_, c, :], BkT[hs_, c, :], False, False, [VTi, "BkT"], [PS(py)])
                        mm(o, nU[hs_, :], BbT[hs_, c, :], False, True, ["nU", "BbT"], [PS(py)])
                b = nps()
                for hs_ in HS:
                    o = psb[b][hs_, 0:64]
                    mm(o, khtok[hs_, c, :], vtok[hs_, c, :], True, False, [KHi, VTi], [PS(b)])
                    mm(o, bhtok[hs_, c, :], nU[hs_, :], False, True, [BHi, "nU"], [PS(b)])
                stt(Hr_bf[:, p, :], Hr[:, p, :], edh[:, c:c + 1], psb[b][:, 0:64], ALU.mult, ALU.add, [("Hr", p), EDi, PS(b)], [("Hr_bf", p)])
                stt(Hr[:, p, :], Hr[:, p, :], edh[:, c:c + 1], psb[b][:, 0:64], ALU.mult, ALU.add, [("Hr", p), EDi, PS(b)], [("Hr", p)])
                yield
            if full:
                act_op(ysb[:], psb[py][:, 0:T], AF.Copy, [PS(py)] + YS, YS)
                act_op(ybf[:], psb[py][:, 0:T], AF.Copy, [PS(py)] + YB, YB)
                reserved.discard(py)
                b = nps()
                mm(psb[b][:, 0:T], bones64[:], ybf[:], True, True, ["bones64"] + YB, [PS(b)])
                tt(ysb[:], ysb[:], psb[b][:, 0:T], ALU.subtract, YS + [PS(b)], YS)
                act_op(ybf[:], ysb[:], AF.Square, YS + YB, YB)
                yield
                b = nps()
                mm(psb[b][:, 0:T], bones64[:], ybf[:], True, True, ["bones64"] + YB, [PS(b)])
                ts(e1b[:], psb[b][:, 0:T], 1.0, RWKV_LN_EPS, ALU.mult, ALU.add, [PS(b)] + EB, EB)
                act_op(e1b[:], e1b[:], AF.Ln, EB, EB)
                act_op(e1b[:], e1b[:], AF.Exp, EB, EB, scale=-0.5)
                tt(ysb[:], ysb[:], e1b[:], ALU.mult, YS + EB, YS)
                act_op(ysb[:], ysb[:], AF.Identity, YS + ["rvec"], YS, scale=V(5, p), bias=V(6, p))
                tt(ysb[:], ysb[:], bon[:], ALU.add, YS + [BON], YS)
                tt(yr[:, p, :], ysb[:], gq[:], ALU.mult, YS + [GQ], [("yr", p)])
                yield

        def drain(g):
            for _ in g:
                pass

        drain(stageA(0))
        for p in range(NP):
            gb = stageB(p)
            ga = stageA(p + 1) if p + 1 < NP else None
            done_a, done_b = ga is None, False
            while not (done_a and done_b):
                for _ in range(3):
                    if not done_b:
                        try:
                            next(gb)
                        except StopIteration:
                            done_b = True
                if not done_a:
                    try:
                        next(ga)
                    except StopIteration:
                        done_a = True

    def mixer(full, carry_all):
        prenorm(2)
        tk.barrier()
        yg = alloc(U("yg"), [128, VC, T], BF16, at=BB)
        yr = alloc(U("yr"), [128, NP, T], BF16, at=BB + VC * T * 2)
        assert NP <= VC
        gla(full, yg)
        tk.barrier()
        rwkv(full, yr, carry_all)
        tk.barrier()
        if not full:
            return
        mrg = alloc(U("mrg"), [128, NCH, T], BF16, at=BB + 2 * VC * T * 2)
        sa = alloc(U("sa"), [128, T], F32, at=BB + 2 * VC * T * 2 + NCH * T * 2)
        sb_ = alloc(U("sb"), [128, T], F32, at=BB + 2 * VC * T * 2 + NCH * T * 2 + T * 4)
        assert BB + 2 * VC * T * 2 + NCH * T * 2 + 2 * T * 4 <= BB + BB_SZ
        tiles = []
        for c in range(NCH):
            tiles += [("win", cfg.win_index[("ga", c)]), ("win", cfg.win_index[("gb", c)]), ("upg", c), ("upr", c)]
        ws = WStream(tiles)
        for c in range(NCH):
            b1, b2, b3, b4 = nps(), nps(), nps(), nps()
            dense(ws, NCH, lambda kc, kp: xn[:, kc, :], lambda kc: [("xn", kc)], b1)
            dense(ws, NCH, lambda kc, kp: xn[:, kc, :], lambda kc: [("xn", kc)], b2)
            dense(ws, VC, lambda kc, kp: yg[:, kc, :], lambda kc: [("yg", kc)], b3)
            dense(ws, NP, lambda kc, kp: yr[:, kc, :], lambda kc: [("yr", kc)], b4)
            act_op(sa[:], psb[b1][:, 0:T], AF.Sigmoid, [PS(b1)], ["sa"])
            act_op(sb_[:], psb[b2][:, 0:T], AF.Sigmoid, [PS(b2)], ["sb"])
            tt(sa[:], sa[:], psb[b3][:, 0:T], ALU.mult, ["sa", PS(b3)], ["sa"])
            tt(sb_[:], sb_[:], psb[b4][:, 0:T], ALU.mult, ["sb", PS(b4)], ["sb"])
            tt(mrg[:, c, :], sa[:], sb_[:], ALU.add, ["sa", "sb"], [("mrg", c)])
        tk.barrier()
        pss = nps()
        ws = WStream([("wo", c) for c in range(NCH)])
        for c in range(NCH):
            pf = nps()
            while pf == pss:
                pf = nps()
            dense(ws, NCH, lambda kc, kp: mrg[:, kc, :], lambda kc: [("mrg", kc)], pf)
            if c > 0:
                mm(psb[pss][:, 0:T], ones[:], tmpb[(c - 1) % 2][:], c - 1 == 0, False, ["ones", ("tmpb", (c - 1) % 2)], [PS(pss)])
            copy_any(fbf[:, c, :], psb[pf][:, 0:T], [PS(pf)], [("xn", c)])
            tb = tmpb[c % 2]
            act_op(tb[:], psb[pf][:, 0:T], AF.Square, [PS(pf)], [("tmpb", c % 2)])
        mm(psb[pss][:, 0:T], ones[:], tmpb[(NCH - 1) % 2][:], NCH - 1 == 0, True, ["ones", ("tmpb", (NCH - 1) % 2)], [PS(pss)])
        tk.barrier()
        post_residual(3, pss)

    dbg = os.environ.get("KDBG", "")
    for ti in range(cfg.NT):
        if dbg == "setup" or (dbg == "pre" and ti >= 0):
            full = ti >= cfg.NPRE
            tk.barrier()
            tk.dma("sp", lambda e, ti=ti: e.dma_start(out=hT[:].rearrange("p c t -> p (c t)"), in_=xT[ti]),
                   reads=(), writes=[("hT", c) for c in range(NCH)], stream="xld")
            if dbg == "pre":
                prenorm(0)
                for c in range(NCH):
                    tk.op("dve", lambda e, c=c: e.tensor_copy(out=hT[:, c, :], in_=xn[:, c, :]), reads=[("xn", c)], writes=[("hT", c)])
            if full:
                tk.dma("sp", lambda e, ti=ti: e.dma_start(out=yT[ti - cfg.NPRE], in_=hT[:].rearrange("p c t -> p (c t)")),
                       reads=[("hT", c) for c in range(NCH)], writes=["yout"], stream="hst")
            continue
        full = ti >= cfg.NPRE
        tk.barrier()
        NQ = 4 if NCH % 4 == 0 else 1
        for q_ in range(NQ):
            c0, c1 = q_ * NCH // NQ, (q_ + 1) * NCH // NQ
            tk.dma("sp", lambda e, ti=ti, c0=c0, c1=c1: e.dma_start(out=hT[:, c0:c1, :].rearrange("p c t -> p (c t)"), in_=xT[ti][:, c0 * T:c1 * T]),
                   reads=(), writes=[("hT", c) for c in range(c0, c1)], stream=("xld", q_))
        tk.dma("sp", lambda e: e.dma_start(out=hs, in_=hT[:].rearrange("p c t -> p (c t)")),
               reads=[("hT", c) for c in range(NCH)], writes=["hs"], stream="hst")
        cur_tile[0] = ti
        last_stage = cfg.stop_after == "ffn1"
        ffn("g1", "u1", "d1", 0, 1, final_out=(yT[ti - cfg.NPRE] if (full and last_stage) else None))
        if dbg in ("gu", "down"):
            tk.barrier()
            if full:
                tk.dma("sp", lambda e, ti=ti: e.dma_start(out=yT[ti - cfg.NPRE], in_=hT[:].rearrange("p c t -> p (c t)")),
                       reads=[("hT", c) for c in range(NCH)], writes=["yout"], stream="hst")
            continue
        if last_stage:
            continue
        mixer(full, ti == cfg.NPRE - 1)
        if not full:
            continue
        if cfg.stop_after == "mix":
            tk.dma("sp", lambda e, ti=ti: e.dma_start(out=yT[ti - cfg.NPRE], in_=hT[:].rearrange("p c t -> p (c t)")),
                   reads=[("hT", c) for c in range(NCH)], writes=["yout"], stream="hst")
            continue
        ffn("g2", "u2", "d2", 4, 5, final_out=yT[ti - cfg.NPRE])
    tk.barrier()
    tk.ops["sp"].append((None, tk._waits("sp", [("d", "hst", tk.streams["hst"])]), None))
    tk.ops["act"].append((None, tk._waits("act", [("d", "hst", tk.streams["hst"])]), None))

    with nc.Block() as block:
        tk.emit(block)
    global LAST_TK
    LAST_TK = tk
    return nc


def _tiles(W, kct_total_chunks, col_chunks, nks=1):
    K = W.shape[0]
    KC = K // 128
    kct = KC // nks
    out = np.zeros((len(col_chunks) * nks, 128, kct, 128), np.float32)
    Wr = W.reshape(KC, 128, W.shape[1])
    i = 0
    for (c0, nc_) in col_chunks:
        for ks in range(nks):
            blk = Wr[ks * kct:(ks + 1) * kct, :, c0:c0 + nc_]
            out[i, :, :, :nc_] = blk.transpose(1, 0, 2)
            i += 1
    return out.reshape(out.shape[0], 128, kct * 128)


def _pc(v):
    v = np.asarray(v, np.float32).reshape(-1, 128)
    return np.ascontiguousarray(v.T)


def prep_shared(cfg, inp):
    D, NCH, NFF = cfg.D, cfg.NCH, cfg.NFF
    sq = lambda k: np.asarray(inp[k], np.float32)[0]
    full_chunks = lambda n: [(i * 128, 128) for i in range(n // 128)]
    m = {}
    m["w_g1"] = _tiles(sq("ffn1_w_gate"), NCH, full_chunks(cfg.DFF))
    m["w_u1"] = _tiles(sq("ffn1_w_up"), NCH, full_chunks(cfg.DFF))
    m["w_d1"] = _tiles(sq("ffn1_w_down"), NFF, full_chunks(D), nks=cfg.NKS)
    m["w_g2"] = _tiles(sq("ffn2_w_gate"), NCH, full_chunks(cfg.DFF))
    m["w_u2"] = _tiles(sq("ffn2_w_up"), NCH, full_chunks(cfg.DFF))
    m["w_d2"] = _tiles(sq("ffn2_w_down"), NFF, full_chunks(D), nks=cfg.NKS)
    win = sq("w_in")
    m["w_win"] = _tiles(win, NCH, [cfg.win_ch[k][i] for (k, i) in cfg.win_order])
    m["w_upg"] = _tiles(sq("w_up_gla"), cfg.VC, full_chunks(D))
    m["w_upr"] = _tiles(sq("w_up_rwkv"), cfg.NP, full_chunks(D))
    m["w_wo"] = _tiles(sq("w_out"), NCH, full_chunks(D))
    m["p_gains"] = np.concatenate([_pc(sq(k)) for k in
                                   ["ffn1_pre_norm", "ffn1_post_norm", "mix_pre_norm", "mix_post_norm", "ffn2_pre_norm", "ffn2_post_norm"]], 1)
    mix = sq("rwkv_shift_mix")
    o0 = cfg.GLA_COLS
    cols = []
    for (k, i) in cfg.rw_chunks:
        c0, n = cfg.win_ch[k][i]
        col = np.zeros(128, np.float32)
        col[:n] = mix[c0 - o0:c0 - o0 + n]
        cols.append(col)
    m["p_rmix"] = np.ascontiguousarray(np.stack(cols, 1))
    m["p_rvec"] = np.concatenate([_pc(sq(k).reshape(-1)) for k in
                                  ["rwkv_w0", "rwkv_a0", "rwkv_k_k", "rwkv_k_a", "rwkv_r_k", "rwkv_ln_w", "rwkv_ln_b"]], 1)
    m["p_gbias"] = _pc(sq("gla_gate_bias"))
    m["p_gonorm"] = _pc(sq("gla_out_norm"))
    m["p_gup"] = np.ascontiguousarray(sq("gla_gate_up"))
    m["w_w2l"] = _tiles(sq("rwkv_w2"), 1, full_chunks(cfg.RW))
    m["w_a2l"] = _tiles(sq("rwkv_a2"), 1, full_chunks(cfg.RW))
    g2 = np.zeros((512, cfg.RW), np.float32)
    g2[:cfg.LG] = sq("rwkv_g2")
    m["w_g2l"] = _tiles(g2, 4, full_chunks(cfg.RW))
    s_i = np.arange(64)[:, None]
    t_i = np.arange(64)[None, :]
    strict = (s_i < t_i).astype(np.float32)
    incl = (s_i <= t_i).astype(np.float32)
    lower = (s_i > t_i).astype(np.float32)
    m["p_cmask"] = np.ascontiguousarray(np.tile(np.stack([strict, incl, -strict, -lower], 1).reshape(64, 256), (2, 1)))
    m["p_ident"] = np.eye(128, dtype=np.float32)
    bo = np.zeros((128, 128), np.float32)
    bo[:64, :64] = 1
    bo[64:, 64:] = 1
    m["p_bones"] = bo
    sm = np.ones((128, cfg.T), np.float32)
    sm[:, ::64] = 0
    m["p_scanm"] = sm
    return m


def run(cfg, inp):
    x = np.asarray(inp["x"], np.float32)
    B, S, D = x.shape
    half = S // 2
    assert half == cfg.NMAIN * cfg.T and cfg.NPRE * cfg.T == half and B * 2 == 8
    shared = prep_shared(cfg, inp)
    nc = build(cfg)
    in_maps = []
    for c in range(8):
        b, r = c // 2, c % 2
        pre = np.zeros((half, D), np.float32) if r == 0 else x[b, :half]
        main = x[b, r * half:(r + 1) * half]
        xx = np.concatenate([pre, main], 0)
        xt = xx.reshape(cfg.NT, cfg.T, cfg.NCH, 128).transpose(0, 3, 2, 1)
        m = dict(shared)
        m["xT"] = np.ascontiguousarray(xt).reshape(cfg.NT, 128, cfg.NCH * cfg.T)
        in_maps.append(m)
    res = run_bass_kernel_spmd(nc, in_maps, core_ids=list(range(8)))
    out = np.zeros((B, S, D), np.float32)
    for c in range(8):
        b, r = c // 2, c % 2
        y = np.asarray(res.results[c]["yT"]).reshape(cfg.NMAIN, 128, cfg.NCH, cfg.T)
        out[b, r * half:(r + 1) * half] = y.transpose(0, 3, 2, 1).reshape(half, D)
    return out


def kernel(**inputs):
    return run(FULL, inputs)
```

```python
import math
import os
import numpy as np
import concourse.bass as bass
import concourse.mybir as mybir
from concourse.bass_utils import run_bass_kernel_spmd

F32 = mybir.dt.float32
BF16 = mybir.dt.bfloat16
ALU = mybir.AluOpType
AF = mybir.ActivationFunctionType

NORM_EPS = 1e-6
RWKV_LN_EPS = 64e-5
C0 = math.exp(-0.5)
SAME_ENGINE_SYNC = os.environ.get("KSES", "0") == "1"


class Cfg:
    def __init__(s, D=4096, DFF=11008, GH=8, RW=2048, T=512, NPRE=4, NMAIN=4, stop_after="full"):
        s.D, s.DFF, s.GH, s.RW, s.T, s.NPRE, s.NMAIN = D, DFF, GH, RW, T, NPRE, NMAIN
        s.stop_after = stop_after
        s.NCH = D // 128
        s.NFF = DFF // 128
        s.DK, s.DV = 128, 256
        s.KEYW, s.VALW = GH * 128, GH * 256
        s.RH = RW // 64
        s.NP = s.RH // 2
        s.LW, s.LA, s.LG = 128, 128, 480
        s.GLA_COLS = 2 * s.KEYW + 2 * s.VALW + 16
        s.RWKV_COLS = 3 * RW + s.LW + s.LA + s.LG
        s.WIN = s.GLA_COLS + s.RWKV_COLS + 2 * D
        s.NT = NPRE + NMAIN
        s.NCK = T // 64
        s.NKS = -(-s.NFF // 43)
        assert s.NFF % s.NKS == 0
        s.KCD = s.NFF // s.NKS
        s.VC = s.VALW // 128
        o = 0
        ch = {}
        def take(name, n):
            nonlocal o
            lst = []
            r = n
            while r > 0:
                w = min(128, r)
                lst.append((o, w))
                o += w
                r -= w
            ch[name] = lst
        take("gq", s.KEYW); take("gk", s.KEYW); take("gv", s.VALW); take("gg", 16); take("go", s.VALW)
        take("rr", RW); take("rw", s.LW); take("rk", RW); take("rv", RW); take("ra", s.LA); take("rg", s.LG)
        take("ga", D); take("gb", D)
        assert o == s.WIN
        s.win_ch = ch
        order = []
        for nm in ["gg", "gq", "gk", "gv", "go", "rw", "ra", "rg"]:
            order += [(nm, i) for i in range(len(ch[nm]))]
        for p in range(s.NP):
            order += [("rr", p), ("rk", p), ("rv", p)]
        for c in range(s.NCH):
            order += [("ga", c), ("gb", c)]
        s.win_order = order
        s.win_index = {k: i for i, k in enumerate(order)}
        s.rw_chunks = ([("rr", p) for p in range(s.NP)] + [("rw", 0)] + [("rk", p) for p in range(s.NP)]
                       + [("rv", p) for p in range(s.NP)] + [("ra", 0)] + [("rg", i) for i in range(4)])
        s.rw_cidx = {k: i for i, k in enumerate(s.rw_chunks)}


FULL = Cfg()
LAST_TK = None


EP = 20000
EPD = 1500


class Tracker:
    def __init__(s, nc):
        s.nc = nc
        s.engs = {"pe": nc.tensor, "dve": nc.vector, "act": nc.scalar, "pool": nc.gpsimd, "sp": nc.sync}
        s.ops = {k: [] for k in s.engs}
        s.cnt = {k: 0 for k in s.engs}
        s.seen = {k: {} for k in s.engs}
        s.last_w = {}
        s.readers = {}
        s.streams = {}
        s.groups = []
        s.sems = {}

    def _deps(s, reads, writes):
        deps = []
        for b in list(reads) + list(writes):
            t = s.last_w.get(b)
            if t is not None:
                deps.append(t)
        for b in writes:
            deps += s.readers.get(b, [])
        return deps

    def _waits(s, eng, deps, pe_acc=False):
        out = []
        seen = s.seen[eng]
        for t in deps:
            if t[0] == "e":
                if t[1] == eng:
                    if eng in ("pe", "sp", "pool") or not SAME_ENGINE_SYNC:
                        continue
                key = ("e", t[1])
                if seen.get(key, -1) >= t[2]:
                    continue
                seen[key] = t[2]
                out.append(t)
            elif t[0] == "d":
                key = ("d", t[1])
                if seen.get(key, 0) >= t[2]:
                    continue
                seen[key] = t[2]
                out.append(t)
            else:
                key = ("g", t[1])
                if key in seen:
                    continue
                seen[key] = 1
                out.append(t)
        best = {}
        for t in out:
            k = (t[0], t[1])
            if k not in best or best[k][2 if t[0] != "g" else 1] < t[2 if t[0] != "g" else 1]:
                best[k] = t
        return list(best.values())

    def _record(s, tok, reads, writes):
        for b in reads:
            s.readers.setdefault(b, []).append(tok)
        for b in writes:
            s.last_w[b] = tok
            s.readers[b] = []

    def op(s, eng, fn, reads=(), writes=()):
        deps = s._deps(reads, writes)
        for b in reads:
            if isinstance(b, tuple) and b[0] == "ps":
                deps += [t for t in s.readers.get(b, []) if not (t[0] == "e" and t[1] == eng)]
        waits = s._waits(eng, deps)
        idx = s.cnt[eng]
        s.cnt[eng] += 1
        tok = ("e", eng, idx)
        s.ops[eng].append((fn, waits, tok))
        s._record(tok, reads, writes)
        return tok

    def dma(s, queue, fn, reads=(), writes=(), stream=None, group=None):
        deps = s._deps(reads, writes)
        waits = s._waits(queue, deps)
        if group is not None:
            tok = ("g", group)
        else:
            n = s.streams.get(stream, 0) + 1
            s.streams[stream] = n
            tok = ("d", stream, n)
        s.ops[queue].append((fn, waits, tok))
        s._record(tok, reads, writes)
        return tok

    def barrier(s):
        toks = []
        for e in s.engs:
            if s.cnt[e] > 0 and e not in ("sp",):
                toks.append(("e", e, s.cnt[e] - 1))
        for st, n in s.streams.items():
            toks.append(("d", st, n))
        for e in s.engs:
            w = s._waits(e, toks)
            if w:
                s.ops[e].append((None, w, None))

    def _sem(s, key):
        if key not in s.sems:
            s.sems[key] = s.nc.alloc_semaphore("s_" + "_".join(str(k) for k in key))
        return s.sems[key]

    def _wait_args(s, t):
        if t[0] == "e":
            return s._sem(("e", t[1], t[2] // EP)), (t[2] % EP) + 1
        if t[0] == "d":
            n = t[2] - 1
            return s._sem(("d", t[1], n // EPD)), 16 * ((n % EPD) + 1)
        return s._sem(("g", t[1])), 16 * s.groups[t[1]]

    def emit(s, block):
        def run(engname):
            def body(eng):
                for fn, waits, tok in s.ops[engname]:
                    for t in waits:
                        sem, val = s._wait_args(t)
                        eng.wait_ge(sem, val)
                    if fn is None:
                        continue
                    ins = fn(eng)
                    if tok[0] == "e":
                        ins.then_inc(s._sem(("e", tok[1], tok[2] // EP)), 1)
                    elif tok[0] == "d":
                        ins.then_inc(s._sem(("d", tok[1], (tok[2] - 1) // EPD)), 16)
                    else:
                        ins.then_inc(s._sem(("g", tok[1])), 16)
            return body
        block.tensor(run("pe"))
        block.vector(run("dve"))
        block.scalar(run("act"))
        block.gpsimd(run("pool"))
        block.sync(run("sp"))


def weight_specs(cfg):
    KC = cfg.NCH
    return {
        "g1": (KC, cfg.NFF), "u1": (KC, cfg.NFF), "d1": (cfg.KCD, cfg.NCH * cfg.NKS),
        "win": (KC, len(cfg.win_order)),
        "upg": (cfg.VC, cfg.NCH), "upr": (cfg.NP, cfg.NCH), "wo": (KC, cfg.NCH),
        "g2": (KC, cfg.NFF), "u2": (KC, cfg.NFF), "d2": (cfg.KCD, cfg.NCH * cfg.NKS),
        "w2l": (1, cfg.NP), "a2l": (1, cfg.NP), "g2l": (4, cfg.NP),
    }


def build(cfg):
    nc = bass.Bass("TRN2", target_bir_lowering=False)
    tk = Tracker(nc)
    T, NCH, NFF, NP, NCK, D = cfg.T, cfg.NCH, cfg.NFF, cfg.NP, cfg.NCK, cfg.D
    GH, VC = cfg.GH, cfg.VC
    specs = weight_specs(cfg)

    xT = nc.dram_tensor("xT", [cfg.NT, 128, NCH * T], F32, kind="ExternalInput").ap()
    yT = nc.dram_tensor("yT", [cfg.NMAIN, 128, NCH * T], F32, kind="ExternalOutput").ap()
    hs = nc.dram_tensor("hs", [128, NCH * T], F32, kind="Internal").ap()
    wsrc, wcache = {}, {}
    for nm, (kct, nt) in specs.items():
        wsrc[nm] = nc.dram_tensor("w_" + nm, [nt, 128, kct * 128], F32, kind="ExternalInput").ap()
        wcache[nm] = nc.dram_tensor("c_" + nm, [nt, 128, kct * 128], BF16, kind="Internal").ap()
    NRC = len(cfg.rw_chunks)
    pspec = {
        "gains": [128, 6 * NCH], "rmix": [128, NRC], "rvec": [128, 7 * NP], "gbias": [128, GH],
        "gonorm": [128, 2], "gup": [16, cfg.KEYW], "cmask": [128, 4 * 64], "ident": [128, 128], "bones": [128, 128],
        "scanm": [128, T],
    }
    pin = {k: nc.dram_tensor("p_" + k, v, F32, kind="ExternalInput").ap() for k, v in pspec.items()}

    base = (nc.sbuf_base + 63) // 64 * 64
    total = nc.sbuf_top - base
    arena = nc.alloc_sbuf_tensor("arena", [128, total // 4 - 8], F32)
    cur = [base]

    def alloc(name, shape, dt, at=None):
        nb = int(np.prod(shape[1:])) * (2 if dt == BF16 else 4)
        nb = (nb + 63) // 64 * 64
        if at is None:
            off = cur[0]
            cur[0] += nb
        else:
            off = at
        assert off + nb <= nc.sbuf_top, (name, off, nb, nc.sbuf_top)
        return nc.alloc_sbuf_tensor_at(name, list(shape), dt, offset=off)

    gains = alloc("gains", [128, 6 * NCH], F32)
    rmix = alloc("rmix", [128, NRC], F32)
    rvec = alloc("rvec", [128, 7 * NP], F32)
    gbias_n = alloc("gbias_n", [128, GH], F32)
    gonorm = alloc("gonorm", [128, 2], F32)
    gup_bf = alloc("gup_bf", [16, cfg.KEYW], BF16)
    cmask = alloc("cmask", [128, 4, 64], F32)
    identb = alloc("identb", [128, 128], BF16)
    ident8 = alloc("ident8", [128, NCK, 64], BF16)
    bones = alloc("bones", [128, 128], BF16)
    bones64 = alloc("bones64", [128, 128], BF16)
    ones = alloc("ones", [128, 128], BF16)
    scanm = alloc("scanm", [128, T], F32)
    Sg = alloc("Sg", [128, GH, 256], F32)
    Sg_bf = alloc("Sg_bf", [128, GH, 256], BF16)
    Hr = alloc("Hr", [128, NP, 64], F32)
    Hr_bf = alloc("Hr_bf", [128, NP, 64], BF16)
    carry = alloc("carry", [128, NRC], F32)
    NSLOT = 4
    WSLOT = 43 * 128
    wslots = [alloc(f"wslot{i}", [128, WSLOT], BF16) for i in range(NSLOT)]
    tmpf = [alloc(f"tmpf{i}", [128, T], F32) for i in range(3)]
    tmpb = [alloc(f"tmpb{i}", [128, T], BF16) for i in range(2)]
    rstd = alloc("rstd", [128, T], F32)
    BA = cur[0]
    BA_SZ = NCH * T * 2
    BB = BA + BA_SZ
    BB_SZ = max(NFF * T * 2, NCH * T * 4, 2 * VC * T * 2 + 54 * 1024)
    assert BB + BB_SZ <= nc.sbuf_top, ("sbuf overflow", BB + BB_SZ, nc.sbuf_top)
    xn = alloc("xn", [128, NCH, T], BF16, at=BA)
    fbf = alloc("fbf", [128, NCH, T], BF16, at=BA)
    act = alloc("act", [128, NFF, T], BF16, at=BB)
    hT = alloc("hT", [128, NCH, T], F32, at=BB)

    psb = [nc.alloc_psum_tensor(f"ps{i}", [128, 512], F32) for i in range(8)]
    ps_i = [0]

    reserved = set()

    def nps():
        while True:
            i = ps_i[0] % 8
            ps_i[0] += 1
            if i not in reserved:
                return i

    def PS(b):
        return ("ps", b)

    rr = {"ev": 0}

    def mm(out, lhsT, rhs, start, stop, reads, writes):
        tk.op("pe", lambda e: e.matmul(out, lhsT=lhsT, rhs=rhs, start=start, stop=stop), reads=reads, writes=writes)

    def act_op(out, in_, func, reads, writes, bias=None, scale=None):
        kw = {}
        if bias is not None:
            kw["bias"] = bias
        if scale is not None:
            kw["scale"] = scale
        tk.op("act", lambda e: e.activation(out=out, in_=in_, func=func, **kw), reads=reads, writes=writes)

    def tt(out, in0, in1, op, reads, writes):
        tk.op("dve", lambda e: e.tensor_tensor(out=out, in0=in0, in1=in1, op=op), reads=reads, writes=writes)

    def ts(out, in0, s1, s2, op0, op1, reads, writes):
        if op1 is None:
            tk.op("dve", lambda e: e.tensor_scalar(out=out, in0=in0, scalar1=s1, scalar2=None, op0=op0), reads=reads, writes=writes)
        else:
            tk.op("dve", lambda e: e.tensor_scalar(out=out, in0=in0, scalar1=s1, scalar2=s2, op0=op0, op1=op1), reads=reads, writes=writes)

    def stt(out, in0, scalar, in1, op0, op1, reads, writes):
        tk.op("dve", lambda e: e.scalar_tensor_tensor(out=out, in0=in0, scalar=scalar, in1=in1, op0=op0, op1=op1), reads=reads, writes=writes)

    def rsqrt(out, in_, mul, add, reads, wid, clamp=None):
        if clamp is not None:
            ts(out, in_, clamp, None, ALU.max, None, reads, [wid])
        else:
            ts(out, in_, mul, add, ALU.mult, ALU.add, reads, [wid])
        act_op(out, out, AF.Ln, [wid], [wid])
        act_op(out, out, AF.Exp, [wid], [wid], scale=-0.5)

    def copy_any(out, in_, reads, writes):
        rr["ev"] += 1
        if rr["ev"] % 2:
            act_op(out, in_, AF.Copy, reads, writes)
        else:
            tk.op("dve", lambda e: e.tensor_copy(out=out, in_=in_), reads=reads, writes=writes)

    conv_order = ["g1", "u1", "d1", "win", "upg", "upr", "wo", "g2", "u2", "d2"]
    conv_list = []
    for j in range(NFF):
        conv_list += [("g1", j), ("u1", j)]
    conv_list += [("d1", i) for i in range(specs["d1"][1])]
    conv_list += [("win", i) for i in range(specs["win"][1])]
    for p in range(NP):
        conv_list += [("w2l", p), ("a2l", p), ("g2l", p)]
    for c in range(NCH):
        conv_list += [("upg", c), ("upr", c)]
    conv_list += [("wo", c) for c in range(NCH)]
    for j in range(NFF):
        conv_list += [("g2", j), ("u2", j)]
    conv_list += [("d2", i) for i in range(specs["d2"][1])]
    EARLY_WIN = ("gg", "gk", "gv", "rw", "ra", "rk", "rv")
    def is_early(nm, t):
        if nm in ("g1", "u1", "d1", "w2l", "a2l"):
            return True
        return nm == "win" and cfg.win_order[t][0] in EARLY_WIN
    early = [x for x in conv_list if is_early(*x)]
    late = [x for x in conv_list if not is_early(*x)]
    late.sort(key=lambda x: 0 if (x[0] == "win" and cfg.win_order[x[1]][0] in ("rr", "rg")) else (1 if x[0] == "g2l" else 2))
    conv_state = {"g": 0}

    def issue_group(grp, dep_reads=()):
        gi = conv_state["g"]
        conv_state["g"] += 1
        tk.groups.append(len(grp))
        for (nm, t) in grp:
            src, dst = wsrc[nm][t], wcache[nm][t]
            tk.dma("pool", lambda e, src=src, dst=dst: e.dma_start(out=dst, in_=src), reads=list(dep_reads), writes=[("wc", nm, t)], group=gi)

    i = 0
    for gs in [2, 4, 8, 16] + [32] * 1000:
        if i >= len(early):
            break
        issue_group(early[i:i + gs])
        i += gs
    LG_SZ = 12
    late_groups = [late[i:i + LG_SZ] for i in range(0, len(late), LG_SZ)]
    first_rel = 1 if cfg.NPRE >= 2 else 0
    slots = [(ti_, j_) for ti_ in range(first_rel, max(cfg.NPRE, 1)) for j_ in range(NFF)]
    release_plan = {}
    for gi_, grp in enumerate(late_groups):
        sl_ = slots[min(len(slots) - 1, gi_ * len(slots) // len(late_groups))]
        release_plan.setdefault(sl_, []).append(grp)
    cur_tile = [0]

    def load(dst_ap, src_ap, bufid, queue="sp"):
        tk.dma(queue, lambda e: e.dma_start(out=dst_ap, in_=src_ap), reads=(), writes=[bufid], stream=("ld", bufid))

    load(gains[:], pin["gains"], "gains")
    load(rmix[:], pin["rmix"], "rmix")
    load(rvec[:], pin["rvec"], "rvec")
    load(gonorm[:], pin["gonorm"], "gonorm")
    load(scanm[:], pin["scanm"], "scanm")
    load(cmask[:].rearrange("p a b -> p (a b)"), pin["cmask"], "cmask")
    stg = alloc("stg", [128, max(cfg.KEYW, 128)], F32, at=BB)
    def load_cast(dst, src, np_, ncols, name):
        load(stg[0:np_, 0:ncols], src, "stg")
        tk.op("dve", lambda e: e.tensor_copy(out=dst, in_=stg[0:np_, 0:ncols]), reads=["stg"], writes=[name])
    load_cast(gup_bf[:], pin["gup"], 16, cfg.KEYW, "gup_bf")
    load_cast(identb[:], pin["ident"], 128, 128, "identb")
    load_cast(bones[:], pin["bones"], 128, 128, "bones")
    load(stg[:, 0:GH], pin["gbias"], "stg")
    ts(gbias_n[:], stg[:, 0:GH], -1.0, None, ALU.mult, None, ["stg"], ["gbias_n"])
    ts(gains[:, NCH:2 * NCH], gains[:, NCH:2 * NCH], 0.5, None, ALU.mult, None, ["gains"], ["gains"])
    ts(gains[:, 5 * NCH:6 * NCH], gains[:, 5 * NCH:6 * NCH], 0.5, None, ALU.mult, None, ["gains"], ["gains"])
    ts(bones64[:], bones[:], 1.0 / 64.0, None, ALU.mult, None, ["bones"], ["bones64"])
    tk.op("dve", lambda e: e.memset(ones[:], 1.0), reads=(), writes=["ones"])
    for c in range(NCK):
        tk.op("dve", lambda e, c=c: e.tensor_copy(out=ident8[0:64, c, :], in_=identb[0:64, 0:64]), reads=["identb"], writes=["ident8"])
        tk.op("dve", lambda e, c=c: e.tensor_copy(out=ident8[64:128, c, :], in_=identb[64:128, 64:128]), reads=["identb"], writes=["ident8"])
    tk.op("dve", lambda e: e.memset(Sg[:], 0.0), reads=(), writes=["Sg"])
    tk.op("dve", lambda e: e.memset(Sg_bf[:], 0.0), reads=(), writes=["Sg_bf"])
    tk.op("dve", lambda e: e.memset(Hr[:], 0.0), reads=(), writes=["Hr"])
    tk.op("dve", lambda e: e.memset(Hr_bf[:], 0.0), reads=(), writes=["Hr_bf"])
    tk.op("dve", lambda e: e.memset(carry[:], 0.0), reads=(), writes=["carry"])
    tk.barrier()

    wstate = {"n": 0}

    def wload(nm, t):
        kct = specs[nm][0]
        si = wstate["n"] % NSLOT
        wstate["n"] += 1
        sl = wslots[si]
        src = wcache[nm][t]
        dst = sl[:, 0:kct * 128]
        tk.dma("sp", lambda e: e.dma_start(out=dst, in_=src), reads=[("wc", nm, t)], writes=[("ws", si)], stream=("ws", si))
        return sl[:, 0:kct * 128].rearrange("p (k n) -> p k n", n=128), ("ws", si)

    class WStream:
        def __init__(s, tiles, depth=NSLOT - 1):
            s.tiles, s.depth, s.q, s.i = tiles, depth, [], 0
            for _ in range(min(depth, len(tiles))):
                s._issue()
        def _issue(s):
            s.q.append(wload(*s.tiles[s.i]))
            s.i += 1
        def next(s):
            r = s.q.pop(0)
            if s.i < len(s.tiles):
                s._issue()
            return r

    def dense(ws, kct, rhs_fn, rhs_ids, out_ps, ncols=128, np_=128, nfree=T, first=True, last=True, kparts=None):
        wt, wid = ws.next()
        for kc in range(kct):
            kp = 128 if kparts is None else kparts[kc]
            mm(psb[out_ps][0:ncols, 0:nfree], wt[0:kp, kc, 0:ncols], rhs_fn(kc, kp),
               first and kc == 0, last and kc == kct - 1, [wid] + rhs_ids(kc), [PS(out_ps)])

    def sumsq_accum(src_fn, src_ids, nchunks, ps_bank, lhs=None, from_psum_ids=None):
        for c in range(nchunks):
            tb = tmpb[c % 2]
            act_op(tb[:], src_fn(c), AF.Square, src_ids(c), [("tmpb", c % 2)])
            mm(psb[ps_bank][:, 0:T], ones[:], tb[:], c == 0, c == nchunks - 1, ["ones", ("tmpb", c % 2)], [PS(ps_bank)])

    def rstd_from(ps_bank, n, eps):
        rsqrt(rstd[:], psb[ps_bank][:, 0:T], 1.0 / n, eps, [PS(ps_bank)], "rstd")

    def prenorm(gi_):
        b = nps()
        sumsq_accum(lambda c: hT[:, c, :], lambda c: [("hT", c)], NCH, b)
        rstd_from(b, D, NORM_EPS)
        for c in range(NCH):
            stt(xn[:, c, :], hT[:, c, :], gains[:, gi_ * NCH + c:gi_ * NCH + c + 1], rstd[:], ALU.mult, ALU.mult,
                [("hT", c), "gains", "rstd"], [("xn", c)])

    def post_residual(gi_, ps_ss, final_out=None):
        rstd_from(ps_ss, D, NORM_EPS)
        tk.dma("sp", lambda e: e.dma_start(out=hT[:].rearrange("p c t -> p (c t)"), in_=hs),
               reads=["hs"], writes=[("hT", c) for c in range(NCH)] + [("act", j) for j in range(NFF)] + ["BBall"], stream="hld")
        for c in range(NCH):
            tf = tmpf[c % 3]
            stt(tf[:], fbf[:, c, :], gains[:, gi_ * NCH + c:gi_ * NCH + c + 1], rstd[:], ALU.mult, ALU.mult,
                [("xn", c), "gains", "rstd"], [("tmpf", c % 3)])
            tt(hT[:, c, :], tf[:], hT[:, c, :], ALU.add, [("tmpf", c % 3), ("hT", c)], [("hT", c)])
        dst = hs if final_out is None else final_out
        tk.dma("sp", lambda e: e.dma_start(out=dst, in_=hT[:].rearrange("p c t -> p (c t)")),
               reads=[("hT", c) for c in range(NCH)], writes=["hs" if final_out is None else "yout"], stream="hst")

    def ffn(gn, un, dn, gpre, gpost, final_out=None):
        prenorm(gpre)
        tiles = []
        for j in range(NFF):
            tiles += [(gn, j), (un, j)]
        ws = WStream(tiles)
        for j in range(NFF):
            pg, pu = nps(), nps()
            dense(ws, NCH, lambda kc, kp: xn[:, kc, :], lambda kc: [("xn", kc)], pg)
            dense(ws, NCH, lambda kc, kp: xn[:, kc, :], lambda kc: [("xn", kc)], pu)
            tf = tmpf[j % 3]
            act_op(tf[:], psb[pg][:, 0:T], AF.Silu, [PS(pg)], [("tmpf", j % 3)])
            tt(act[:, j, :], tf[:], psb[pu][:, 0:T], ALU.mult, [("tmpf", j % 3), PS(pu)], [("act", j)] + ([("hT", j // 2)] if j // 2 < NCH else []))
            if gn == "g1":
                for grp in release_plan.get((cur_tile[0], j), []):
                    issue_group(grp, dep_reads=[("act", j)])
        if os.environ.get("KDBG", "") == "gu":
            return
        pss = nps()
        ws = WStream([(dn, i) for i in range(NCH * cfg.NKS)])
        for c in range(NCH):
            pf = nps()
            while pf == pss:
                pf = nps()
            for ks in range(cfg.NKS):
                k0 = ks * cfg.KCD
                dense(ws, cfg.KCD, lambda kc, kp, k0=k0: act[:, k0 + kc, :], lambda kc, k0=k0: [("act", k0 + kc)], pf,
                      first=(ks == 0), last=(ks == cfg.NKS - 1))
            if c > 0:
                mm(psb[pss][:, 0:T], ones[:], tmpb[(c - 1) % 2][:], c - 1 == 0, False, ["ones", ("tmpb", (c - 1) % 2)], [PS(pss)])
            copy_any(fbf[:, c, :], psb[pf][:, 0:T], [PS(pf)], [("xn", c)])
            tb = tmpb[c % 2]
            act_op(tb[:], psb[pf][:, 0:T], AF.Square, [PS(pf)], [("tmpb", c % 2)])
        mm(psb[pss][:, 0:T], ones[:], tmpb[(NCH - 1) % 2][:], NCH - 1 == 0, True, ["ones", ("tmpb", (NCH - 1) % 2)], [PS(pss)])
        if os.environ.get("KDBG", "") == "down":
            return
        post_residual(gpost, pss, final_out)

    class Bump:
        def __init__(s, start, end):
            s.o, s.end = start, end
        def get(s, name, shape, dt):
            nb = (int(np.prod(shape[1:])) * (2 if dt == BF16 else 4) + 63) // 64 * 64
            t_ = alloc(name, shape, dt, at=s.o)
            s.o += nb
            assert s.o <= s.end, ("mixer working set overflow", name, s.o, s.end)
            return t_

    uid = [0]
    def U(p):
        uid[0] += 1
        return f"{p}{uid[0]}"

    def win_tiles(keys):
        return [("win", cfg.win_index[k]) for k in keys]

    def proj_u(ws, key, ps_bank):
        ncols = cfg.win_ch[key[0]][key[1]][1]
        dense(ws, NCH, lambda kc, kp: xn[:, kc, :], lambda kc: [("xn", kc)], ps_bank, ncols=ncols)
        return ncols

    def to_tok(dst, dst_id, srcT, src_id, ncols=128):
        per_bank = 512 // ncols
        c = 0
        while c < NCK:
            b = nps()
            n = min(per_bank, NCK - c)
            for i in range(n):
                mm(psb[b][0:64, i * ncols:(i + 1) * ncols], srcT[0:ncols, (c + i) * 64:(c + i + 1) * 64], identb[0:ncols, 0:ncols],
                   True, True, [src_id, "identb"], [PS(b)])
            copy_any(dst[:, c:c + n, :], psb[b][0:64, 0:n * ncols].rearrange("p (a b) -> p a b", b=ncols), [PS(b)], [dst_id])
            c += n

    def gla(full, yg):
        bp = Bump(BB + (2 * VC * T * 2 if True else 0), BB + BB_SZ)
        ggT = bp.get(U("ggT"), [16, T], BF16)
        cs = bp.get(U("gcs"), [128, T], F32)
        ex = bp.get(U("gex"), [128, T], F32)
        eq = bp.get(U("geq"), [128, T], F32)
        ek = bp.get(U("gek"), [128, T], F32)
        el = bp.get(U("gel"), [128, T], F32)
        edec = bp.get(U("gedec"), [128, NCK], F32)
        qt = bp.get(U("gqt"), [128, T], BF16)
        kt = bp.get(U("gkt"), [128, T], BF16)
        khT = bp.get(U("gkhT"), [128, T], BF16)
        vT = [bp.get(U("gvT"), [128, T], BF16) for _ in range(2)]
        vtok = bp.get(U("gvtok"), [64, NCK, 256], BF16)
        khtok = bp.get(U("gkhtok"), [64, NCK, 128], BF16)
        attn = bp.get(U("gattn"), [64, NCK, 64], BF16)
        sgo = [bp.get(U("gsgo"), [128, T], F32) for _ in range(2)]
        ws = WStream(win_tiles([("gg", 0)]))
        b = nps()
        proj_u(ws, ("gg", 0), b)
        copy_any(ggT[:], psb[b][0:16, 0:T], [PS(b)], ["ggT"])
        for h in range(GH):
            keys = [("gq", h), ("gk", h), ("gv", 2 * h), ("gv", 2 * h + 1)] + ([("go", 2 * h), ("go", 2 * h + 1)] if full else [])
            if not full:
                keys = [("gk", h), ("gv", 2 * h), ("gv", 2 * h + 1)]
            ws = WStream(win_tiles(keys))
            b = nps()
            mm(psb[b][:, 0:T], gup_bf[:, h * 128:(h + 1) * 128], ggT[:], True, True, ["gup_bf", "ggT"], [PS(b)])
            act_op(ex[:], psb[b][:, 0:T], AF.Exp, [PS(b), "gbias_n"], ["gex"], bias=gbias_n[:, h:h + 1], scale=-1.0)
            act_op(ex[:], ex[:], AF.Ln, ["gex"], ["gex"], bias=1.0)
            tk.op("dve", lambda e: e.tensor_tensor_scan(out=cs[:], data0=scanm[:], data1=ex[:], initial=0.0, op0=ALU.mult, op1=ALU.add),
                  reads=["scanm", "gex"], writes=["gcs"])
            cs3 = cs[:].rearrange("p (c t) -> p c t", t=64)
            csl = cs3[:, :, 63:64]
            tt(el[:].rearrange("p (c t) -> p c t", t=64), csl.to_broadcast([128, NCK, 64]), cs3, ALU.subtract, ["gcs"], ["gel"])
            act_op(el[:], el[:], AF.Exp, ["gel"], ["gel"], scale=-1.0 / 16)
            act_op(edec[:].rearrange("p (c o) -> p c o", o=1), csl, AF.Exp, ["gcs"], ["gedec"], scale=-1.0 / 16)
            act_op(ek[:], cs[:], AF.Exp, ["gcs"], ["gek"], scale=1.0 / 16)
            if full:
                act_op(eq[:], cs[:], AF.Exp, ["gcs"], ["geq"], scale=-1.0 / 16, bias=float(math.log(128 ** -0.5)))
                b = nps()
                proj_u(ws, ("gq", h), b)
                tt(qt[:], psb[b][:, 0:T], eq[:], ALU.mult, [PS(b), "geq"], ["gqt"])
            b = nps()
            proj_u(ws, ("gk", h), b)
            if full:
                tt(kt[:], psb[b][:, 0:T], ek[:], ALU.mult, [PS(b), "gek"], ["gkt"])
            tt(khT[:], psb[b][:, 0:T], el[:], ALU.mult, [PS(b), "gel"], ["gkhT"])
            for hf in range(2):
                b = nps()
                proj_u(ws, ("gv", 2 * h + hf), b)
                copy_any(vT[hf][:], psb[b][:, 0:T], [PS(b)], [("gvT", hf)])
            for hf in range(2):
                c = 0
                while c < NCK:
                    b = nps()
                    n = min(4, NCK - c)
                    for i in range(n):
                        mm(psb[b][0:64, i * 128:(i + 1) * 128], vT[hf][:, (c + i) * 64:(c + i + 1) * 64], identb[:], True, True,
                           [("gvT", hf), "identb"], [PS(b)])
                    copy_any(vtok[:, c:c + n, hf * 128:(hf + 1) * 128], psb[b][0:64, 0:n * 128].rearrange("p (a b) -> p a b", b=128),
                             [PS(b)], ["gvtok"])
                    c += n
            to_tok(khtok, "gkhtok", khT, "gkhT")
            if full:
                b = nps()
                for c in range(NCK):
                    mm(psb[b][0:64, c * 64:(c + 1) * 64], kt[:, c * 64:(c + 1) * 64], qt[:, c * 64:(c + 1) * 64], True, True,
                       ["gkt", "gqt"], [PS(b)])
                tt(attn[:], psb[b][0:64, 0:NCK * 64].rearrange("p (c t) -> p c t", t=64),
                   cmask[0:64, 1:2, :].to_broadcast([64, NCK, 64]), ALU.mult, [PS(b), "cmask"], ["gattn"])
                po = [nps(), nps()]
                reserved.update(po)
                bgo = [nps(), nps()]
                reserved.update(bgo)
                go_mm = []
                for hf in range(2):
                    wt_, wid_ = ws.next()
                    for kc in range(NCH):
                        go_mm.append((bgo[hf], wt_[:, kc, :], kc, wid_))
                per_step = -(-len(go_mm) // NCK)
            for c in range(NCK):
                if full:
                    for hf in range(2):
                        mm(psb[po[hf]][:, c * 64:(c + 1) * 64], vtok[:, c, hf * 128:(hf + 1) * 128], attn[:, c, :], True, False,
                           ["gvtok", "gattn"], [PS(po[hf])])
                        mm(psb[po[hf]][:, c * 64:(c + 1) * 64], Sg_bf[:, h, hf * 128:(hf + 1) * 128], qt[:, c * 64:(c + 1) * 64], False, True,
                           [("Sg_bf", h), "gqt"], [PS(po[hf])])
                b = nps()
                mm(psb[b][:, 0:256], khtok[:, c, :], vtok[:, c, :], True, True, ["gkhtok", "gvtok"], [PS(b)])
                stt(Sg_bf[:, h, :], Sg[:, h, :], edec[:, c:c + 1], psb[b][:, 0:256], ALU.mult, ALU.add,
                    [("Sg", h), "gedec", PS(b)], [("Sg_bf", h)])
                stt(Sg[:, h, :], Sg[:, h, :], edec[:, c:c + 1], psb[b][:, 0:256], ALU.mult, ALU.add,
                    [("Sg", h), "gedec", PS(b)], [("Sg", h)])
                if full:
                    for (bk, lw, kc, wid_) in go_mm[c * per_step:(c + 1) * per_step]:
                        mm(psb[bk][:, 0:T], lw, xn[:, kc, :], kc == 0, kc == NCH - 1, [wid_, ("xn", kc)], [PS(bk)])
            if full:
                for hf in range(2):
                    act_op(sgo[hf][:], psb[bgo[hf]][:, 0:T], AF.Silu, [PS(bgo[hf])], [("gsgo", hf)])
                reserved.difference_update(bgo)
                bn = nps()
                for hf in range(2):
                    tb = tmpb[hf]
                    act_op(tb[:], psb[po[hf]][:, 0:T], AF.Square, [PS(po[hf])], [("tmpb", hf)])
                    mm(psb[bn][:, 0:T], ones[:], tb[:], hf == 0, hf == 1, ["ones", ("tmpb", hf)], [PS(bn)])
                rsqrt(rstd[:], psb[bn][:, 0:T], 1.0 / 256, NORM_EPS, [PS(bn)], "rstd")
                for hf in range(2):
                    tf = tmpf[hf]
                    stt(tf[:], psb[po[hf]][:, 0:T], gonorm[:, hf:hf + 1], rstd[:], ALU.mult, ALU.mult,
                        [PS(po[hf]), "gonorm", "rstd"], [("tmpf", hf)])
                    tt(yg[:, 2 * h + hf, :], tf[:], sgo[hf][:], ALU.mult, [("tmpf", hf), ("gsgo", hf)], [("yg", 2 * h + hf)])
                reserved.difference_update(po)

    def rwkv(full, yr, carry_all):
        bp = Bump(BB + 2 * VC * T * 2, BB + BB_SZ)
        pbuf = [bp.get(U("rp"), [128, T + 1], F32) for _ in range(2)]
        twd = bp.get(U("twd"), [128, T], BF16)
        tad = bp.get(U("tad"), [128, T], BF16)
        sgd = [bp.get(U("sgd"), [128, T], BF16) for _ in range(4)]
        rq = bp.get(U("rq"), [128, T], F32)
        kq = bp.get(U("kq"), [128, T], F32)
        vq = bp.get(U("vq"), [128, T], F32)
        ld = bp.get(U("ld"), [128, T], F32)
        aa = bp.get(U("aa"), [128, T], F32)
        kap = bp.get(U("kap"), [128, T], F32)
        bbq = bp.get(U("bbq"), [128, T], F32)
        trT = bp.get(U("trT"), [128, T], BF16)
        edh2 = [bp.get(U("edh"), [128, NCK], F32) for _ in range(2)]
        KR2 = [bp.get(U("KR"), [128, NCK, 128], BF16) for _ in range(2)]
        bbar2 = [bp.get(U("bbar"), [128, T], BF16) for _ in range(2)]
        kbar2 = [bp.get(U("kbar"), [128, T], BF16) for _ in range(2)]
        vtok2 = [bp.get(U("rvtok"), [128, NCK, 64], BF16) for _ in range(2)]
        khtok2 = [bp.get(U("rkhtok"), [128, NCK, 64], BF16) for _ in range(2)]
        bhtok2 = [bp.get(U("rbhtok"), [128, NCK, 64], BF16) for _ in range(2)]
        gq2 = [tmpf[1], bp.get(U("gq1"), [128, T], F32)]
        bon2 = [tmpf[2], bp.get(U("bon1"), [128, T], F32)]
        GQ2 = [("tmpf", 1), "gq1"]
        BON2 = [("tmpf", 2), "bon1"]
        oY = bp.o
        Y = bp.get(U("Y"), [128, NCK, 64], BF16)
        YT = bp.get(U("YT"), [128, NCK, 64], BF16)
        oI = bp.o
        IYT = bp.get(U("IYT"), [128, NCK, 64], BF16)
        oG = bp.o
        G0 = bp.get(U("G0"), [128, NCK, 64], BF16)
        G1 = bp.get(U("G1"), [128, NCK, 64], BF16)
        AkT = bp.get(U("AkT"), [128, NCK, 64], BF16)
        BkT = bp.get(U("BkT"), [128, NCK, 64], BF16)
        BbT = bp.get(U("BbT"), [128, NCK, 64], BF16)
        rhs_sb = bp.get(U("rhs"), [128, 64], BF16)
        nU = bp.get(U("nU"), [128, 64], BF16)
        assert NCK * 64 * 2 * 2 >= T * 4 and NCK * 64 * 2 >= T * 2
        ysb = alloc(U("ysb"), [128, T], F32, at=oY)
        ybf = alloc(U("ybf"), [128, T], BF16, at=oI)
        e1b = alloc(U("e1b"), [128, T], F32, at=oG)
        YS, YB, EB = ["Y", "YT"], ["IYT"], ["G0", "G1"]
        dtmp, csr, e2, e3 = aa, kq, ld, aa
        e1, kmod = tmpf[0], rstd
        E1, KMOD = ("tmpf", 0), "rstd"
        HS = (slice(0, 64), slice(64, 128))

        def shifted(key, ws, dst, dst_id, func=None):
            ci = cfg.rw_cidx[key]
            pb = pbuf[ci % 2]
            pid = ("rp", ci % 2)
            b = nps()
            ncols = proj_u(ws, key, b)
            act_op(pb[0:ncols, 1:T + 1], psb[b][0:ncols, 0:T], AF.Copy, [PS(b)], [pid])
            act_op(pb[0:ncols, 0:1], carry[0:ncols, ci:ci + 1], AF.Copy, [("carry", ci)], [pid])
            act_op(carry[0:ncols, ci:ci + 1], pb[0:ncols, T:T + 1], AF.Copy, [pid], [("carry", ci)])
            if dst is None:
                return ncols
            tt(dtmp[0:ncols, :], pb[0:ncols, 0:T], pb[0:ncols, 1:T + 1], ALU.subtract, [pid], ["aa"])
            if func is None:
                stt(dst[0:ncols, :], dtmp[0:ncols, :], rmix[0:ncols, ci:ci + 1], pb[0:ncols, 1:T + 1], ALU.mult, ALU.add,
                    ["aa", "rmix", pid], [dst_id])
            else:
                stt(dtmp[0:ncols, :], dtmp[0:ncols, :], rmix[0:ncols, ci:ci + 1], pb[0:ncols, 1:T + 1], ALU.mult, ALU.add,
                    ["aa", "rmix", pid], ["aa"])
                act_op(dst[0:ncols, :], dtmp[0:ncols, :], func, ["aa"], [dst_id])
            return ncols

        keys = [("rw", 0), ("ra", 0)] + ([("rg", i) for i in range(4)] if (full or carry_all) else [])
        ws = WStream(win_tiles(keys))
        shifted(("rw", 0), ws, twd, "twd", AF.Tanh)
        shifted(("ra", 0), ws, tad, "tad", AF.Copy)
        if full or carry_all:
            for i in range(4):
                if full:
                    shifted(("rg", i), ws, sgd[i], ("sgd", i), AF.Sigmoid)
                else:
                    shifted(("rg", i), ws, None, None)
        V = lambda j, p: rvec[:, j * NP + p:j * NP + p + 1]
        c3 = lambda a_: a_[:].rearrange("p (c t) -> p c t", t=64)

        def stageA(p):
            q = p % 2
            KR, bbar, kbar, vtok, khtok, bhtok, edh = KR2[q], bbar2[q], kbar2[q], vtok2[q], khtok2[q], bhtok2[q], edh2[q]
            gq, bon, GQ, BON = gq2[q], bon2[q], GQ2[q], BON2[q]
            KRi, BBi, KBi, VTi, KHi, BHi, EDi = ("KR", q), ("bbar", q), ("kbar", q), ("rvtok", q), ("rkhtok", q), ("rbhtok", q), ("edh", q)
            tl = win_tiles(([("rr", p)] if (full or carry_all) else []) + [("rk", p), ("rv", p)])
            tl += [("w2l", p), ("a2l", p)] + ([("g2l", p)] if full else [])
            ws = WStream(tl)
            if full:
                shifted(("rr", p), ws, rq, "rq")
                yield
            elif carry_all:
                shifted(("rr", p), ws, None, None)
                yield
            shifted(("rk", p), ws, kq, "kq")
            yield
            shifted(("rv", p), ws, vq, "vq")
            yield
            wt, wid = ws.next()
            b = nps()
            mm(psb[b][:, 0:T], wt[:, 0, :], twd[:], True, True, [wid, "twd"], [PS(b)])
            act_op(ld[:], psb[b][:, 0:T], AF.Sigmoid, [PS(b), "rvec"], ["ld"], bias=V(0, p))
            wt, wid = ws.next()
            b = nps()
            mm(psb[b][:, 0:T], wt[:, 0, :], tad[:], True, True, [wid, "tad"], [PS(b)])
            act_op(aa[:], psb[b][:, 0:T], AF.Sigmoid, [PS(b), "rvec"], ["aa"], bias=V(1, p))
            if full:
                wt, wid = ws.next()
                b = nps()
                for kc in range(4):
                    kp = 128 if kc < 3 else cfg.LG - 384
                    mm(psb[b][:, 0:T], wt[0:kp, kc, :], sgd[kc][0:kp, :], kc == 0, kc == 3, [wid, ("sgd", kc)], [PS(b)])
                act_op(gq[:], psb[b][:, 0:T], AF.Copy, [PS(b)], [GQ])
            yield
            act_op(kap[:], kq[:], AF.Copy, ["kq", "rvec"], ["kap"], scale=V(2, p))
            act_op(tmpb[0][:], kq[:], AF.Square, ["kq", "rvec"], [("tmpb", 0)], scale=V(2, p))
            b = nps()
            mm(psb[b][:, 0:T], bones[:], tmpb[0][:], True, True, ["bones", ("tmpb", 0)], [PS(b)])
            rsqrt(e1[:], psb[b][:, 0:T], None, None, [PS(b)], E1, clamp=1e-18)
            tt(kap[:], kap[:], e1[:], ALU.mult, ["kap", E1], ["kap"])
            yield
            ts(e1[:], aa[:], 1.0, V(3, p), ALU.subtract, ALU.mult, ["aa", "rvec"], [E1])
            stt(kmod[:], e1[:], 1.0, kq[:], ALU.add, ALU.mult, [E1, "kq"], [KMOD])
            tt(bbq[:], aa[:], kap[:], ALU.mult, ["aa", "kap"], ["bbq"])
            if full:
                stt(tmpb[1][:], rq[:], V(4, p), kmod[:], ALU.mult, ALU.mult, ["rq", "rvec", KMOD], [("tmpb", 1)])
                b = nps()
                mm(psb[b][:, 0:T], bones[:], tmpb[1][:], True, True, ["bones", ("tmpb", 1)], [PS(b)])
                tt(bon[:], psb[b][:, 0:T], vq[:], ALU.mult, [PS(b), "vq"], [BON])
            yield
            tk.op("dve", lambda e: e.tensor_tensor_scan(out=csr[:], data0=scanm[:], data1=ld[:], initial=0.0, op0=ALU.mult, op1=ALU.add),
                  reads=["scanm", "ld", KMOD], writes=["kq"])
            cs3 = csr[:].rearrange("p (c t) -> p c t", t=64)
            csl = cs3[:, :, 63:64]
            tt(e1[:], csr[:], ld[:], ALU.subtract, ["kq", "ld"], [E1])
            act_op(e1[:], e1[:], AF.Exp, [E1], [E1], scale=-C0)
            tt(KR[:, :, 0:64], c3(kap), c3(e1), ALU.mult, ["kap", E1], [KRi])
            if full:
                act_op(e2[:], csr[:], AF.Exp, ["kq", E1], ["ld"], scale=-C0)
                tt(KR[:, :, 64:128], c3(rq), c3(e2), ALU.mult, ["rq", "ld"], [KRi])
            yield
            act_op(e3[:], csr[:], AF.Exp, ["kq", "bbq"], ["aa"], scale=C0)
            tt(bbar[:], bbq[:], e3[:], ALU.mult, ["bbq", "aa"], [BBi])
            tt(kbar[:], kmod[:], e3[:], ALU.mult, [KMOD, "aa"], [KBi])
            tt(c3(e2), csl.to_broadcast([128, NCK, 64]), cs3, ALU.subtract, ["kq", KRi], ["ld"])
            act_op(e2[:], e2[:], AF.Exp, ["ld"], ["ld"], scale=-C0)
            act_op(edh[:].rearrange("p (c o) -> p c o", o=1), csl, AF.Exp, ["kq"], [EDi], scale=-C0)
            yield

            def to_tok_pair(dst, dst_id):
                bq = nps()
                for c in range(NCK):
                    for hs_ in HS:
                        mm(psb[bq][hs_, c * 64:(c + 1) * 64], trT[hs_, c * 64:(c + 1) * 64], identb[hs_, hs_], True, True,
                           ["trT", "identb"], [PS(bq)])
                copy_any(dst[:], psb[bq][:, 0:NCK * 64].rearrange("p (c t) -> p c t", t=64), [PS(bq)], [dst_id])

            act_op(trT[:], vq[:], AF.Copy, ["vq"], ["trT"])
            to_tok_pair(vtok, VTi)
            yield
            tt(trT[:], kmod[:], e2[:], ALU.mult, [KMOD, "ld"], ["trT"])
            to_tok_pair(khtok, KHi)
            yield
            tt(trT[:], bbq[:], e2[:], ALU.mult, ["bbq", "ld"], ["trT"])
            to_tok_pair(bhtok, BHi)
            yield

        def stageB(p):
            q = p % 2
            KR, bbar, kbar, vtok, khtok, bhtok, edh = KR2[q], bbar2[q], kbar2[q], vtok2[q], khtok2[q], bhtok2[q], edh2[q]
            gq, bon, GQ, BON = gq2[q], bon2[q], GQ2[q], BON2[q]
            KRi, BBi, KBi, VTi, KHi, BHi, EDi = ("KR", q), ("bbar", q), ("kbar", q), ("rvtok", q), ("rkhtok", q), ("rbhtok", q), ("edh", q)
            v1 = lambda bb_: psb[bb_][:, 0:NCK * 64].rearrange("p (c t) -> p c t", t=64)
            for (src, srcid, dA, dAid, mA, dB, dBid) in ((bbar, BBi, Y, "Y", 2, BbT, "BbT"), (kbar, KBi, AkT, "AkT", 0, BkT, "BkT")):
                c = 0
                while c < NCK:
                    b = nps()
                    n = min(4, NCK - c)
                    ncol = 128 if full else 64
                    for i in range(n):
                        for hs_ in HS:
                            mm(psb[b][hs_, i * 128:i * 128 + ncol], src[hs_, (c + i) * 64:(c + i + 1) * 64], KR[hs_, c + i, 0:ncol], True, True,
                               [srcid, KRi], [PS(b)])
                    v4 = psb[b][:, 0:n * 128].rearrange("p (a b) -> p a b", b=128)
                    tt(dA[:, c:c + n, :], v4[:, :, 0:64], cmask[:, mA:mA + 1, :].to_broadcast([128, n, 64]), ALU.mult, [PS(b), "cmask"], [dAid])
                    if full:
                        tt(dB[:, c:c + n, :], v4[:, :, 64:128], cmask[:, 1:2, :].to_broadcast([128, n, 64]), ALU.mult, [PS(b), "cmask"], [dBid])
                    c += n
                    yield
            b = nps()
            for c in range(NCK):
                for hs_ in HS:
                    mm(psb[b][hs_, c * 64:(c + 1) * 64], KR[hs_, c, 0:64], bbar[hs_, c * 64:(c + 1) * 64], True, True, [KRi, BBi], [PS(b)])
            tt(YT[:], v1(b), cmask[:, 3:4, :].to_broadcast([128, NCK, 64]), ALU.mult, [PS(b), "cmask"], ["YT"])
            tt(G0[:], Y[:], ident8[:], ALU.add, ["Y", "ident8"], ["G0"])
            yield
            gcur, gcid, gnxt, gnid = G0, "G0", G1, "G1"
            for lvl in range(5):
                last = lvl == 4
                if not last:
                    b1 = nps()
                    for c in range(NCK):
                        for hs_ in HS:
                            mm(psb[b1][hs_, c * 64:(c + 1) * 64], YT[hs_, c, :], Y[hs_, c, :], True, True, ["YT", "Y"], [PS(b1)])
                b2 = nps()
                for c in range(NCK):
                    for hs_ in HS:
                        mm(psb[b2][hs_, c * 64:(c + 1) * 64], Y[hs_, c, :], YT[hs_, c, :], True, True, ["YT", "Y"], [PS(b2)])
                tt(IYT[:], v1(b2), ident8[:], ALU.add, [PS(b2), "ident8"], ["IYT"])
                if not last:
                    copy_any(Y[:], v1(b1), [PS(b1)], ["Y"])
                    copy_any(YT[:], v1(b2), [PS(b2)], ["YT"])
                yield
                b3 = nps()
                for c in range(NCK):
                    for hs_ in HS:
                        mm(psb[b3][hs_, c * 64:(c + 1) * 64], IYT[hs_, c, :], gcur[hs_, c, :], True, True, ["IYT", gcid], [PS(b3)])
                copy_any(gnxt[:], v1(b3), [PS(b3)], [gnid])
                gcur, gcid, gnxt, gnid = gnxt, gnid, gcur, gcid
                yield
            assert gcid == "G1"
            ktok, WT, AkV, Uv = Y, YT, IYT, G0
            b = nps()
            for c in range(NCK):
                for hs_ in HS:
                    mm(psb[b][hs_, c * 64:(c + 1) * 64], KR[hs_, c, 0:64], identb[hs_, hs_], True, True, [KRi, "identb"], [PS(b)])
            copy_any(ktok[:], v1(b), [PS(b)], ["Y"])
            b = nps()
            for c in range(NCK):
                for hs_ in HS:
                    mm(psb[b][hs_, c * 64:(c + 1) * 64], AkT[hs_, c, :], vtok[hs_, c, :], True, True, ["AkT", VTi], [PS(b)])
            copy_any(AkV[:], v1(b), [PS(b)], ["IYT"])
            yield
            b = nps()
            for c in range(NCK):
                for hs_ in HS:
                    mm(psb[b][hs_, c * 64:(c + 1) * 64], ktok[hs_, c, :], G1[hs_, c, :], True, True, ["Y", "G1"], [PS(b)])
            copy_any(WT[:], v1(b), [PS(b)], ["YT"])
            b = nps()
            for c in range(NCK):
                for hs_ in HS:
                    mm(psb[b][hs_, c * 64:(c + 1) * 64], G1[hs_, c, :], AkV[hs_, c, :], True, True, ["G1", "IYT"], [PS(b)])
            copy_any(Uv[:], v1(b), [PS(b)], ["G0"])
            yield
            if full:
                py = nps()
                reserved.add(py)
            for c in range(NCK):
                b = nps()
                for hs_ in HS:
                    o = psb[b][hs_, 0:64]
                    mm(o, WT[hs_, c, :], Hr_bf[hs_, p, :], True, False, ["YT", ("Hr_bf", p)], [PS(b)])
                    mm(o, identb[hs_, hs_], Uv[hs_, c, :], False, True, ["identb", "G0"], [PS(b)])
                ts(nU[:], psb[b][:, 0:64], -1.0, None, ALU.mult, None, [PS(b)], ["nU"])
                yield
                if full:
                    for hs_ in HS:
                        o = psb[py][hs_, c * 64:(c + 1) * 64]
                        mm(o, Hr_bf[hs_, p, :], KR[hs_, c, 64:128], True, False, [("Hr_bf", p), KRi], [PS(py)])
                        mm(o, vtok[hs_, c, :], BkT[hs_, c, :], False, False, [VTi, "BkT"], [PS(py)])
                        mm(o, nU[hs_, :], BbT[hs_, c, :], False, True, ["nU", "BbT"], [PS(py)])
                b = nps()
                for hs_ in HS:
                    o = psb[b][hs_, 0:64]
                    mm(o, khtok[hs_, c, :], vtok[hs_, c, :], True, False, [KHi, VTi], [PS(b)])
                    mm(o, bhtok[hs_, c, :], nU[hs_, :], False, True, [BHi, "nU"], [PS(b)])
                stt(Hr_bf[:, p, :], Hr[:, p, :], edh[:, c:c + 1], psb[b][:, 0:64], ALU.mult, ALU.add, [("Hr", p), EDi, PS(b)], [("Hr_bf", p)])
                stt(Hr[:, p, :], Hr[:, p, :], edh[:, c:c + 1], psb[b][:, 0:64], ALU.mult, ALU.add, [("Hr", p), EDi, PS(b)], [("Hr", p)])
                yield
            if full:
                act_op(ysb[:], psb[py][:, 0:T], AF.Copy, [PS(py)] + YS, YS)
                act_op(ybf[:], psb[py][:, 0:T], AF.Copy, [PS(py)] + YB, YB)
                reserved.discard(py)
                b = nps()
                mm(psb[b][:, 0:T], bones64[:], ybf[:], True, True, ["bones64"] + YB, [PS(b)])
                tt(ysb[:], ysb[:], psb[b][:, 0:T], ALU.subtract, YS + [PS(b)], YS)
                act_op(ybf[:], ysb[:], AF.Square, YS + YB, YB)
                yield
                b = nps()
                mm(psb[b][:, 0:T], bones64[:], ybf[:], True, True, ["bones64"] + YB, [PS(b)])
                ts(e1b[:], psb[b][:, 0:T], 1.0, RWKV_LN_EPS, ALU.mult, ALU.add, [PS(b)] + EB, EB)
                act_op(e1b[:], e1b[:], AF.Ln, EB, EB)
                act_op(e1b[:], e1b[:], AF.Exp, EB, EB, scale=-0.5)
                tt(ysb[:], ysb[:], e1b[:], ALU.mult, YS + EB, YS)
                act_op(ysb[:], ysb[:], AF.Identity, YS + ["rvec"], YS, scale=V(5, p), bias=V(6, p))
                tt(ysb[:], ysb[:], bon[:], ALU.add, YS + [BON], YS)
                tt(yr[:, p, :], ysb[:], gq[:], ALU.mult, YS + [GQ], [("yr", p)])
                yield

        def drain(g):
            for _ in g:
                pass

        drain(stageA(0))
        for p in range(NP):
            gb = stageB(p)
            ga = stageA(p + 1) if p + 1 < NP else None
            done_a, done_b = ga is None, False
            while not (done_a and done_b):
                for _ in range(3):
                    if not done_b:
                        try:
                            next(gb)
                        except StopIteration:
                            done_b = True
                if not done_a:
                    try:
                        next(ga)
                    except StopIteration:
                        done_a = True

    def mixer(full, carry_all):
        prenorm(2)
        tk.barrier()
        yg = alloc(U("yg"), [128, VC, T], BF16, at=BB)
        yr = alloc(U("yr"), [128, NP, T], BF16, at=BB + VC * T * 2)
        assert NP <= VC
        gla(full, yg)
        tk.barrier()
        rwkv(full, yr, carry_all)
        tk.barrier()
        if not full:
            return
        mrg = alloc(U("mrg"), [128, NCH, T], BF16, at=BB + 2 * VC * T * 2)
        sa = alloc(U("sa"), [128, T], F32, at=BB + 2 * VC * T * 2 + NCH * T * 2)
        sb_ = alloc(U("sb"), [128, T], F32, at=BB + 2 * VC * T * 2 + NCH * T * 2 + T * 4)
        assert BB + 2 * VC * T * 2 + NCH * T * 2 + 2 * T * 4 <= BB + BB_SZ
        tiles = []
        for c in range(NCH):
            tiles += [("win", cfg.win_index[("ga", c)]), ("win", cfg.win_index[("gb", c)]), ("upg", c), ("upr", c)]
        ws = WStream(tiles)
        for c in range(NCH):
            b1, b2, b3, b4 = nps(), nps(), nps(), nps()
            dense(ws, NCH, lambda kc, kp: xn[:, kc, :], lambda kc: [("xn", kc)], b1)
            dense(ws, NCH, lambda kc, kp: xn[:, kc, :], lambda kc: [("xn", kc)], b2)
            dense(ws, VC, lambda kc, kp: yg[:, kc, :], lambda kc: [("yg", kc)], b3)
            dense(ws, NP, lambda kc, kp: yr[:, kc, :], lambda kc: [("yr", kc)], b4)
            act_op(sa[:], psb[b1][:, 0:T], AF.Sigmoid, [PS(b1)], ["sa"])
            act_op(sb_[:], psb[b2][:, 0:T], AF.Sigmoid, [PS(b2)], ["sb"])
            tt(sa[:], sa[:], psb[b3][:, 0:T], ALU.mult, ["sa", PS(b3)], ["sa"])
            tt(sb_[:], sb_[:], psb[b4][:, 0:T], ALU.mult, ["sb", PS(b4)], ["sb"])
            tt(mrg[:, c, :], sa[:], sb_[:], ALU.add, ["sa", "sb"], [("mrg", c)])
        tk.barrier()
        pss = nps()
        ws = WStream([("wo", c) for c in range(NCH)])
        for c in range(NCH):
            pf = nps()
            while pf == pss:
                pf = nps()
            dense(ws, NCH, lambda kc, kp: mrg[:, kc, :], lambda kc: [("mrg", kc)], pf)
            if c > 0:
                mm(psb[pss][:, 0:T], ones[:], tmpb[(c - 1) % 2][:], c - 1 == 0, False, ["ones", ("tmpb", (c - 1) % 2)], [PS(pss)])
            copy_any(fbf[:, c, :], psb[pf][:, 0:T], [PS(pf)], [("xn", c)])
            tb = tmpb[c % 2]
            act_op(tb[:], psb[pf][:, 0:T], AF.Square, [PS(pf)], [("tmpb", c % 2)])
        mm(psb[pss][:, 0:T], ones[:], tmpb[(NCH - 1) % 2][:], NCH - 1 == 0, True, ["ones", ("tmpb", (NCH - 1) % 2)], [PS(pss)])
        tk.barrier()
        post_residual(3, pss)

    dbg = os.environ.get("KDBG", "")
    for ti in range(cfg.NT):
        if dbg == "setup" or (dbg == "pre" and ti >= 0):
            full = ti >= cfg.NPRE
            tk.barrier()
            tk.dma("sp", lambda e, ti=ti: e.dma_start(out=hT[:].rearrange("p c t -> p (c t)"), in_=xT[ti]),
                   reads=(), writes=[("hT", c) for c in range(NCH)], stream="xld")
            if dbg == "pre":
                prenorm(0)
                for c in range(NCH):
                    tk.op("dve", lambda e, c=c: e.tensor_copy(out=hT[:, c, :], in_=xn[:, c, :]), reads=[("xn", c)], writes=[("hT", c)])
            if full:
                tk.dma("sp", lambda e, ti=ti: e.dma_start(out=yT[ti - cfg.NPRE], in_=hT[:].rearrange("p c t -> p (c t)")),
                       reads=[("hT", c) for c in range(NCH)], writes=["yout"], stream="hst")
            continue
        full = ti >= cfg.NPRE
        tk.barrier()
        tk.dma("sp", lambda e, ti=ti: e.dma_start(out=hT[:].rearrange("p c t -> p (c t)"), in_=xT[ti]),
               reads=(), writes=[("hT", c) for c in range(NCH)], stream="xld")
        tk.dma("sp", lambda e: e.dma_start(out=hs, in_=hT[:].rearrange("p c t -> p (c t)")),
               reads=[("hT", c) for c in range(NCH)], writes=["hs"], stream="hst")
        cur_tile[0] = ti
        last_stage = cfg.stop_after == "ffn1"
        ffn("g1", "u1", "d1", 0, 1, final_out=(yT[ti - cfg.NPRE] if (full and last_stage) else None))
        if dbg in ("gu", "down"):
            tk.barrier()
            if full:
                tk.dma("sp", lambda e, ti=ti: e.dma_start(out=yT[ti - cfg.NPRE], in_=hT[:].rearrange("p c t -> p (c t)")),
                       reads=[("hT", c) for c in range(NCH)], writes=["yout"], stream="hst")
            continue
        if last_stage:
            continue
        mixer(full, ti == cfg.NPRE - 1)
        if not full:
            continue
        if cfg.stop_after == "mix":
            tk.dma("sp", lambda e, ti=ti: e.dma_start(out=yT[ti - cfg.NPRE], in_=hT[:].rearrange("p c t -> p (c t)")),
                   reads=[("hT", c) for c in range(NCH)], writes=["yout"], stream="hst")
            continue
        ffn("g2", "u2", "d2", 4, 5, final_out=yT[ti - cfg.NPRE])
    tk.barrier()
    tk.ops["sp"].append((None, tk._waits("sp", [("d", "hst", tk.streams["hst"])]), None))
    tk.ops["act"].append((None, tk._waits("act", [("d", "hst", tk.streams["hst"])]), None))

    with nc.Block() as block:
        tk.emit(block)
    global LAST_TK
    LAST_TK = tk
    return nc


def _tiles(W, kct_total_chunks, col_chunks, nks=1):
    K = W.shape[0]
    KC = K // 128
    kct = KC // nks
    out = np.zeros((len(col_chunks) * nks, 128, kct, 128), np.float32)
    Wr = W.reshape(KC, 128, W.shape[1])
    i = 0
    for (c0, nc_) in col_chunks:
        for ks in range(nks):
            blk = Wr[ks * kct:(ks + 1) * kct, :, c0:c0 + nc_]
            out[i, :, :, :nc_] = blk.transpose(1, 0, 2)
            i += 1
    return out.reshape(out.shape[0], 128, kct * 128)


def _pc(v):
    v = np.asarray(v, np.float32).reshape(-1, 128)
    return np.ascontiguousarray(v.T)


def prep_shared(cfg, inp):
    D, NCH, NFF = cfg.D, cfg.NCH, cfg.NFF
    sq = lambda k: np.asarray(inp[k], np.float32)[0]
    full_chunks = lambda n: [(i * 128, 128) for i in range(n // 128)]
    m = {}
    m["w_g1"] = _tiles(sq("ffn1_w_gate"), NCH, full_chunks(cfg.DFF))
    m["w_u1"] = _tiles(sq("ffn1_w_up"), NCH, full_chunks(cfg.DFF))
    m["w_d1"] = _tiles(sq("ffn1_w_down"), NFF, full_chunks(D), nks=cfg.NKS)
    m["w_g2"] = _tiles(sq("ffn2_w_gate"), NCH, full_chunks(cfg.DFF))
    m["w_u2"] = _tiles(sq("ffn2_w_up"), NCH, full_chunks(cfg.DFF))
    m["w_d2"] = _tiles(sq("ffn2_w_down"), NFF, full_chunks(D), nks=cfg.NKS)
    win = sq("w_in")
    m["w_win"] = _tiles(win, NCH, [cfg.win_ch[k][i] for (k, i) in cfg.win_order])
    m["w_upg"] = _tiles(sq("w_up_gla"), cfg.VC, full_chunks(D))
    m["w_upr"] = _tiles(sq("w_up_rwkv"), cfg.NP, full_chunks(D))
    m["w_wo"] = _tiles(sq("w_out"), NCH, full_chunks(D))
    m["p_gains"] = np.concatenate([_pc(sq(k)) for k in
                                   ["ffn1_pre_norm", "ffn1_post_norm", "mix_pre_norm", "mix_post_norm", "ffn2_pre_norm", "ffn2_post_norm"]], 1)
    mix = sq("rwkv_shift_mix")
    o0 = cfg.GLA_COLS
    cols = []
    for (k, i) in cfg.rw_chunks:
        c0, n = cfg.win_ch[k][i]
        col = np.zeros(128, np.float32)
        col[:n] = mix[c0 - o0:c0 - o0 + n]
        cols.append(col)
    m["p_rmix"] = np.ascontiguousarray(np.stack(cols, 1))
    m["p_rvec"] = np.concatenate([_pc(sq(k).reshape(-1)) for k in
                                  ["rwkv_w0", "rwkv_a0", "rwkv_k_k", "rwkv_k_a", "rwkv_r_k", "rwkv_ln_w", "rwkv_ln_b"]], 1)
    m["p_gbias"] = _pc(sq("gla_gate_bias"))
    m["p_gonorm"] = _pc(sq("gla_out_norm"))
    m["p_gup"] = np.ascontiguousarray(sq("gla_gate_up"))
    m["w_w2l"] = _tiles(sq("rwkv_w2"), 1, full_chunks(cfg.RW))
    m["w_a2l"] = _tiles(sq("rwkv_a2"), 1, full_chunks(cfg.RW))
    g2 = np.zeros((512, cfg.RW), np.float32)
    g2[:cfg.LG] = sq("rwkv_g2")
    m["w_g2l"] = _tiles(g2, 4, full_chunks(cfg.RW))
    s_i = np.arange(64)[:, None]
    t_i = np.arange(64)[None, :]
    strict = (s_i < t_i).astype(np.float32)
    incl = (s_i <= t_i).astype(np.float32)
    lower = (s_i > t_i).astype(np.float32)
    m["p_cmask"] = np.ascontiguousarray(np.tile(np.stack([strict, incl, -strict, -lower], 1).reshape(64, 256), (2, 1)))
    m["p_ident"] = np.eye(128, dtype=np.float32)
    bo = np.zeros((128, 128), np.float32)
    bo[:64, :64] = 1
    bo[64:, 64:] = 1
    m["p_bones"] = bo
    sm = np.ones((128, cfg.T), np.float32)
    sm[:, ::64] = 0
    m["p_scanm"] = sm
    return m


def run(cfg, inp):
    x = np.asarray(inp["x"], np.float32)
    B, S, D = x.shape
    half = S // 2
    assert half == cfg.NMAIN * cfg.T and cfg.NPRE * cfg.T == half and B * 2 == 8
    shared = prep_shared(cfg, inp)
    nc = build(cfg)
    in_maps = []
    for c in range(8):
        b, r = c // 2, c % 2
        pre = np.zeros((half, D), np.float32) if r == 0 else x[b, :half]
        main = x[b, r * half:(r + 1) * half]
        xx = np.concatenate([pre, main], 0)
        xt = xx.reshape(cfg.NT, cfg.T, cfg.NCH, 128).transpose(0, 3, 2, 1)
        m = dict(shared)
        m["xT"] = np.ascontiguousarray(xt).reshape(cfg.NT, 128, cfg.NCH * cfg.T)
        in_maps.append(m)
    res = run_bass_kernel_spmd(nc, in_maps, core_ids=list(range(8)))
    out = np.zeros((B, S, D), np.float32)
    for c in range(8):
        b, r = c // 2, c % 2
        y = np.asarray(res.results[c]["yT"]).reshape(cfg.NMAIN, 128, cfg.NCH, cfg.T)
        out[b, r * half:(r + 1) * half] = y.transpose(0, 3, 2, 1).reshape(half, D)
    return out


def kernel(**inputs):
    return run(FULL, inputs)
```
